# Optimizing a Trainium2 kernel written in Bass

```python
import jax, jax.numpy as jnp
from jax import lax
import numpy as np

D_MODEL = 1024
BATCH = 2
SEQ = 16384
DEPTH = 4
DEC_BATCH = 32
DEC_SEQ = 2048
PAST_LEN = 128

EXPAND = 2
D_MIX = EXPAND * D_MODEL
D_ATTN = D_MIX // 2
D_POOL = D_MIX - D_ATTN
HEAD_DIM = 64
N_HEADS = D_ATTN // HEAD_DIM
DILATED_PATTERNS = ((128, 1), (512, 4), (2048, 16))
POOL_WINDOWS = (2, 4, 8, 16)
N_POOL_GROUPS = len(POOL_WINDOWS)
POOL_GROUP_DIM = D_POOL // N_POOL_GROUPS
ROPE_THETA = 10000.0
RMS_EPS = 1e-6
D_IN = 4 * D_ATTN + 2 * D_POOL

kernel_name = "hybrid_dilated_attn_pool_encoder"


def _rmsnorm(x, g):
    xf = x.astype(jnp.float32)
    y = xf * lax.rsqrt(jnp.mean(xf * xf, axis=-1, keepdims=True) + RMS_EPS)
    return (y * g.astype(jnp.float32)).astype(x.dtype)


def _rope(x):
    s, hd = x.shape[1], x.shape[3]
    inv_freq = ROPE_THETA ** (-jnp.arange(0, hd, 2, dtype=jnp.float32) / hd)
    ang = jnp.arange(s, dtype=jnp.float32)[:, None] * inv_freq[None, :]
    cos = jnp.cos(ang)[None, :, None, :]
    sin = jnp.sin(ang)[None, :, None, :]
    xf = x.astype(jnp.float32)
    x1, x2 = xf[..., : hd // 2], xf[..., hd // 2:]
    return jnp.concatenate([x1 * cos - x2 * sin, x1 * sin + x2 * cos], axis=-1).astype(x.dtype)


def _dilated_window_attention(q, k, v, dilation, half):
    b, s, h, hd = q.shape
    t = s // dilation
    blk = half
    nb = -(-t // blk)
    tp = nb * blk
    pad = tp - t

    def to_sub(a):
        return a.reshape(b, t, dilation, h, hd).transpose(0, 2, 3, 1, 4)

    qs = jnp.pad(to_sub(q), ((0, 0), (0, 0), (0, 0), (0, pad), (0, 0)))
    kpad = ((0, 0), (0, 0), (0, 0), (blk, pad + blk), (0, 0))
    ks = jnp.pad(to_sub(k), kpad)
    vs = jnp.pad(to_sub(v), kpad)
    qb = qs.reshape(b, dilation, h, nb, blk, hd)
    kb = ks.reshape(b, dilation, h, nb + 2, blk, hd)
    vb = vs.reshape(b, dilation, h, nb + 2, blk, hd)
    kn = jnp.concatenate([kb[:, :, :, :-2], kb[:, :, :, 1:-1], kb[:, :, :, 2:]], axis=4)
    vn = jnp.concatenate([vb[:, :, :, :-2], vb[:, :, :, 1:-1], vb[:, :, :, 2:]], axis=4)

    scores = jnp.einsum('brhnqd,brhnkd->brhnqk', qb, kn,
                        preferred_element_type=jnp.float32) * (hd ** -0.5)
    qi = jnp.arange(blk)[:, None]
    kj = jnp.arange(3 * blk)[None, :]
    rel = kj - blk - qi
    key_pos = jnp.arange(nb)[:, None, None] * blk + kj[None] - blk
    valid = (jnp.abs(rel) <= half)[None] & (key_pos >= 0) & (key_pos < t)
    scores = jnp.where(valid, scores, -jnp.inf)
    lse = jax.nn.logsumexp(scores, axis=-1)
    p = jnp.exp(scores - lse[..., None])
    out = jnp.einsum('brhnqk,brhnkd->brhnqd', p.astype(v.dtype), vn,
                     preferred_element_type=jnp.float32)
    out = out.reshape(b, dilation, h, tp, hd)[:, :, :, :t].transpose(0, 3, 1, 2, 4).reshape(b, s, h, hd)
    lse = lse.reshape(b, dilation, h, tp)[..., :t].transpose(0, 3, 1, 2).reshape(b, s, h)
    return out, lse


def _dilated_attention_mixture(q, k, v):
    outs, lses = [], []
    for window, dilation in DILATED_PATTERNS:
        o, l = _dilated_window_attention(q, k, v, dilation, window // (2 * dilation))
        outs.append(o)
        lses.append(l)
    w = jax.nn.softmax(jnp.stack(lses, axis=0), axis=0)
    return jnp.einsum('pbsh,pbshd->bshd', w, jnp.stack(outs, axis=0))


def _centred_mean_minus_identity(u, window):
    s = u.shape[1]
    uf = u.astype(jnp.float32)
    csum = jnp.pad(jnp.cumsum(uf, axis=1), ((0, 0), (1, 0), (0, 0)))
    pos = jnp.arange(s)
    lo = jnp.clip(pos - window // 2, 0, s)
    hi = jnp.clip(pos + window // 2, 0, s)
    total = jnp.take(csum, hi, axis=1) - jnp.take(csum, lo, axis=1)
    count = (hi - lo).astype(jnp.float32)[None, :, None]
    return total / count - uf


def _layer(x, norm_g, w_in, w_pool, pool_scale, w_out):
    b, s, _ = x.shape
    h = _rmsnorm(x, norm_g)
    proj = jnp.einsum('bsd,de->bse', h, w_in)
    q, k, v, gate_a, u_pool, gate_p = jnp.split(
        proj, [D_ATTN, 2 * D_ATTN, 3 * D_ATTN, 4 * D_ATTN, 4 * D_ATTN + D_POOL], axis=-1)
    q = _rope(q.reshape(b, s, N_HEADS, HEAD_DIM))
    k = _rope(k.reshape(b, s, N_HEADS, HEAD_DIM))
    v = v.reshape(b, s, N_HEADS, HEAD_DIM)
    attn = _dilated_attention_mixture(q, k, v).reshape(b, s, D_ATTN).astype(x.dtype)
    u = u_pool.reshape(b, s, N_POOL_GROUPS, POOL_GROUP_DIM)
    pooled = jnp.stack([_centred_mean_minus_identity(u[:, :, g], w)
                        for g, w in enumerate(POOL_WINDOWS)], axis=2)
    pool = jnp.einsum('bsgc,gcd->bsgd', pooled.astype(x.dtype), w_pool).reshape(b, s, D_POOL) * pool_scale
    y = jnp.concatenate([attn * jax.nn.silu(gate_a), pool * jax.nn.silu(gate_p)], axis=-1)
    return x + jnp.einsum('bse,ed->bsd', y, w_out)


def _trunk(x, norm_g, w_in, w_pool, pool_scale, w_out, final_norm_g):
    for i in range(DEPTH):
        x = _layer(x, norm_g[i], w_in[i], w_pool[i], pool_scale[i], w_out[i])
    return _rmsnorm(x, final_norm_g)


def setup_inputs(seed: int = 0) -> dict:
    key = jax.random.key(seed)
    ks = jax.random.split(key, 8)
    f32 = jnp.float32
    x_prompt = jax.random.normal(ks[0], (BATCH, SEQ, D_MODEL), f32)
    x_sample = jax.random.normal(ks[1], (DEC_BATCH, DEC_SEQ, D_MODEL), f32)
    norm_g = 1.0 + 0.05 * jax.random.normal(ks[2], (DEPTH, D_MODEL), f32)
    w_in = jax.random.normal(ks[3], (DEPTH, D_MODEL, D_IN), f32) * D_MODEL ** -0.5
    w_pool = jax.random.normal(ks[4], (DEPTH, N_POOL_GROUPS, POOL_GROUP_DIM, POOL_GROUP_DIM), f32) * POOL_GROUP_DIM ** -0.5
    pool_scale = 1.0 + 0.1 * jax.random.normal(ks[5], (DEPTH, D_POOL), f32)
    w_out = jax.random.normal(ks[6], (DEPTH, D_MIX, D_MODEL), f32) * D_MIX ** -0.5
    final_norm_g = 1.0 + 0.05 * jax.random.normal(ks[7], (D_MODEL,), f32)
    return {"x_prompt": x_prompt, "x_sample": x_sample, "norm_g": norm_g, "w_in": w_in,
            "w_pool": w_pool, "pool_scale": pool_scale, "w_out": w_out,
            "final_norm_g": final_norm_g}


def reference(x_prompt, x_sample, norm_g, w_in, w_pool, pool_scale, w_out, final_norm_g):
    y_prompt = _trunk(x_prompt, norm_g, w_in, w_pool, pool_scale, w_out, final_norm_g)
    y_sample = _trunk(x_sample, norm_g, w_in, w_pool, pool_scale, w_out, final_norm_g)
    return (y_prompt, y_sample)
```

```python
import numpy as np
import ml_dtypes
from contextlib import ExitStack
import concourse.bass as bass
import concourse.mybir as mybir
from concourse.bass_utils import run_bass_kernel_spmd

F32 = mybir.dt.float32
BF16 = mybir.dt.bfloat16
AF = mybir.ActivationFunctionType
ALU = mybir.AluOpType
SEG = 2048
PATTERNS = (1, 4, 16)
POOL_W = (2, 4, 8, 16)
RMS_EPS = 1e-6


DBG = {}


class Cfg:
    def __init__(self, DM=1024, HP=8, PCG=2, L=4, NSEG=8):
        self.DM, self.HP, self.PCG, self.L, self.NSEG = DM, HP, PCG, L, NSEG
        self.PG = 4
        self.KC = DM // 128
        self.PC = self.PG * PCG
        self.DA = 128 * HP
        self.DP = 128 * self.PC
        self.DIN = 4 * self.DA + 2 * self.DP
        self.DMIX = self.DA + self.DP
        self.MC = HP + self.PC
        self.NCH = 4 * HP + 2 * self.PC
        self.NT = NSEG * SEG
        self.NTT = self.NT // 512
        self.G = 128 * PCG


class Rec:
    ENG = ("pe", "act", "dve", "pool", "sp")

    def __init__(self):
        self.ops = {e: [] for e in self.ENG}
        self.cnt = {}
        self.seen = {e: {} for e in self.ENG}

    def _waits(self, eng, deps):
        waits = []
        for d in deps:
            if d is None:
                continue
            key, val = d
            if self.seen[eng].get(key, 0) < val:
                self.seen[eng][key] = val
                waits.append((key, val))
        return waits

    def op(self, eng, fn, deps=(), sig=True):
        waits = self._waits(eng, deps)
        tok = None
        if sig:
            self.cnt[eng] = self.cnt.get(eng, 0) + 1
            tok = (eng, self.cnt[eng])
        self.ops[eng].append((waits, fn, eng if sig else None, 1))
        return tok

    def dma(self, q, fn, sem, deps=()):
        waits = self._waits(q, deps)
        self.cnt[sem] = self.cnt.get(sem, 0) + 16
        tok = (sem, self.cnt[sem])
        self.ops[q].append((waits, fn, sem, 16))
        return tok

    def wait_only(self, eng, deps):
        waits = self._waits(eng, deps)
        if waits:
            self.ops[eng].append((waits, None, None, 0))


def build_program(cfg, phases=(1, 2, 3, 4)):
    c = cfg
    DM, HP, PCG, L, NSEG, KC, PC, PG = c.DM, c.HP, c.PCG, c.L, c.NSEG, c.KC, c.PC, c.PG
    DIN, MC, NCH, NT, NTT, G = c.DIN, c.MC, c.NCH, c.NT, c.NTT, c.G
    nc = bass.Bass("TRN2", target_bir_lowering=False)
    dt = nc.dram_tensor
    x_in = dt("x", [NT, DM], F32, kind="ExternalInput").ap()
    y_out = dt("y", [NT, DM], F32, kind="ExternalOutput").ap()
    w_in = dt("w_in", [L, DM, DIN], F32, kind="ExternalInput").ap()
    w_out = dt("w_out", [L, c.DMIX, DM], F32, kind="ExternalInput").ap()
    w_pool = dt("w_pool", [L, PG, G, G], F32, kind="ExternalInput").ap()
    gB_d = dt("gB", [L + 1, 128, DM], F32, kind="ExternalInput").ap()
    psc_d = dt("psc", [L, 128, PC], F32, kind="ExternalInput").ap()
    ropeC_d = dt("ropeC", [128, NT], F32, kind="ExternalInput").ap()
    ropeS_d = dt("ropeS", [128, NT], F32, kind="ExternalInput").ap()
    segfl_d = dt("segfl", [128, 2 * NSEG], F32, kind="ExternalInput").ap()
    pfl_d = dt("pfl", [128, 2 * NSEG], F32, kind="ExternalInput").ap()
    prc_d = dt("prc", [128, PG * NSEG * 16], F32, kind="ExternalInput").ap()
    ident_d = dt("ident", [128, 128], BF16, kind="ExternalInput").ap()
    perm_d = dt("perm", [128, 128], BF16, kind="ExternalInput").ap()
    mask_d = dt("mask0", [128, 512], BF16, kind="ExternalInput").ap()
    PJ = dt("PJ", [NCH, 128, NT], BF16, kind="Internal").ap()
    YT = dt("YT", [MC, 128, NT], BF16, kind="Internal").ap()
    XS = dt("XS", [NT, DM], F32, kind="Internal").ap()

    R = Rec()
    es = ExitStack()
    sb = lambda n, s, d: es.enter_context(nc.sbuf_tensor(n, s, d))
    NBF = max(KC * DIN + 8 * DM + 2 * KC * 512 + 9 * 512 + DM, 46080, MC * DM + 2 * MC * 512 + DM)
    NFP = 12288
    BF = sb("BF", [128, NBF], BF16)
    FP = sb("FP", [128, NFP], F32)
    ident = sb("ident_sb", [128, 128], BF16)
    perm = sb("perm_sb", [128, 128], BF16)
    MK0 = sb("MK0", [128, 512], BF16)
    segfl = sb("segfl_sb", [128, 2 * NSEG], F32)
    pfl = sb("pfl_sb", [128, 2 * NSEG], F32)
    prc = sb("prc_sb", [128, PG * NSEG * 16], F32)
    gB = sb("gBt", [128, DM], F32)
    psc = sb("psct", [128, PC], F32)
    stat = sb("stat", [128, 64], F32)
    epsc = sb("epsc", [128, 1], F32)
    Bk = [es.enter_context(nc.psum_tensor(f"B{i}", [128, 512], F32)) for i in range(6)]
    Tk = [es.enter_context(nc.psum_tensor(f"T{i}", [128, 1024], BF16)) for i in range(2)]

    class Arena:
        def __init__(self, t):
            self.t, self.off = t, 0

        def reset(self):
            self.off = 0

        def take(self, n):
            a = self.t[:, self.off:self.off + n]
            self.off += n
            assert self.off <= self.t.shape[1], (self.off, self.t.shape)
            return a

    bfa, fpa = Arena(BF), Arena(FP)
    barrier = []

    def phase_sync(tokens):
        for e in Rec.ENG:
            R.wait_only(e, tokens)

    last = {}

    def OP(eng, fn, deps=(), sig=True):
        t = R.op(eng, fn, deps, sig)
        if t is not None:
            last[eng] = t
        return t

    def DMA(q, fn, sem, deps=()):
        t = R.dma(q, fn, sem, deps)
        last[sem] = t
        return t

    def end_phase():
        phase_sync([v for v in last.values()])

    ctoks = []
    for i, (dst, src) in enumerate([(ident, ident_d), (perm, perm_d), (MK0, mask_d), (segfl, segfl_d),
                                    (pfl, pfl_d), (prc, prc_d)]):
        ctoks.append(DMA("sp", lambda e, dst=dst, src=src: e.dma_start(out=dst[:], in_=src[:]), f"c{i}"))
    ctoks.append(OP("pool", lambda e: e.memset(epsc[:], RMS_EPS)))
    phase_sync(ctoks)

    def phase1(l):
        bfa.reset(); fpa.reset()
        Wi = bfa.take(KC * DIN).rearrange("p (k n) -> p k n", k=KC)
        hb = [bfa.take(4 * DM).rearrange("p (s d) -> p s d", s=4) for _ in range(2)]
        hT = [bfa.take(KC * 512).rearrange("p (k n) -> p k n", k=KC) for _ in range(2)]
        qb = [bfa.take(512) for _ in range(3)]
        NST = 6
        st = [bfa.take(512) for _ in range(NST)]
        junk = bfa.take(DM)
        XT = [fpa.take(DM) for _ in range(4)]
        CC = [fpa.take(512) for _ in range(2)]
        SSn = [fpa.take(512) for _ in range(2)]
        t1 = [fpa.take(512) for _ in range(2)]
        t2 = [fpa.take(512) for _ in range(2)]
        x_src = x_in if l == 0 else XS

        wt = [DMA("pool", lambda e, kc=kc: e.dma_start(out=Wi[:, kc, :], in_=w_in[l, kc * 128:(kc + 1) * 128, :]),
                  f"w{kc}") for kc in range(KC)]
        gt = DMA("sp", lambda e: e.dma_start(out=gB[:], in_=gB_d[l]), "g")

        xt_free = [None] * 4
        hb_free = [None] * 2
        hT_free = [None] * 2
        tb_free = [None] * 2
        pa_free = [None] * 4
        pr_free = [None] * 2
        qb_free = [None] * 3
        t1_free = [None] * 2
        st_free = [None] * NST
        cs_free = [None] * 2
        state = {"st": 0, "qk": 0}

        def norm_tile(t):
            tb = t % 2
            toks = []
            xts, sqs = [], []
            c0 = (4 * t) % 16
            for s in range(4):
                i = 4 * t + s
                sl = i % 4
                r0 = t * 512 + s * 128
                xt = DMA("sp", lambda e, sl=sl, r0=r0: e.dma_start(out=XT[sl], in_=x_src[r0:r0 + 128, :]),
                         f"x{sl}", deps=[xt_free[sl]])
                col = c0 + s
                a = OP("act", lambda e, sl=sl, col=col: e.activation(out=junk, in_=XT[sl], func=AF.Square,
                                                                     accum_out=stat[:, col:col + 1]), deps=[xt, state.get("junk")])
                state["junk"] = a
                xts.append(xt); sqs.append(a)
            b = OP("act", lambda e, c0=c0: e.activation(out=stat[:, 16 + c0:20 + c0], in_=stat[:, c0:c0 + 4], func=AF.Sqrt,
                                                        scale=1.0 / DM, bias=epsc[:, 0:1]), deps=sqs)
            b2 = OP("dve", lambda e, c0=c0: e.reciprocal(out=stat[:, 32 + c0:36 + c0], in_=stat[:, 16 + c0:20 + c0]), deps=[b])
            for s in range(4):
                sl = (4 * t + s) % 4
                col = c0 + s
                h = OP("dve", lambda e, sl=sl, s=s, col=col, tb=tb: e.scalar_tensor_tensor(
                    out=hb[tb][:, s, :], in0=XT[sl], scalar=stat[:, 32 + col:33 + col], in1=gB[:],
                    op0=ALU.mult, op1=ALU.mult), deps=[b2, xts[s], gt, hb_free[tb]])
                xt_free[sl] = h
                toks.append(h)
            return toks

        def transposes(t, htoks):
            tb = t % 2
            evs = []
            for f in range(KC // 2):
                bk = f % 2
                pt = None
                for kk in range(2):
                    for s in range(4):
                        kc = 2 * f + kk
                        lastone = (kk == 1 and s == 3)
                        pt = OP("pe", lambda e, bk=bk, kk=kk, s=s, kc=kc, tb=tb: e.transpose(
                            out=Tk[bk][:, kk * 512 + s * 128: kk * 512 + (s + 1) * 128],
                            in_=hb[tb][:, s, kc * 128:(kc + 1) * 128], identity=ident[:]),
                            deps=[htoks[s], tb_free[bk]], sig=lastone)
                ev = OP("act", lambda e, bk=bk, f=f, tb=tb: e.activation(
                    out=hT[tb][:, 2 * f:2 * f + 2, :], in_=Tk[bk][:, :].rearrange("p (k n) -> p k n", k=2),
                    func=AF.Copy), deps=[pt, hT_free[tb]])
                tb_free[bk] = ev
                evs.append(ev)
            hb_free[tb] = pt
            return evs

        def chunk(j, t, hT_toks, pending):
            tb = t % 2
            pa = j % 4
            mm = None
            for kc in range(KC):
                mm = OP("pe", lambda e, pa=pa, kc=kc, j=j, tb=tb: e.matmul(
                    Bk[pa][:, :], lhsT=Wi[:, kc, j * 128:(j + 1) * 128], rhs=hT[tb][:, kc, :],
                    start=(kc == 0), stop=(kc == KC - 1)),
                    deps=[wt[kc]] + list(hT_toks), sig=(kc == KC - 1))
            if pending:
                pending.pop()()
            k = state["st"] % NST
            state["st"] += 1
            dst = PJ[j, :, t * 512:(t + 1) * 512]
            if j < 2 * HP:
                n = state["qk"]; state["qk"] += 1
                qs, ts, pr, cs = n % 3, n % 2, n % 2, t % 2
                a = OP("act", lambda e, pa=pa, qs=qs: e.activation(out=qb[qs], in_=Bk[pa][:, :], func=AF.Copy),
                       deps=[mm, qb_free[qs]])
                d1 = OP("dve", lambda e, pa=pa, ts=ts, cs=cs: e.tensor_tensor(out=t1[ts], in0=Bk[pa][:, :], in1=CC[cs],
                                                                              op=ALU.mult),
                        deps=[mm, t1_free[ts], rope_c[cs]])

                def perm_mm(qs=qs, pr=pr, ts=ts, cs=cs, a=a, d1=d1, k=k, dst=dst, pa=pa):
                    pm = OP("pe", lambda e: e.matmul(Bk[4 + pr][:, :], lhsT=perm[:], rhs=qb[qs], start=True, stop=True),
                            deps=[a, pr_free[pr]])
                    qb_free[qs] = pm
                    d2 = OP("dve", lambda e: e.tensor_tensor(out=t2[ts], in0=Bk[4 + pr][:, :], in1=SSn[cs], op=ALU.mult),
                            deps=[pm, d1, rope_s[cs], t1_free[ts]])
                    pr_free[pr] = d2
                    ad = OP("pool", lambda e: e.tensor_tensor(out=st[k], in0=t1[ts], in1=t2[ts], op=ALU.add),
                            deps=[d1, d2, st_free[k]])
                    t1_free[ts] = ad
                    st_free[k] = DMA("sp", lambda e: e.dma_start(out=dst, in_=st[k]), f"st{k}", deps=[ad])
                    rope_use[cs] = ad
                pending.append(perm_mm)
                pa_wait[pa] = [a, d1]
            else:
                if j < 3 * HP:
                    ev = OP("act", lambda e, pa=pa, k=k: e.activation(out=st[k], in_=Bk[pa][:, :], func=AF.Copy),
                            deps=[mm, st_free[k]])
                elif j < 4 * HP or j >= 4 * HP + PC:
                    ev = OP("act", lambda e, pa=pa, k=k: e.activation(out=st[k], in_=Bk[pa][:, :], func=AF.Silu),
                            deps=[mm, st_free[k]])
                else:
                    ev = OP("dve", lambda e, pa=pa, k=k: e.tensor_copy(out=st[k], in_=Bk[pa][:, :]),
                            deps=[mm, st_free[k]])
                pa_wait[pa] = [ev]
                st_free[k] = DMA("sp", lambda e, k=k, dst=dst: e.dma_start(out=dst, in_=st[k]), f"st{k}", deps=[ev])

        pa_wait = [[] for _ in range(4)]
        rope_c = [None, None]
        rope_s = [None, None]
        rope_use = [None, None]

        htoks = norm_tile(0)
        hT_toks = transposes(0, htoks)
        for t in range(NTT):
            cs = t % 2
            rc = DMA("sp", lambda e, cs=cs, t=t: e.dma_start(out=CC[cs], in_=ropeC_d[:, t * 512:(t + 1) * 512]),
                     f"rc{cs}", deps=[rope_use[cs]])
            rs = DMA("sp", lambda e, cs=cs, t=t: e.dma_start(out=SSn[cs], in_=ropeS_d[:, t * 512:(t + 1) * 512]),
                     f"rs{cs}", deps=[rope_use[cs]])
            rope_c[cs] = rc
            rope_s[cs] = rs
            if t + 1 < NTT:
                nh = norm_tile(t + 1)
            pending = []
            nxt = None
            for j in range(NCH):
                pa = j % 4
                R.wait_only("pe", pa_wait[pa])
                pa_wait[pa] = []
                chunk(j, t, hT_toks, pending)
                if j == NCH // 2 and t + 1 < NTT:
                    nxt = transposes(t + 1, nh)
            if pending:
                pending.pop()()
            hT_free[t % 2] = last["pe"]
            if t + 1 < NTT:
                hT_toks = nxt
        end_phase()

    def phase2(l):
        bfa.reset(); fpa.reset()
        KT = [bfa.take(4096) for _ in range(2)]
        VT = [bfa.take(4096) for _ in range(2)]
        QT = [[bfa.take(2048) for _ in range(2)] for _ in range(2)]
        GA = [bfa.take(2048) for _ in range(2)]
        E = [bfa.take(512) for _ in range(3)]
        P = [bfa.take(512) for _ in range(3)]
        NV = 12
        VA = [bfa.take(256) for _ in range(NV)]
        YB = [bfa.take(2048) for _ in range(2)]
        MKV = [[bfa.take(512) for _ in range(3)] for _ in range(2)]
        ACC = [[fpa.take(2048) for _ in range(2)] for _ in range(2)]
        RR = fpa.take(2048)
        TT = fpa.take(2048)
        init = []
        for b in KT + VT + QT[0] + QT[1]:
            init.append(OP("pool", lambda e, b=b: e.memset(b, 0.0)))
        for v in VA:
            init.append(OP("pool", lambda e, v=v: e.memset(v, 1.0)))
        phase_sync(init)

        kv_free = [None, None]; q_free = [None, None]; ga_free = [None, None]
        e_free = [None] * 3; p_free = [None] * 3
        s_free = [None, None]; t_free = [None, None]
        o_free = [None, None]
        va_free = [None] * NV
        acc_free = [None, None]; yb_free = [None, None]; mk_free = [None, None]
        rr_free = None; tt_free = None
        vstate = {"n": 0}
        it = 0
        for hp in range(HP):
            for cseg in range(NSEG):
                sl = it % 2
                lo = SEG * cseg - 1024
                g0, g1 = max(lo, 0), min(lo + 4096, NT)
                kt = DMA("sp", lambda e, sl=sl, g0=g0, g1=g1, lo=lo, hp=hp: e.dma_start(
                    out=KT[sl][:, g0 - lo:g1 - lo], in_=PJ[HP + hp, :, g0:g1]), f"kt{sl}", deps=[kv_free[sl]])
                vt = DMA("sp", lambda e, sl=sl, g0=g0, g1=g1, lo=lo, hp=hp: e.dma_start(
                    out=VT[sl][:, g0 - lo:g1 - lo], in_=PJ[2 * HP + hp, :, g0:g1]), f"vt{sl}", deps=[kv_free[sl]])
                qt0 = DMA("sp", lambda e, sl=sl, cseg=cseg, hp=hp: e.dma_start(
                    out=QT[sl][0][0:64, :], in_=PJ[hp, 0:64, cseg * SEG:(cseg + 1) * SEG]), f"qa{sl}", deps=[q_free[sl]])
                qt = DMA("sp", lambda e, sl=sl, cseg=cseg, hp=hp: e.dma_start(
                    out=QT[sl][1][64:128, :], in_=PJ[hp, 64:128, cseg * SEG:(cseg + 1) * SEG]), f"qb{sl}", deps=[q_free[sl]])
                gat = DMA("sp", lambda e, sl=sl, cseg=cseg, hp=hp: e.dma_start(
                    out=GA[sl], in_=PJ[3 * HP + hp, :, cseg * SEG:(cseg + 1) * SEG]), f"ga{sl}", deps=[ga_free[sl]])
                cL = segfl[:, 2 * cseg:2 * cseg + 1]
                cR = segfl[:, 2 * cseg + 1:2 * cseg + 2]
                mL, mR, mLR = MKV[sl]
                av = lambda m: m.rearrange("p (h x) -> p h x", h=2)[:, :, 0:128]
                bv = lambda m: m.rearrange("p (h x) -> p h x", h=2)[:, :, 128:256]
                m1 = OP("pool", lambda e, mL=mL: e.tensor_copy(out=mL, in_=MK0[:]), deps=[mk_free[sl]])
                m2 = OP("pool", lambda e, mL=mL, cL=cL: e.tensor_scalar(out=av(mL), in0=av(MK0[:]), scalar1=cL, scalar2=None,
                                                                        op0=ALU.mult), deps=[m1])
                m3 = OP("pool", lambda e, mR=mR: e.tensor_copy(out=mR, in_=MK0[:]), deps=[mk_free[sl]])
                m4 = OP("pool", lambda e, mR=mR, cR=cR: e.tensor_scalar(out=bv(mR), in0=bv(MK0[:]), scalar1=cR, scalar2=None,
                                                                        op0=ALU.mult), deps=[m3])
                m5 = OP("pool", lambda e, mLR=mLR, mL=mL: e.tensor_copy(out=mLR, in_=mL), deps=[m2])
                m6 = OP("pool", lambda e, mLR=mLR, cR=cR: e.tensor_scalar(out=bv(mLR), in0=bv(MK0[:]), scalar1=cR,
                                                                          scalar2=None, op0=ALU.mult), deps=[m5])
                mtok = [m2, m4, m6]

                combos = []
                for d in PATTERNS:
                    nb = SEG // (128 * d)
                    if d == 1:
                        order = [(0, b) for b in range(16)]
                    elif d == 4:
                        order = [(r, b) for b in range(4) for r in range(4)]
                    else:
                        order = [(r, 0) for r in range(16)]
                    for (r, b) in order:
                        combos.append((d, r, b, nb))
                if 'combos' in DBG:
                    combos = [combos[ii] for ii in DBG['combos']]
                vcache = {}
                pv_pending = None
                copy_toks = [None] * 4
                add_toks = []
                cp_all = []
                slot_key = {}
                acc = ACC[sl]
                mask_users = []
                kv_users = []
                for i, (d, r, b, nb) in enumerate(combos):
                    gidx, k = i // 4, i % 4
                    pat = i // 16
                    gi = gidx % 4
                    vts = []
                    newt = []
                    for tau in (b, b + 1):
                        key = (d, r, tau)
                        if key not in vcache:
                            slot = vstate["n"] % NV
                            vstate["n"] += 1
                            vcache[key] = [slot, None]
                            slot_key[slot] = key
                            newt.append((key, slot))
                        assert slot_key[vcache[key][0]] == key
                        vts.append(key)
                    tbk = i % 2
                    for n, (key, slot) in enumerate(newt):
                        start = 1024 + d * (128 * key[2] - 64) + r
                        tp = OP("pe", lambda e, tbk=tbk, n=n, start=start, d=d, sl=sl: e.transpose(
                            out=Tk[tbk][:, n * 128:(n + 1) * 128],
                            in_=VT[sl][:, start:start + 127 * d + 1:d], identity=ident[:]),
                            deps=[vt, t_free[tbk]])
                        ev = OP("dve", lambda e, tbk=tbk, n=n, slot=slot: e.tensor_copy(
                            out=VA[slot].rearrange("p (a x) -> p a x", a=4)[:, 0:4:3, :],
                            in_=Tk[tbk][:, n * 128:(n + 1) * 128].rearrange("p (a x) -> p a x", a=2)),
                            deps=[tp, va_free[slot]])
                        vcache[key][1] = ev
                        t_free[tbk] = ev
                    STG = DBG.get('stage', 9)
                    if STG < 2:
                        continue
                    sb_ = i % 2
                    qs0 = d * 128 * b + r
                    smm = None
                    for hh in range(2):
                        for ab in range(2):
                            ks0 = 1024 + d * (128 * (b + ab) - 64) + r
                            smm = OP("pe", lambda e, sb_=sb_, hh=hh, ab=ab, ks0=ks0, qs0=qs0, d=d, sl=sl: e.matmul(
                                Bk[sb_][:, (2 * hh + ab) * 128:(2 * hh + ab + 1) * 128],
                                lhsT=KT[sl][:, ks0:ks0 + 127 * d + 1:d],
                                rhs=QT[sl][hh][:, qs0:qs0 + 127 * d + 1:d], start=True, stop=True),
                                deps=[kt, qt0, qt, s_free[sb_]], sig=(hh == 1 and ab == 1))
                    if STG < 3:
                        continue
                    es_ = i % 3
                    ex = OP("act", lambda e, es_=es_, sb_=sb_: e.activation(out=E[es_], in_=Bk[sb_][:, :], func=AF.Exp,
                                                                           scale=0.125), deps=[smm, e_free[es_]])
                    s_free[sb_] = ex
                    if STG < 4:
                        continue
                    first, lastb = (b == 0), (b == nb - 1)
                    if first and lastb:
                        mk, mt = MKV[sl][2], mtok[2]
                    elif first:
                        mk, mt = MKV[sl][0], mtok[0]
                    elif lastb:
                        mk, mt = MKV[sl][1], mtok[1]
                    else:
                        mk, mt = MK0[:], None
                    meng = "pool" if (i % 3 == 2) else "dve"
                    pm = OP(meng, lambda e, es_=es_, mk=mk: e.tensor_tensor(out=P[es_], in0=E[es_], in1=mk, op=ALU.mult),
                            deps=[ex, p_free[es_], mt])
                    e_free[es_] = pm
                    mask_users.append(pm)

                    if STG < 5:
                        continue
                    def do_pv(i=i, k=k, gidx=gidx, gi=gi, pat=pat, d=d, es_=es_, pm=pm, vts=vts, sl=sl, acc=acc):
                        oset = gidx % 2
                        pv = None
                        for hh in range(2):
                            for ab in range(2):
                                slot, evt = vcache[vts[ab]]
                                pv = OP("pe", lambda e, oset=oset, hh=hh, ab=ab, k=k, slot=slot, es_=es_: e.matmul(
                                    Bk[2 + 2 * oset + hh][:, k * 128:(k + 1) * 128],
                                    lhsT=VA[slot][:, 128 * hh:128 * hh + 128],
                                    rhs=P[es_][:, (2 * hh + ab) * 128:(2 * hh + ab + 1) * 128],
                                    start=(ab == 0), stop=(ab == 1)),
                                    deps=[pm, evt, o_free[oset] if k == 0 else None], sig=(hh == 1 and ab == 1))
                        p_free[es_] = pv
                        for key in vts:
                            va_free[vcache[key][0]] = pv
                        if k == 3 and STG >= 6:
                            toks = []
                            for hh in range(2):
                                ob = Bk[2 + 2 * oset + hh]
                                if pat == 0:
                                    tk = OP("act", lambda e, hh=hh, gi=gi, ob=ob: e.activation(
                                        out=acc[hh][:, 512 * gi:512 * gi + 512], in_=ob[:, :], func=AF.Copy),
                                        deps=[pv, acc_free[sl]])
                                    toks.append(tk)
                                else:
                                    if pat == 1:
                                        view = acc[hh][:, 512 * gi:512 * gi + 512].rearrange("p (t r) -> p r t", r=4)
                                        dd = [copy_toks[gi]]
                                    else:
                                        view = acc[hh][:, :].rearrange("p (t r) -> p r t", r=16)[:, 4 * gi:4 * gi + 4, :]
                                        dd = list(copy_toks) + add_toks[-1:]
                                    tk = OP("dve", lambda e, view=view, ob=ob: e.tensor_tensor(
                                        out=view, in0=ob[:, :].rearrange("p (r t) -> p r t", r=4), in1=view, op=ALU.add),
                                        deps=[pv] + dd)
                                    toks.append(tk)
                            if pat == 0:
                                copy_toks[gi] = toks[-1]
                                cp_all.append(toks[0]); cp_all.append(toks[1])
                            else:
                                add_toks.extend(toks)
                            o_free[oset] = toks[-1]
                        return pv

                    if pv_pending is not None:
                        pv_pending()
                    pv_pending = do_pv
                lastpv = pv_pending() if pv_pending is not None else None
                kv_free[sl] = lastpv
                q_free[sl] = lastpv
                mk_free[sl] = lastpv
                fin = add_toks[-4:] + cp_all
                if DBG.get('nonorm'):
                    it += 1
                    continue
                r0 = OP("dve", lambda e, acc=acc: e.reciprocal(out=RR[0:64, :], in_=acc[0][64:128, :]), deps=fin + [rr_free])
                r1 = OP("dve", lambda e, acc=acc: e.reciprocal(out=RR[64:128, :], in_=acc[1][0:64, :]), deps=fin + [rr_free])
                n0 = OP("pool", lambda e, acc=acc: e.tensor_tensor(out=TT[0:64, :], in0=acc[0][0:64, :], in1=RR[0:64, :],
                                                                   op=ALU.mult), deps=[r0, tt_free])
                n1 = OP("pool", lambda e, acc=acc: e.tensor_tensor(out=TT[64:128, :], in0=acc[1][64:128, :],
                                                                   in1=RR[64:128, :], op=ALU.mult), deps=[r1, tt_free])
                yy = OP("pool", lambda e, sl=sl: e.tensor_tensor(out=YB[sl], in0=TT, in1=GA[sl], op=ALU.mult),
                        deps=[n0, n1, gat, yb_free[sl]])
                rr_free = n1
                tt_free = yy
                acc_free[sl] = n1
                ga_free[sl] = yy
                yb_free[sl] = DMA("sp", lambda e, sl=sl, hp=hp, cseg=cseg: e.dma_start(
                    out=YT[hp, :, cseg * SEG:(cseg + 1) * SEG], in_=YB[sl]), f"yb{sl}", deps=[yy])
                it += 1
        end_phase()

    def phase2b(l):
        bfa.reset(); fpa.reset()
        EXT = SEG + 16
        UE = [[bfa.take(EXT) for _ in range(PCG)] for _ in range(2)]
        GP = [[bfa.take(SEG) for _ in range(PCG)] for _ in range(2)]
        PLD = [bfa.take(SEG) for _ in range(PCG)]
        YP = [bfa.take(SEG) for _ in range(2)]
        Wp = bfa.take(PG * PCG * G).rearrange("p (g c n) -> p g c n", g=PG, c=PCG)
        SA = fpa.take(EXT)
        SBb = fpa.take(EXT)
        TS = fpa.take(16)
        wts = []
        for g in range(PG):
            for ci in range(PCG):
                wts.append(DMA("pool", lambda e, g=g, ci=ci: e.dma_start(out=Wp[:, g, ci, :],
                                                                         in_=w_pool[l, g, ci * 128:(ci + 1) * 128, :]),
                               f"wp{g}_{ci}"))
        pt = DMA("sp", lambda e: e.dma_start(out=psc[:], in_=psc_d[l]), "psc")
        init = []
        for s_ in range(2):
            for ci in range(PCG):
                init.append(OP("pool", lambda e, u=UE[s_][ci]: e.memset(u, 0.0)))
        phase_sync(init + wts + [pt])
        ue_free = [None, None]; gp_free = [None, None]; pld_free = [None] * PCG
        yp_free = [None, None]; po_free = [None] * 6
        s_free = None
        it = 0
        pon = 0
        ypn = 0
        for g in range(PG):
            w = POOL_W[g]
            for cseg in range(NSEG):
                sl = it % 2
                lo = SEG * cseg - 8
                g0, g1 = max(lo, 0), min(lo + EXT, NT)
                uts, gts = [], []
                for ci in range(PCG):
                    pc = g * PCG + ci
                    uts.append(DMA("sp", lambda e, sl=sl, ci=ci, pc=pc, g0=g0, g1=g1, lo=lo: e.dma_start(
                        out=UE[sl][ci][:, g0 - lo:g1 - lo], in_=PJ[4 * HP + pc, :, g0:g1]), f"ue{sl}_{ci}",
                        deps=[ue_free[sl]]))
                    gts.append(DMA("sp", lambda e, sl=sl, ci=ci, pc=pc, cseg=cseg: e.dma_start(
                        out=GP[sl][ci], in_=PJ[4 * HP + PC + pc, :, cseg * SEG:(cseg + 1) * SEG]), f"gp{sl}_{ci}",
                        deps=[gp_free[sl]]))
                fL = pfl[:, 2 * cseg:2 * cseg + 1]
                fR = pfl[:, 2 * cseg + 1:2 * cseg + 2]
                plds = []
                for ci in range(PCG):
                    u = UE[sl][ci]
                    h1 = OP("pool", lambda e, u=u, fL=fL: e.tensor_scalar(out=u[:, 0:8], in0=u[:, 0:8], scalar1=fL,
                                                                          scalar2=None, op0=ALU.mult), deps=[uts[ci]])
                    h2 = OP("pool", lambda e, u=u, fR=fR: e.tensor_scalar(out=u[:, EXT - 8:EXT], in0=u[:, EXT - 8:EXT],
                                                                          scalar1=fR, scalar2=None, op0=ALU.mult),
                            deps=[uts[ci]])
                    a = OP("pool", lambda e, u=u: e.tensor_tensor(out=SA[:, 1:EXT], in0=u[:, 0:EXT - 1], in1=u[:, 1:EXT],
                                                                  op=ALU.add), deps=[h1, h2, s_free])
                    cur, oth = SA, SBb
                    lo_v, hi_v, sh = 1, EXT, 1
                    ww = 2
                    while ww < w:
                        nl, nh = lo_v + sh, hi_v - sh
                        a = OP("pool", lambda e, cur=cur, oth=oth, nl=nl, nh=nh, sh=sh: e.tensor_tensor(
                            out=oth[:, nl:nh], in0=cur[:, nl - sh:nh - sh], in1=cur[:, nl + sh:nh + sh], op=ALU.add),
                            deps=[a])
                        cur, oth = oth, cur
                        lo_v, hi_v, sh, ww = nl, nh, sh * 2, ww * 2
                    assert lo_v <= 8 and hi_v >= EXT - 8
                    pl = OP("dve", lambda e, cur=cur, u=u, ci=ci, w=w: e.scalar_tensor_tensor(
                        out=PLD[ci], in0=cur[:, 8:8 + SEG], scalar=1.0 / w, in1=u[:, 8:8 + SEG], op0=ALU.mult,
                        op1=ALU.subtract), deps=[a, pld_free[ci]])
                    pb = (g * NSEG + cseg) * 16
                    s1 = OP("dve", lambda e, cur=cur, pb=pb: e.tensor_tensor(out=TS[:, 0:8], in0=cur[:, 8:16],
                                                                             in1=prc[:, pb:pb + 8], op=ALU.mult), deps=[pl])
                    s2 = OP("dve", lambda e, u=u, ci=ci: e.tensor_tensor(out=PLD[ci][:, 0:8], in0=TS[:, 0:8], in1=u[:, 8:16],
                                                                         op=ALU.subtract), deps=[s1])
                    s3 = OP("dve", lambda e, cur=cur, pb=pb: e.tensor_tensor(out=TS[:, 8:16], in0=cur[:, SEG:SEG + 8],
                                                                             in1=prc[:, pb + 8:pb + 16], op=ALU.mult),
                            deps=[s2])
                    s4 = OP("dve", lambda e, u=u, ci=ci: e.tensor_tensor(out=PLD[ci][:, SEG - 8:SEG], in0=TS[:, 8:16],
                                                                         in1=u[:, SEG:SEG + 8], op=ALU.subtract), deps=[s3])
                    s_free = s4
                    plds.append(s4)
                ue_free[sl] = plds[-1]
                for do in range(PCG):
                    pc_out = g * PCG + do
                    ys = ypn % 2
                    ypn += 1
                    evs = []
                    for tt in range(4):
                        pb_ = pon % 6
                        pon += 1
                        mm = None
                        for ci in range(PCG):
                            mm = OP("pe", lambda e, pb_=pb_, g=g, ci=ci, do=do, tt=tt: e.matmul(
                                Bk[pb_][:, :], lhsT=Wp[:, g, ci, do * 128:(do + 1) * 128],
                                rhs=PLD[ci][:, tt * 512:(tt + 1) * 512], start=(ci == 0), stop=(ci == PCG - 1)),
                                deps=plds + [po_free[pb_]], sig=(ci == PCG - 1))
                        ev = OP("dve", lambda e, pb_=pb_, ys=ys, tt=tt, pc_out=pc_out, sl=sl, do=do: e.scalar_tensor_tensor(
                            out=YP[ys][:, tt * 512:(tt + 1) * 512], in0=Bk[pb_][:, :], scalar=psc[:, pc_out:pc_out + 1],
                            in1=GP[sl][do][:, tt * 512:(tt + 1) * 512], op0=ALU.mult, op1=ALU.mult),
                            deps=[mm, gts[do], yp_free[ys]])
                        po_free[pb_] = ev
                        evs.append(ev)
                    yp_free[ys] = DMA("sp", lambda e, ys=ys, pc_out=pc_out, cseg=cseg: e.dma_start(
                        out=YT[HP + pc_out, :, cseg * SEG:(cseg + 1) * SEG], in_=YP[ys]), f"yp{ys}", deps=[evs[-1]])
                    lastmm = mm
                for ci in range(PCG):
                    pld_free[ci] = lastmm
                gp_free[sl] = evs[-1]
                it += 1
        end_phase()

    def phase3(l):
        bfa.reset(); fpa.reset()
        Wo = bfa.take(MC * DM).rearrange("p (m n) -> p m n", m=MC)
        yT = [bfa.take(MC * 512).rearrange("p (m n) -> p m n", m=MC) for _ in range(2)]
        junk = bfa.take(DM)
        X3 = [fpa.take(DM) for _ in range(3)]
        XN = [fpa.take(DM) for _ in range(3)]
        x_src = x_in if l == 0 else XS
        final = (l == L - 1)
        NH = DM // 512
        wts = [DMA("pool", lambda e, m=m: e.dma_start(out=Wo[:, m, :], in_=w_out[l, m * 128:(m + 1) * 128, :]), f"wo{m}")
               for m in range(MC)]
        gt = DMA("sp", lambda e: e.dma_start(out=gB[:], in_=gB_d[L]), "g") if final else None
        phase_sync(wts)
        yt_free = [None, None]; x3_free = [None] * 3; xn_free = [None] * 3; po_free = [None] * 6
        pon = 0
        jstate = {}
        for t in range(NTT):
            tb = t % 2
            ytk = DMA("sp", lambda e, tb=tb, t=t: e.dma_start(
                out=yT[tb], in_=YT[:, :, t * 512:(t + 1) * 512].rearrange("m p n -> p m n")), f"yt{tb}",
                deps=[yt_free[tb]])
            for s in range(4):
                i = 4 * t + s
                xs = i % 3
                r0 = t * 512 + s * 128
                xt = DMA("sp", lambda e, xs=xs, r0=r0: e.dma_start(out=X3[xs], in_=x_src[r0:r0 + 128, :]), f"x3{xs}",
                         deps=[x3_free[xs]])
                adds = []
                for hf in range(NH):
                    pb_ = pon % 6
                    pon += 1
                    mm = None
                    for m in range(MC):
                        mm = OP("pe", lambda e, pb_=pb_, m=m, s=s, hf=hf, tb=tb: e.matmul(
                            Bk[pb_][:, :], lhsT=yT[tb][:, m, s * 128:(s + 1) * 128], rhs=Wo[:, m, hf * 512:(hf + 1) * 512],
                            start=(m == 0), stop=(m == MC - 1)), deps=[ytk, po_free[pb_]], sig=(m == MC - 1))
                    ad = OP("dve", lambda e, pb_=pb_, xs=xs, hf=hf: e.tensor_tensor(
                        out=XN[xs][:, hf * 512:(hf + 1) * 512], in0=Bk[pb_][:, :], in1=X3[xs][:, hf * 512:(hf + 1) * 512],
                        op=ALU.add), deps=[mm, xt, xn_free[xs]])
                    po_free[pb_] = ad
                    adds.append(ad)
                x3_free[xs] = adds[-1]
                if not final:
                    xn_free[xs] = DMA("sp", lambda e, xs=xs, r0=r0: e.dma_start(out=XS[r0:r0 + 128, :], in_=XN[xs]),
                                      f"xn{xs}", deps=adds)
                else:
                    col = i % 16
                    a = OP("act", lambda e, xs=xs, col=col: e.activation(out=junk, in_=XN[xs], func=AF.Square,
                                                                         accum_out=stat[:, col:col + 1]),
                           deps=adds + [jstate.get("junk")])
                    jstate["junk"] = a
                    b = OP("act", lambda e, col=col: e.activation(out=stat[:, 16 + col:17 + col], in_=stat[:, col:col + 1],
                                                                  func=AF.Sqrt, scale=1.0 / DM, bias=epsc[:, 0:1]), deps=[a])
                    b2 = OP("dve", lambda e, col=col: e.reciprocal(out=stat[:, 32 + col:33 + col],
                                                                   in_=stat[:, 16 + col:17 + col]), deps=[b])
                    fo = OP("dve", lambda e, xs=xs, col=col: e.scalar_tensor_tensor(
                        out=XN[xs], in0=XN[xs], scalar=stat[:, 32 + col:33 + col], in1=gB[:], op0=ALU.mult, op1=ALU.mult),
                        deps=[b2, gt])
                    xn_free[xs] = DMA("sp", lambda e, xs=xs, r0=r0: e.dma_start(out=y_out[r0:r0 + 128, :], in_=XN[xs]),
                                      f"xn{xs}", deps=[fo])
            yt_free[tb] = last["pe"]
        end_phase()

    for l in range(L):
        if 1 in phases:
            phase1(l)
        if 2 in phases:
            phase2(l)
        if 3 in phases:
            phase2b(l)
        if 4 in phases:
            phase3(l)

    keys = sorted(set(R.cnt.keys()))
    sems = {k: es.enter_context(nc.semaphore(f"s_{k}")) for k in keys}
    block = es.enter_context(nc.Block())

    def replay(engname):
        def f(eng):
            for waits, fn, semk, inc in R.ops[engname]:
                for (k, v) in waits:
                    eng.wait_ge(sems[k], v)
                if fn is None:
                    continue
                ins = fn(eng)
                if semk is not None:
                    ins.then_inc(sems[semk], inc)
        return f

    block.tensor(replay("pe"))
    block.scalar(replay("act"))
    block.vector(replay("dve"))
    block.gpsimd(replay("pool"))
    block.sync(replay("sp"))
    es.close()
    return nc


def _consts():
    ident = np.eye(128, dtype=np.float32)
    m = np.arange(128)
    sw = np.where((m % 64) < 32, m + 32, m - 32)
    perm = np.zeros((128, 128), np.float32)
    perm[sw, m] = 1.0
    i = np.arange(128)[:, None]
    j = np.arange(128)[None, :]
    ma = (i >= j).astype(np.float32)
    mb = (i <= j).astype(np.float32)
    mask0 = np.concatenate([ma, mb, ma, mb], axis=1)
    bf = ml_dtypes.bfloat16
    return ident.astype(bf), perm.astype(bf), mask0.astype(bf)


def _rope_tables(pos):
    hd = 64
    inv_freq = (np.float32(10000.0) ** (-(np.arange(0, hd, 2, dtype=np.float32)) / np.float32(hd))).astype(np.float32)
    ang = pos.astype(np.float32)[None, :] * inv_freq[:, None]
    cos = np.cos(ang).astype(np.float32)
    sin = np.sin(ang).astype(np.float32)
    m = np.arange(128)
    f = m % 32
    sgn = np.where((m % 64) < 32, -1.0, 1.0).astype(np.float32)
    return np.ascontiguousarray(cos[f]), np.ascontiguousarray(sin[f] * sgn[:, None])


def make_core_inputs(cfg, segs, xs, shared):
    NSEG, PG = cfg.NSEG, cfg.PG
    x = np.concatenate([xs[k][b, s:s + SEG] for (k, b, s, _, _) in segs], axis=0)
    pos = np.concatenate([np.arange(s, s + SEG) for (_, _, s, _, _) in segs])
    rc, rs = _rope_tables(pos)
    segfl = np.ones((128, 2 * NSEG), np.float32)
    pfl = np.ones((128, 2 * NSEG), np.float32)
    prc = np.zeros((128, PG, NSEG, 16), np.float32)
    for ci, (_, _, _, cl, cr) in enumerate(segs):
        segfl[:64, 2 * ci] = cl
        segfl[64:, 2 * ci + 1] = cr
        pfl[:, 2 * ci] = cl
        pfl[:, 2 * ci + 1] = cr
        for g, w in enumerate(POOL_W):
            jj = np.arange(8)
            left = np.where(jj < w // 2, jj + w // 2, w) if not cl else np.full(8, w)
            right = np.where(jj > 8 - w // 2, 8 - jj + w // 2, w) if not cr else np.full(8, w)
            prc[:, g, ci, 0:8] = 1.0 / left
            prc[:, g, ci, 8:16] = 1.0 / right
    d = dict(shared)
    d.update({"x": np.ascontiguousarray(x, dtype=np.float32), "ropeC": rc, "ropeS": rs, "segfl": segfl, "pfl": pfl,
              "prc": np.ascontiguousarray(prc.reshape(128, -1))})
    return d


def make_shared(cfg, norm_g, w_in, w_pool, pool_scale, w_out, final_norm_g):
    L, DM, PC = cfg.L, cfg.DM, cfg.PC
    ident, perm, mask0 = _consts()
    g_all = np.concatenate([norm_g, final_norm_g[None, :]], axis=0)
    gB = np.ascontiguousarray(np.broadcast_to(g_all[:, None, :], (L + 1, 128, DM)), dtype=np.float32)
    psc = np.ascontiguousarray(pool_scale.reshape(L, PC, 128).transpose(0, 2, 1), dtype=np.float32)
    return {"w_in": np.ascontiguousarray(w_in, dtype=np.float32), "w_out": np.ascontiguousarray(w_out, dtype=np.float32),
            "w_pool": np.ascontiguousarray(w_pool, dtype=np.float32), "gB": gB, "psc": psc,
            "ident": ident, "perm": perm, "mask0": mask0}


_NC_CACHE = {}


def run_cfg(cfg, core_segs, xs, shared, n_cores):
    key = (cfg.DM, cfg.HP, cfg.PCG, cfg.L, cfg.NSEG)
    if key not in _NC_CACHE:
        _NC_CACHE[key] = build_program(cfg)
    nc = _NC_CACHE[key]
    in_maps = [make_core_inputs(cfg, segs, xs, shared) for segs in core_segs]
    res = run_bass_kernel_spmd(nc, in_maps, core_ids=list(range(n_cores)))
    return [r["y"] for r in res.results]


def kernel(x_prompt, x_sample, norm_g, w_in, w_pool, pool_scale, w_out, final_norm_g):
    cfg = Cfg()
    xs = {"p": np.asarray(x_prompt, np.float32), "s": np.asarray(x_sample, np.float32)}
    shared = make_shared(cfg, np.asarray(norm_g), np.asarray(w_in), np.asarray(w_pool), np.asarray(pool_scale),
                         np.asarray(w_out), np.asarray(final_norm_g))
    NSEG = cfg.NSEG
    core_segs = []
    for b in range(2):
        core_segs.append([("p", b, i * SEG, int(i > 0), int(i < NSEG - 1)) for i in range(NSEG)])
    for cidx in range(4):
        core_segs.append([("s", cidx * 8 + i, 0, 0, 0) for i in range(NSEG)])
    core_segs.append(core_segs[2]); core_segs.append(core_segs[3])
    outs = run_cfg(cfg, core_segs, xs, shared, 8)
    y_p = np.stack([outs[0], outs[1]], axis=0).astype(np.float32)
    y_s = np.concatenate([outs[2 + cidx].reshape(8, SEG, cfg.DM) for cidx in range(4)], axis=0).astype(np.float32)
    return (y_p, y_s)
```

```python
import numpy as np
import ml_dtypes
from contextlib import ExitStack
import concourse.bass as bass
import concourse.mybir as mybir
from concourse.bass_utils import run_bass_kernel_spmd

F32 = mybir.dt.float32
BF16 = mybir.dt.bfloat16
AF = mybir.ActivationFunctionType
ALU = mybir.AluOpType
SEG = 2048
PATTERNS = (1, 4, 16)
POOL_W = (2, 4, 8, 16)
RMS_EPS = 1e-6


DBG = {}


class Cfg:
    def __init__(self, DM=1024, HP=8, PCG=2, L=4, NSEG=8):
        self.DM, self.HP, self.PCG, self.L, self.NSEG = DM, HP, PCG, L, NSEG
        self.PG = 4
        self.KC = DM // 128
        self.PC = self.PG * PCG
        self.DA = 128 * HP
        self.DP = 128 * self.PC
        self.DIN = 4 * self.DA + 2 * self.DP
        self.DMIX = self.DA + self.DP
        self.MC = HP + self.PC
        self.NCH = 4 * HP + 2 * self.PC
        self.NT = NSEG * SEG
        self.NTT = self.NT // 512
        self.G = 128 * PCG


class Rec:
    ENG = ("pe", "act", "dve", "pool", "sp")

    def __init__(self):
        self.ops = {e: [] for e in self.ENG}
        self.cnt = {}
        self.seen = {e: {} for e in self.ENG}

    def _waits(self, eng, deps):
        waits = []
        for d in deps:
            if d is None:
                continue
            key, val = d
            if self.seen[eng].get(key, 0) < val:
                self.seen[eng][key] = val
                waits.append((key, val))
        return waits

    def op(self, eng, fn, deps=(), sig=True):
        waits = self._waits(eng, deps)
        tok = None
        if sig:
            self.cnt[eng] = self.cnt.get(eng, 0) + 1
            tok = (eng, self.cnt[eng])
        self.ops[eng].append((waits, fn, eng if sig else None, 1))
        return tok

    def dma(self, q, fn, sem, deps=()):
        waits = self._waits(q, deps)
        self.cnt[sem] = self.cnt.get(sem, 0) + 16
        tok = (sem, self.cnt[sem])
        self.ops[q].append((waits, fn, sem, 16))
        return tok

    def wait_only(self, eng, deps):
        waits = self._waits(eng, deps)
        if waits:
            self.ops[eng].append((waits, None, None, 0))


def build_program(cfg, phases=(1, 2, 3, 4)):
    c = cfg
    DM, HP, PCG, L, NSEG, KC, PC, PG = c.DM, c.HP, c.PCG, c.L, c.NSEG, c.KC, c.PC, c.PG
    DIN, MC, NCH, NT, NTT, G = c.DIN, c.MC, c.NCH, c.NT, c.NTT, c.G
    nc = bass.Bass("TRN2", target_bir_lowering=False)
    dt = nc.dram_tensor
    x_in = dt("x", [NT, DM], F32, kind="ExternalInput").ap()
    y_out = dt("y", [NT, DM], F32, kind="ExternalOutput").ap()
    w_in = dt("w_in", [L, DM, DIN], F32, kind="ExternalInput").ap()
    w_out = dt("w_out", [L, c.DMIX, DM], F32, kind="ExternalInput").ap()
    w_pool = dt("w_pool", [L, PG, G, G], F32, kind="ExternalInput").ap()
    gB_d = dt("gB", [L + 1, 128, DM], F32, kind="ExternalInput").ap()
    psc_d = dt("psc", [L, 128, PC], F32, kind="ExternalInput").ap()
    ropeC_d = dt("ropeC", [128, NT], F32, kind="ExternalInput").ap()
    ropeS_d = dt("ropeS", [128, NT], F32, kind="ExternalInput").ap()
    segfl_d = dt("segfl", [128, 2 * NSEG], F32, kind="ExternalInput").ap()
    pfl_d = dt("pfl", [128, 2 * NSEG], F32, kind="ExternalInput").ap()
    prc_d = dt("prc", [128, PG * NSEG * 16], F32, kind="ExternalInput").ap()
    ident_d = dt("ident", [128, 128], BF16, kind="ExternalInput").ap()
    perm_d = dt("perm", [128, 128], BF16, kind="ExternalInput").ap()
    mask_d = dt("mask0", [128, 512], BF16, kind="ExternalInput").ap()
    PJ = dt("PJ", [NCH, 128, NT], BF16, kind="Internal").ap()
    YT = dt("YT", [MC, 128, NT], BF16, kind="Internal").ap()
    XS = dt("XS", [NT, DM], F32, kind="Internal").ap()

    R = Rec()
    es = ExitStack()
    sb = lambda n, s, d: es.enter_context(nc.sbuf_tensor(n, s, d))
    NBF = max(KC * DIN + 8 * DM + 2 * KC * 512 + 9 * 512 + DM, 46080, MC * DM + 2 * MC * 512 + DM)
    NFP = 12288
    BF = sb("BF", [128, NBF], BF16)
    FP = sb("FP", [128, NFP], F32)
    ident = sb("ident_sb", [128, 128], BF16)
    perm = sb("perm_sb", [128, 128], BF16)
    MK0 = sb("MK0", [128, 512], BF16)
    segfl = sb("segfl_sb", [128, 2 * NSEG], F32)
    pfl = sb("pfl_sb", [128, 2 * NSEG], F32)
    prc = sb("prc_sb", [128, PG * NSEG * 16], F32)
    gB = sb("gBt", [128, DM], F32)
    psc = sb("psct", [128, PC], F32)
    stat = sb("stat", [128, 64], F32)
    epsc = sb("epsc", [128, 1], F32)
    Bk = [es.enter_context(nc.psum_tensor(f"B{i}", [128, 512], F32)) for i in range(6)]
    Tk = [es.enter_context(nc.psum_tensor(f"T{i}", [128, 1024], BF16)) for i in range(2)]

    class Arena:
        def __init__(self, t):
            self.t, self.off = t, 0

        def reset(self):
            self.off = 0

        def take(self, n):
            a = self.t[:, self.off:self.off + n]
            self.off += n
            assert self.off <= self.t.shape[1], (self.off, self.t.shape)
            return a

    bfa, fpa = Arena(BF), Arena(FP)
    barrier = []

    def phase_sync(tokens):
        for e in Rec.ENG:
            R.wait_only(e, tokens)

    last = {}

    def OP(eng, fn, deps=(), sig=True):
        t = R.op(eng, fn, deps, sig)
        if t is not None:
            last[eng] = t
        return t

    def DMA(q, fn, sem, deps=()):
        t = R.dma(q, fn, sem, deps)
        last[sem] = t
        return t

    def end_phase():
        phase_sync([v for v in last.values()])

    ctoks = []
    for i, (dst, src) in enumerate([(ident, ident_d), (perm, perm_d), (MK0, mask_d), (segfl, segfl_d),
                                    (pfl, pfl_d), (prc, prc_d)]):
        ctoks.append(DMA("sp", lambda e, dst=dst, src=src: e.dma_start(out=dst[:], in_=src[:]), f"c{i}"))
    ctoks.append(OP("pool", lambda e: e.memset(epsc[:], RMS_EPS)))
    phase_sync(ctoks)

    def phase1(l):
        bfa.reset(); fpa.reset()
        Wi = bfa.take(KC * DIN).rearrange("p (k n) -> p k n", k=KC)
        hb = [bfa.take(4 * DM).rearrange("p (s d) -> p s d", s=4) for _ in range(2)]
        hT = [bfa.take(KC * 512).rearrange("p (k n) -> p k n", k=KC) for _ in range(2)]
        qb = [bfa.take(512) for _ in range(3)]
        NST = 6
        st = [bfa.take(512) for _ in range(NST)]
        junk = bfa.take(DM)
        XT = [fpa.take(DM) for _ in range(4)]
        CC = [fpa.take(512) for _ in range(2)]
        SSn = [fpa.take(512) for _ in range(2)]
        t1 = [fpa.take(512) for _ in range(2)]
        t2 = [fpa.take(512) for _ in range(2)]
        x_src = x_in if l == 0 else XS

        wt = [DMA("pool", lambda e, kc=kc: e.dma_start(out=Wi[:, kc, :], in_=w_in[l, kc * 128:(kc + 1) * 128, :]),
                  f"w{kc}") for kc in range(KC)]
        gt = DMA("sp", lambda e: e.dma_start(out=gB[:], in_=gB_d[l]), "g")

        xt_free = [None] * 4
        hb_free = [None] * 2
        hT_free = [None] * 2
        tb_free = [None] * 2
        pa_free = [None] * 4
        pr_free = [None] * 2
        qb_free = [None] * 3
        t1_free = [None] * 2
        st_free = [None] * NST
        cs_free = [None] * 2
        state = {"st": 0, "qk": 0}

        def norm_tile(t):
            tb = t % 2
            toks = []
            xts, sqs = [], []
            c0 = (4 * t) % 16
            for s in range(4):
                i = 4 * t + s
                sl = i % 4
                r0 = t * 512 + s * 128
                xt = DMA("sp", lambda e, sl=sl, r0=r0: e.dma_start(out=XT[sl], in_=x_src[r0:r0 + 128, :]),
                         f"x{sl}", deps=[xt_free[sl]])
                col = c0 + s
                a = OP("act", lambda e, sl=sl, col=col: e.activation(out=junk, in_=XT[sl], func=AF.Square,
                                                                     accum_out=stat[:, col:col + 1]), deps=[xt, state.get("junk")])
                state["junk"] = a
                xts.append(xt); sqs.append(a)
            b = OP("act", lambda e, c0=c0: e.activation(out=stat[:, 16 + c0:20 + c0], in_=stat[:, c0:c0 + 4], func=AF.Sqrt,
                                                        scale=1.0 / DM, bias=epsc[:, 0:1]), deps=sqs)
            b2 = OP("dve", lambda e, c0=c0: e.reciprocal(out=stat[:, 32 + c0:36 + c0], in_=stat[:, 16 + c0:20 + c0]), deps=[b])
            for s in range(4):
                sl = (4 * t + s) % 4
                col = c0 + s
                h = OP("dve", lambda e, sl=sl, s=s, col=col, tb=tb: e.scalar_tensor_tensor(
                    out=hb[tb][:, s, :], in0=XT[sl], scalar=stat[:, 32 + col:33 + col], in1=gB[:],
                    op0=ALU.mult, op1=ALU.mult), deps=[b2, xts[s], gt, hb_free[tb]])
                xt_free[sl] = h
                toks.append(h)
            return toks

        def transposes(t, htoks):
            tb = t % 2
            evs = []
            for f in range(KC // 2):
                bk = f % 2
                pt = None
                for kk in range(2):
                    for s in range(4):
                        kc = 2 * f + kk
                        lastone = (kk == 1 and s == 3)
                        pt = OP("pe", lambda e, bk=bk, kk=kk, s=s, kc=kc, tb=tb: e.transpose(
                            out=Tk[bk][:, kk * 512 + s * 128: kk * 512 + (s + 1) * 128],
                            in_=hb[tb][:, s, kc * 128:(kc + 1) * 128], identity=ident[:]),
                            deps=[htoks[s], tb_free[bk]], sig=lastone)
                ev = OP("act", lambda e, bk=bk, f=f, tb=tb: e.activation(
                    out=hT[tb][:, 2 * f:2 * f + 2, :], in_=Tk[bk][:, :].rearrange("p (k n) -> p k n", k=2),
                    func=AF.Copy), deps=[pt, hT_free[tb]])
                tb_free[bk] = ev
                evs.append(ev)
            hb_free[tb] = pt
            return evs

        def chunk(j, t, hT_toks, pending):
            tb = t % 2
            pa = j % 4
            mm = None
            for kc in range(KC):
                mm = OP("pe", lambda e, pa=pa, kc=kc, j=j, tb=tb: e.matmul(
                    Bk[pa][:, :], lhsT=Wi[:, kc, j * 128:(j + 1) * 128], rhs=hT[tb][:, kc, :],
                    start=(kc == 0), stop=(kc == KC - 1)),
                    deps=[wt[kc]] + list(hT_toks), sig=(kc == KC - 1))
            if pending:
                pending.pop()()
            k = state["st"] % NST
            state["st"] += 1
            dst = PJ[j, :, t * 512:(t + 1) * 512]
            if j < 2 * HP:
                n = state["qk"]; state["qk"] += 1
                qs, ts, pr, cs = n % 3, n % 2, n % 2, t % 2
                a = OP("act", lambda e, pa=pa, qs=qs: e.activation(out=qb[qs], in_=Bk[pa][:, :], func=AF.Copy),
                       deps=[mm, qb_free[qs]])
                d1 = OP("dve", lambda e, pa=pa, ts=ts, cs=cs: e.tensor_tensor(out=t1[ts], in0=Bk[pa][:, :], in1=CC[cs],
                                                                              op=ALU.mult),
                        deps=[mm, t1_free[ts], rope_c[cs]])

                def perm_mm(qs=qs, pr=pr, ts=ts, cs=cs, a=a, d1=d1, k=k, dst=dst, pa=pa):
                    pm = OP("pe", lambda e: e.matmul(Bk[4 + pr][:, :], lhsT=perm[:], rhs=qb[qs], start=True, stop=True),
                            deps=[a, pr_free[pr]])
                    qb_free[qs] = pm
                    d2 = OP("dve", lambda e: e.tensor_tensor(out=t2[ts], in0=Bk[4 + pr][:, :], in1=SSn[cs], op=ALU.mult),
                            deps=[pm, d1, rope_s[cs], t1_free[ts]])
                    pr_free[pr] = d2
                    ad = OP("pool", lambda e: e.tensor_tensor(out=st[k], in0=t1[ts], in1=t2[ts], op=ALU.add),
                            deps=[d1, d2, st_free[k]])
                    t1_free[ts] = ad
                    st_free[k] = DMA("sp", lambda e: e.dma_start(out=dst, in_=st[k]), f"st{k}", deps=[ad])
                    rope_use[cs] = ad
                pending.append(perm_mm)
                pa_wait[pa] = [a, d1]
            else:
                if j < 3 * HP:
                    ev = OP("act", lambda e, pa=pa, k=k: e.activation(out=st[k], in_=Bk[pa][:, :], func=AF.Copy),
                            deps=[mm, st_free[k]])
                elif j < 4 * HP or j >= 4 * HP + PC:
                    ev = OP("act", lambda e, pa=pa, k=k: e.activation(out=st[k], in_=Bk[pa][:, :], func=AF.Silu),
                            deps=[mm, st_free[k]])
                else:
                    ev = OP("dve", lambda e, pa=pa, k=k: e.tensor_copy(out=st[k], in_=Bk[pa][:, :]),
                            deps=[mm, st_free[k]])
                pa_wait[pa] = [ev]
                st_free[k] = DMA("sp", lambda e, k=k, dst=dst: e.dma_start(out=dst, in_=st[k]), f"st{k}", deps=[ev])

        pa_wait = [[] for _ in range(4)]
        rope_c = [None, None]
        rope_s = [None, None]
        rope_use = [None, None]

        htoks = norm_tile(0)
        hT_toks = transposes(0, htoks)
        for t in range(NTT):
            cs = t % 2
            rc = DMA("sp", lambda e, cs=cs, t=t: e.dma_start(out=CC[cs], in_=ropeC_d[:, t * 512:(t + 1) * 512]),
                     f"rc{cs}", deps=[rope_use[cs]])
            rs = DMA("sp", lambda e, cs=cs, t=t: e.dma_start(out=SSn[cs], in_=ropeS_d[:, t * 512:(t + 1) * 512]),
                     f"rs{cs}", deps=[rope_use[cs]])
            rope_c[cs] = rc
            rope_s[cs] = rs
            if t + 1 < NTT:
                nh = norm_tile(t + 1)
            pending = []
            nxt = None
            for j in range(NCH):
                pa = j % 4
                R.wait_only("pe", pa_wait[pa])
                pa_wait[pa] = []
                chunk(j, t, hT_toks, pending)
                if j == NCH // 2 and t + 1 < NTT:
                    nxt = transposes(t + 1, nh)
            if pending:
                pending.pop()()
            hT_free[t % 2] = last["pe"]
            if t + 1 < NTT:
                hT_toks = nxt
        end_phase()

    def phase2(l):
        bfa.reset(); fpa.reset()
        KT = [bfa.take(4096) for _ in range(2)]
        VT = [bfa.take(4096) for _ in range(2)]
        QT = [[bfa.take(2048) for _ in range(2)] for _ in range(2)]
        GA = [bfa.take(2048) for _ in range(2)]
        E = [bfa.take(512) for _ in range(3)]
        NP = 5
        P = [bfa.take(512) for _ in range(NP)]
        NV = 24
        VA = [bfa.take(256) for _ in range(NV)]
        YB = [bfa.take(2048) for _ in range(2)]
        MKV = [[bfa.take(512) for _ in range(3)] for _ in range(2)]
        ACC = [[fpa.take(2048) for _ in range(2)] for _ in range(2)]
        RR = fpa.take(2048)
        TT = fpa.take(2048)
        init = []
        for b in KT + VT + QT[0] + QT[1]:
            init.append(OP("pool", lambda e, b=b: e.memset(b, 0.0)))
        for v in VA:
            init.append(OP("pool", lambda e, v=v: e.memset(v, 1.0)))
        phase_sync(init)

        kv_free = [None, None]; q_free = [None, None]; ga_free = [None, None]
        e_free = [None] * 3; p_free = [None] * NP
        s_free = [None, None]; t_free = [None, None]
        o_free = [None, None]
        va_free = [None] * NV
        acc_free = [None, None]; yb_free = [None, None]; mk_free = [None, None]
        rr_free = None; tt_free = None
        vstate = {"n": 0}
        it = 0
        for hp in range(HP):
            for cseg in range(NSEG):
                sl = it % 2
                lo = SEG * cseg - 1024
                g0, g1 = max(lo, 0), min(lo + 4096, NT)
                kt = DMA("sp", lambda e, sl=sl, g0=g0, g1=g1, lo=lo, hp=hp: e.dma_start(
                    out=KT[sl][:, g0 - lo:g1 - lo], in_=PJ[HP + hp, :, g0:g1]), f"kt{sl}", deps=[kv_free[sl]])
                vt = DMA("sp", lambda e, sl=sl, g0=g0, g1=g1, lo=lo, hp=hp: e.dma_start(
                    out=VT[sl][:, g0 - lo:g1 - lo], in_=PJ[2 * HP + hp, :, g0:g1]), f"vt{sl}", deps=[kv_free[sl]])
                qt0 = DMA("sp", lambda e, sl=sl, cseg=cseg, hp=hp: e.dma_start(
                    out=QT[sl][0][0:64, :], in_=PJ[hp, 0:64, cseg * SEG:(cseg + 1) * SEG]), f"qa{sl}", deps=[q_free[sl]])
                qt = DMA("sp", lambda e, sl=sl, cseg=cseg, hp=hp: e.dma_start(
                    out=QT[sl][1][64:128, :], in_=PJ[hp, 64:128, cseg * SEG:(cseg + 1) * SEG]), f"qb{sl}", deps=[q_free[sl]])
                gat = DMA("sp", lambda e, sl=sl, cseg=cseg, hp=hp: e.dma_start(
                    out=GA[sl], in_=PJ[3 * HP + hp, :, cseg * SEG:(cseg + 1) * SEG]), f"ga{sl}", deps=[ga_free[sl]])
                cL = segfl[:, 2 * cseg:2 * cseg + 1]
                cR = segfl[:, 2 * cseg + 1:2 * cseg + 2]
                mL, mR, mLR = MKV[sl]
                av = lambda m: m.rearrange("p (h x) -> p h x", h=2)[:, :, 0:128]
                bv = lambda m: m.rearrange("p (h x) -> p h x", h=2)[:, :, 128:256]
                m1 = OP("pool", lambda e, mL=mL: e.tensor_copy(out=mL, in_=MK0[:]), deps=[mk_free[sl]])
                m2 = OP("pool", lambda e, mL=mL, cL=cL: e.tensor_scalar(out=av(mL), in0=av(MK0[:]), scalar1=cL, scalar2=None,
                                                                        op0=ALU.mult), deps=[m1])
                m3 = OP("pool", lambda e, mR=mR: e.tensor_copy(out=mR, in_=MK0[:]), deps=[mk_free[sl]])
                m4 = OP("pool", lambda e, mR=mR, cR=cR: e.tensor_scalar(out=bv(mR), in0=bv(MK0[:]), scalar1=cR, scalar2=None,
                                                                        op0=ALU.mult), deps=[m3])
                m5 = OP("pool", lambda e, mLR=mLR, mL=mL: e.tensor_copy(out=mLR, in_=mL), deps=[m2])
                m6 = OP("pool", lambda e, mLR=mLR, cR=cR: e.tensor_scalar(out=bv(mLR), in0=bv(MK0[:]), scalar1=cR,
                                                                          scalar2=None, op0=ALU.mult), deps=[m5])
                mtok = [m2, m4, m6]

                combos = []
                for d in PATTERNS:
                    nb = SEG // (128 * d)
                    if d == 1:
                        order = [(0, b) for b in range(16)]
                    elif d == 4:
                        order = [(r, b) for b in range(4) for r in range(4)]
                    else:
                        order = [(r, 0) for r in range(16)]
                    for (r, b) in order:
                        combos.append((d, r, b, nb))
                if 'combos' in DBG:
                    combos = [combos[ii] for ii in DBG['combos']]
                vcache = {}
                copy_toks = [None] * 4
                add_toks = []
                cp_all = []
                slot_key = {}
                acc = ACC[sl]
                NCB = len(combos)
                info = [dict() for _ in range(NCB)]

                def st_T(i):
                    d, r, b, nb = combos[i]
                    vts, newt = [], []
                    for tau in (b, b + 1):
                        key = (d, r, tau)
                        if key not in vcache:
                            slot = vstate["n"] % NV
                            vstate["n"] += 1
                            vcache[key] = [slot, None]
                            slot_key[slot] = key
                            newt.append((key, slot))
                        assert slot_key[vcache[key][0]] == key
                        vts.append(key)
                    tbk = i % 2
                    evl = []
                    tfl = t_free[tbk] if isinstance(t_free[tbk], list) else [t_free[tbk]]
                    tp = None
                    for n, (key, slot) in enumerate(newt):
                        start = 1024 + d * (128 * key[2] - 64) + r
                        tp = OP("pe", lambda e, tbk=tbk, n=n, start=start, d=d, sl=sl: e.transpose(
                            out=Tk[tbk][:, n * 128:(n + 1) * 128],
                            in_=VT[sl][:, start:start + 127 * d + 1:d], identity=ident[:]),
                            deps=[vt] + tfl, sig=(n == len(newt) - 1))
                    for n, (key, slot) in enumerate(newt):
                        if n == 5:
                            ev = OP("act", lambda e, tbk=tbk, n=n, slot=slot: e.activation(
                                out=VA[slot].rearrange("p (a x) -> p a x", a=4)[:, 0:4:3, :],
                                in_=Tk[tbk][:, n * 128:(n + 1) * 128].rearrange("p (a x) -> p a x", a=2), func=AF.Copy),
                                deps=[tp, va_free[slot]])
                        else:
                            ev = OP("dve", lambda e, tbk=tbk, n=n, slot=slot: e.tensor_copy(
                                out=VA[slot].rearrange("p (a x) -> p a x", a=4)[:, 0:4:3, :],
                                in_=Tk[tbk][:, n * 128:(n + 1) * 128].rearrange("p (a x) -> p a x", a=2)),
                                deps=[tp, va_free[slot]])
                        vcache[key][1] = ev
                        evl.append(ev)
                    if evl:
                        t_free[tbk] = evl
                    info[i]["vts"] = vts

                def st_S(i):
                    d, r, b, nb = combos[i]
                    sb_ = i % 2
                    qs0 = d * 128 * b + r
                    smm = None
                    for hh in range(2):
                        for ab in range(2):
                            ks0 = 1024 + d * (128 * (b + ab) - 64) + r
                            smm = OP("pe", lambda e, sb_=sb_, hh=hh, ab=ab, ks0=ks0, qs0=qs0, d=d, sl=sl: e.matmul(
                                Bk[sb_][:, (2 * hh + ab) * 128:(2 * hh + ab + 1) * 128],
                                lhsT=KT[sl][:, ks0:ks0 + 127 * d + 1:d],
                                rhs=QT[sl][hh][:, qs0:qs0 + 127 * d + 1:d], start=True, stop=True),
                                deps=[kt, qt0, qt, s_free[sb_]], sig=(hh == 1 and ab == 1))
                    es_ = i % 3
                    ex = OP("act", lambda e, es_=es_, sb_=sb_: e.activation(out=E[es_], in_=Bk[sb_][:, :], func=AF.Exp,
                                                                           scale=0.125), deps=[smm, e_free[es_]])
                    s_free[sb_] = ex
                    info[i]["ex"] = ex

                def st_M(i):
                    d, r, b, nb = combos[i]
                    es_, ps_ = i % 3, i % NP
                    first, lastb = (b == 0), (b == nb - 1)
                    if first and lastb:
                        mk, mt = MKV[sl][2], mtok[2]
                    elif first:
                        mk, mt = MKV[sl][0], mtok[0]
                    elif lastb:
                        mk, mt = MKV[sl][1], mtok[1]
                    else:
                        mk, mt = MK0[:], None
                    pm = OP("dve", lambda e, es_=es_, ps_=ps_, mk=mk: e.tensor_tensor(out=P[ps_], in0=E[es_], in1=mk,
                                                                                     op=ALU.mult),
                            deps=[info[i]["ex"], p_free[ps_], mt])
                    e_free[es_] = pm
                    info[i]["pm"] = pm

                def st_V(i):
                    d, r, b, nb = combos[i]
                    gidx, k = i // 4, i % 4
                    pat, gi = i // 16, (i // 4) % 4
                    ps_ = i % NP
                    pm, vts = info[i]["pm"], info[i]["vts"]
                    oset = gidx % 2
                    pv = None
                    for hh in range(2):
                        for ab in range(2):
                            slot, evt = vcache[vts[ab]]
                            assert slot_key[slot] == vts[ab]
                            pv = OP("pe", lambda e, oset=oset, hh=hh, ab=ab, k=k, slot=slot, ps_=ps_: e.matmul(
                                Bk[2 + 2 * oset + hh][:, k * 128:(k + 1) * 128],
                                lhsT=VA[slot][:, 128 * hh:128 * hh + 128],
                                rhs=P[ps_][:, (2 * hh + ab) * 128:(2 * hh + ab + 1) * 128],
                                start=(ab == 0), stop=(ab == 1)),
                                deps=[pm, evt, o_free[oset] if k == 0 else None], sig=(hh == 1 and ab == 1))
                    p_free[ps_] = pv
                    for key in vts:
                        va_free[vcache[key][0]] = pv
                    if k == 3:
                        pend_comb.append((pv, pat, gi, oset))
                    return pv

                def st_C(pv, pat, gi, oset):
                    if True:
                        toks = []
                        for hh in range(2):
                            ob = Bk[2 + 2 * oset + hh]
                            if pat == 0:
                                tk = OP("act", lambda e, hh=hh, gi=gi, ob=ob, acc=acc: e.activation(
                                    out=acc[hh][:, 512 * gi:512 * gi + 512], in_=ob[:, :], func=AF.Copy),
                                    deps=[pv, acc_free[sl]])
                            else:
                                if pat == 1:
                                    view = acc[hh][:, 512 * gi:512 * gi + 512].rearrange("p (t r) -> p r t", r=4)
                                    dd = [copy_toks[gi]]
                                else:
                                    view = acc[hh][:, :].rearrange("p (t r) -> p r t", r=16)[:, 4 * gi:4 * gi + 4, :]
                                    dd = list(copy_toks) + add_toks[-1:]
                                tk = OP("dve", lambda e, view=view, ob=ob: e.tensor_tensor(
                                    out=view, in0=ob[:, :].rearrange("p (r t) -> p r t", r=4), in1=view, op=ALU.add),
                                    deps=[pv] + dd)
                            toks.append(tk)
                        if pat == 0:
                            copy_toks[gi] = toks[-1]
                            cp_all.extend(toks)
                        else:
                            add_toks.extend(toks)
                        o_free[oset] = toks[-1]

                LAG = 3
                lastpv = None
                pend_comb = []
                for n in range(NCB + LAG):
                    if n < NCB:
                        st_T(n)
                        st_S(n)
                    if 1 <= n <= NCB:
                        st_M(n - 1)
                    while pend_comb:
                        st_C(*pend_comb.pop(0))
                    if n - LAG >= 0:
                        lastpv = st_V(n - LAG)
                while pend_comb:
                    st_C(*pend_comb.pop(0))
                kv_free[sl] = lastpv
                q_free[sl] = lastpv
                mk_free[sl] = lastpv
                fin = add_toks[-4:] + cp_all
                if DBG.get('nonorm'):
                    it += 1
                    continue
                l0 = OP("act", lambda e, acc=acc: e.activation(out=RR[0:64, :], in_=acc[0][64:128, :], func=AF.Ln),
                        deps=fin + [rr_free])
                l1 = OP("act", lambda e, acc=acc: e.activation(out=RR[64:128, :], in_=acc[1][0:64, :], func=AF.Ln),
                        deps=fin + [rr_free])
                r0 = OP("act", lambda e: e.activation(out=RR[0:64, :], in_=RR[0:64, :], func=AF.Exp, scale=-1.0), deps=[l0])
                r1 = OP("act", lambda e: e.activation(out=RR[64:128, :], in_=RR[64:128, :], func=AF.Exp, scale=-1.0), deps=[l1])
                n0 = OP("pool", lambda e, acc=acc: e.tensor_tensor(out=TT[0:64, :], in0=acc[0][0:64, :], in1=RR[0:64, :],
                                                                   op=ALU.mult), deps=[r0, tt_free])
                n1 = OP("pool", lambda e, acc=acc: e.tensor_tensor(out=TT[64:128, :], in0=acc[1][64:128, :],
                                                                   in1=RR[64:128, :], op=ALU.mult), deps=[r1, tt_free])
                yy = OP("pool", lambda e, sl=sl: e.tensor_tensor(out=YB[sl], in0=TT, in1=GA[sl], op=ALU.mult),
                        deps=[n0, n1, gat, yb_free[sl]])
                rr_free = n1
                tt_free = yy
                acc_free[sl] = n1
                ga_free[sl] = yy
                yb_free[sl] = DMA("sp", lambda e, sl=sl, hp=hp, cseg=cseg: e.dma_start(
                    out=YT[hp, :, cseg * SEG:(cseg + 1) * SEG], in_=YB[sl]), f"yb{sl}", deps=[yy])
                it += 1
        end_phase()

    def phase2b(l):
        bfa.reset(); fpa.reset()
        EXT = SEG + 16
        UE = [[bfa.take(EXT) for _ in range(PCG)] for _ in range(2)]
        GP = [[bfa.take(SEG) for _ in range(PCG)] for _ in range(2)]
        PLD = [bfa.take(SEG) for _ in range(PCG)]
        YP = [bfa.take(SEG) for _ in range(2)]
        Wp = bfa.take(PG * PCG * G).rearrange("p (g c n) -> p g c n", g=PG, c=PCG)
        SAB = [(fpa.take(EXT), fpa.take(EXT)) for _ in range(min(PCG, 2))]
        TS = fpa.take(16)
        wts = []
        for g in range(PG):
            for ci in range(PCG):
                wts.append(DMA("pool", lambda e, g=g, ci=ci: e.dma_start(out=Wp[:, g, ci, :],
                                                                         in_=w_pool[l, g, ci * 128:(ci + 1) * 128, :]),
                               f"wp{g}_{ci}"))
        pt = DMA("sp", lambda e: e.dma_start(out=psc[:], in_=psc_d[l]), "psc")
        init = []
        for s_ in range(2):
            for ci in range(PCG):
                init.append(OP("pool", lambda e, u=UE[s_][ci]: e.memset(u, 0.0)))
        phase_sync(init + wts + [pt])
        ue_free = [None, None]; gp_free = [None, None]; pld_free = [None] * PCG
        yp_free = [None, None]; po_free = [None] * 6
        s_free = [None, None]
        it = 0
        pon = 0
        ypn = 0
        for g in range(PG):
            w = POOL_W[g]
            for cseg in range(NSEG):
                sl = it % 2
                lo = SEG * cseg - 8
                g0, g1 = max(lo, 0), min(lo + EXT, NT)
                uts, gts = [], []
                for ci in range(PCG):
                    pc = g * PCG + ci
                    uts.append(DMA("sp", lambda e, sl=sl, ci=ci, pc=pc, g0=g0, g1=g1, lo=lo: e.dma_start(
                        out=UE[sl][ci][:, g0 - lo:g1 - lo], in_=PJ[4 * HP + pc, :, g0:g1]), f"ue{sl}_{ci}",
                        deps=[ue_free[sl]]))
                    gts.append(DMA("sp", lambda e, sl=sl, ci=ci, pc=pc, cseg=cseg: e.dma_start(
                        out=GP[sl][ci], in_=PJ[4 * HP + PC + pc, :, cseg * SEG:(cseg + 1) * SEG]), f"gp{sl}_{ci}",
                        deps=[gp_free[sl]]))
                fL = pfl[:, 2 * cseg:2 * cseg + 1]
                fR = pfl[:, 2 * cseg + 1:2 * cseg + 2]
                plds = []
                for ci in range(PCG):
                    u = UE[sl][ci]
                    h1 = OP("pool", lambda e, u=u, fL=fL: e.tensor_scalar(out=u[:, 0:8], in0=u[:, 0:8], scalar1=fL,
                                                                          scalar2=None, op0=ALU.mult), deps=[uts[ci]])
                    h2 = OP("pool", lambda e, u=u, fR=fR: e.tensor_scalar(out=u[:, EXT - 8:EXT], in0=u[:, EXT - 8:EXT],
                                                                          scalar1=fR, scalar2=None, op0=ALU.mult),
                            deps=[uts[ci]])
                    SA, SBb = SAB[ci % 2]
                    seng = "pool" if ci % 2 == 0 else "dve"
                    a = OP(seng, lambda e, u=u, SA=SA: e.tensor_tensor(out=SA[:, 1:EXT], in0=u[:, 0:EXT - 1], in1=u[:, 1:EXT],
                                                                       op=ALU.add), deps=[h1, h2, s_free[ci % 2]])
                    cur, oth = SA, SBb
                    lo_v, hi_v, sh = 1, EXT, 1
                    ww = 2
                    while ww < w:
                        nl, nh = lo_v + sh, hi_v - sh
                        a = OP(seng, lambda e, cur=cur, oth=oth, nl=nl, nh=nh, sh=sh: e.tensor_tensor(
                            out=oth[:, nl:nh], in0=cur[:, nl - sh:nh - sh], in1=cur[:, nl + sh:nh + sh], op=ALU.add),
                            deps=[a])
                        cur, oth = oth, cur
                        lo_v, hi_v, sh, ww = nl, nh, sh * 2, ww * 2
                    assert lo_v <= 8 and hi_v >= EXT - 8
                    pl = OP("dve", lambda e, cur=cur, u=u, ci=ci, w=w: e.scalar_tensor_tensor(
                        out=PLD[ci], in0=cur[:, 8:8 + SEG], scalar=1.0 / w, in1=u[:, 8:8 + SEG], op0=ALU.mult,
                        op1=ALU.subtract), deps=[a, pld_free[ci]])
                    pb = (g * NSEG + cseg) * 16
                    s1 = OP("dve", lambda e, cur=cur, pb=pb: e.tensor_tensor(out=TS[:, 0:8], in0=cur[:, 8:16],
                                                                             in1=prc[:, pb:pb + 8], op=ALU.mult), deps=[pl])
                    s2 = OP("dve", lambda e, u=u, ci=ci: e.tensor_tensor(out=PLD[ci][:, 0:8], in0=TS[:, 0:8], in1=u[:, 8:16],
                                                                         op=ALU.subtract), deps=[s1])
                    s3 = OP("dve", lambda e, cur=cur, pb=pb: e.tensor_tensor(out=TS[:, 8:16], in0=cur[:, SEG:SEG + 8],
                                                                             in1=prc[:, pb + 8:pb + 16], op=ALU.mult),
                            deps=[s2])
                    s4 = OP("dve", lambda e, u=u, ci=ci: e.tensor_tensor(out=PLD[ci][:, SEG - 8:SEG], in0=TS[:, 8:16],
                                                                         in1=u[:, SEG:SEG + 8], op=ALU.subtract), deps=[s3])
                    s_free[ci % 2] = s4
                    plds.append(s4)
                ue_free[sl] = plds[-1]
                for do in range(PCG):
                    pc_out = g * PCG + do
                    ys = ypn % 2
                    ypn += 1
                    evs = []
                    for tt in range(4):
                        pb_ = pon % 6
                        pon += 1
                        mm = None
                        for ci in range(PCG):
                            mm = OP("pe", lambda e, pb_=pb_, g=g, ci=ci, do=do, tt=tt: e.matmul(
                                Bk[pb_][:, :], lhsT=Wp[:, g, ci, do * 128:(do + 1) * 128],
                                rhs=PLD[ci][:, tt * 512:(tt + 1) * 512], start=(ci == 0), stop=(ci == PCG - 1)),
                                deps=plds + [po_free[pb_]], sig=(ci == PCG - 1))
                        ev = OP("dve", lambda e, pb_=pb_, ys=ys, tt=tt, pc_out=pc_out, sl=sl, do=do: e.scalar_tensor_tensor(
                            out=YP[ys][:, tt * 512:(tt + 1) * 512], in0=Bk[pb_][:, :], scalar=psc[:, pc_out:pc_out + 1],
                            in1=GP[sl][do][:, tt * 512:(tt + 1) * 512], op0=ALU.mult, op1=ALU.mult),
                            deps=[mm, gts[do], yp_free[ys]])
                        po_free[pb_] = ev
                        evs.append(ev)
                    yp_free[ys] = DMA("sp", lambda e, ys=ys, pc_out=pc_out, cseg=cseg: e.dma_start(
                        out=YT[HP + pc_out, :, cseg * SEG:(cseg + 1) * SEG], in_=YP[ys]), f"yp{ys}", deps=[evs[-1]])
                    lastmm = mm
                for ci in range(PCG):
                    pld_free[ci] = lastmm
                gp_free[sl] = evs[-1]
                it += 1
        end_phase()

    def phase3(l):
        bfa.reset(); fpa.reset()
        Wo = bfa.take(MC * DM).rearrange("p (m n) -> p m n", m=MC)
        yT = [bfa.take(MC * 512).rearrange("p (m n) -> p m n", m=MC) for _ in range(2)]
        junk = bfa.take(DM)
        X3 = [fpa.take(DM) for _ in range(3)]
        XN = [fpa.take(DM) for _ in range(3)]
        x_src = x_in if l == 0 else XS
        final = (l == L - 1)
        NH = DM // 512
        wts = [DMA("pool", lambda e, m=m: e.dma_start(out=Wo[:, m, :], in_=w_out[l, m * 128:(m + 1) * 128, :]), f"wo{m}")
               for m in range(MC)]
        gt = DMA("sp", lambda e: e.dma_start(out=gB[:], in_=gB_d[L]), "g") if final else None
        phase_sync(wts)
        yt_free = [None, None]; x3_free = [None] * 3; xn_free = [None] * 3; po_free = [None] * 6
        pon = 0
        jstate = {}
        for t in range(NTT):
            tb = t % 2
            ytk = DMA("sp", lambda e, tb=tb, t=t: e.dma_start(
                out=yT[tb], in_=YT[:, :, t * 512:(t + 1) * 512].rearrange("m p n -> p m n")), f"yt{tb}",
                deps=[yt_free[tb]])
            for s in range(4):
                i = 4 * t + s
                xs = i % 3
                r0 = t * 512 + s * 128
                xt = DMA("sp", lambda e, xs=xs, r0=r0: e.dma_start(out=X3[xs], in_=x_src[r0:r0 + 128, :]), f"x3{xs}",
                         deps=[x3_free[xs]])
                adds = []
                for hf in range(NH):
                    pb_ = pon % 6
                    pon += 1
                    mm = None
                    for m in range(MC):
                        mm = OP("pe", lambda e, pb_=pb_, m=m, s=s, hf=hf, tb=tb: e.matmul(
                            Bk[pb_][:, :], lhsT=yT[tb][:, m, s * 128:(s + 1) * 128], rhs=Wo[:, m, hf * 512:(hf + 1) * 512],
                            start=(m == 0), stop=(m == MC - 1)), deps=[ytk, po_free[pb_]], sig=(m == MC - 1))
                    ad = OP("dve", lambda e, pb_=pb_, xs=xs, hf=hf: e.tensor_tensor(
                        out=XN[xs][:, hf * 512:(hf + 1) * 512], in0=Bk[pb_][:, :], in1=X3[xs][:, hf * 512:(hf + 1) * 512],
                        op=ALU.add), deps=[mm, xt, xn_free[xs]])
                    po_free[pb_] = ad
                    adds.append(ad)
                x3_free[xs] = adds[-1]
                if not final:
                    xn_free[xs] = DMA("sp", lambda e, xs=xs, r0=r0: e.dma_start(out=XS[r0:r0 + 128, :], in_=XN[xs]),
                                      f"xn{xs}", deps=adds)
                else:
                    col = i % 16
                    a = OP("act", lambda e, xs=xs, col=col: e.activation(out=junk, in_=XN[xs], func=AF.Square,
                                                                         accum_out=stat[:, col:col + 1]),
                           deps=adds + [jstate.get("junk")])
                    jstate["junk"] = a
                    b = OP("act", lambda e, col=col: e.activation(out=stat[:, 16 + col:17 + col], in_=stat[:, col:col + 1],
                                                                  func=AF.Sqrt, scale=1.0 / DM, bias=epsc[:, 0:1]), deps=[a])
                    b2 = OP("dve", lambda e, col=col: e.reciprocal(out=stat[:, 32 + col:33 + col],
                                                                   in_=stat[:, 16 + col:17 + col]), deps=[b])
                    fo = OP("dve", lambda e, xs=xs, col=col: e.scalar_tensor_tensor(
                        out=XN[xs], in0=XN[xs], scalar=stat[:, 32 + col:33 + col], in1=gB[:], op0=ALU.mult, op1=ALU.mult),
                        deps=[b2, gt])
                    xn_free[xs] = DMA("sp", lambda e, xs=xs, r0=r0: e.dma_start(out=y_out[r0:r0 + 128, :], in_=XN[xs]),
                                      f"xn{xs}", deps=[fo])
            yt_free[tb] = last["pe"]
        end_phase()

    for l in range(L):
        if 1 in phases:
            phase1(l)
        if 2 in phases:
            phase2(l)
        if 3 in phases:
            phase2b(l)
        if 4 in phases:
            phase3(l)

    keys = sorted(set(R.cnt.keys()))
    sems = {k: es.enter_context(nc.semaphore(f"s_{k}")) for k in keys}
    block = es.enter_context(nc.Block())

    def replay(engname):
        def f(eng):
            for waits, fn, semk, inc in R.ops[engname]:
                for (k, v) in waits:
                    eng.wait_ge(sems[k], v)
                if fn is None:
                    continue
                ins = fn(eng)
                if semk is not None:
                    ins.then_inc(sems[semk], inc)
        return f

    block.tensor(replay("pe"))
    block.scalar(replay("act"))
    block.vector(replay("dve"))
    block.gpsimd(replay("pool"))
    block.sync(replay("sp"))
    es.close()
    return nc


def _consts():
    ident = np.eye(128, dtype=np.float32)
    m = np.arange(128)
    sw = np.where((m % 64) < 32, m + 32, m - 32)
    perm = np.zeros((128, 128), np.float32)
    perm[sw, m] = 1.0
    i = np.arange(128)[:, None]
    j = np.arange(128)[None, :]
    ma = (i >= j).astype(np.float32)
    mb = (i <= j).astype(np.float32)
    mask0 = np.concatenate([ma, mb, ma, mb], axis=1)
    bf = ml_dtypes.bfloat16
    return ident.astype(bf), perm.astype(bf), mask0.astype(bf)


def _rope_tables(pos):
    hd = 64
    inv_freq = (np.float32(10000.0) ** (-(np.arange(0, hd, 2, dtype=np.float32)) / np.float32(hd))).astype(np.float32)
    ang = pos.astype(np.float32)[None, :] * inv_freq[:, None]
    cos = np.cos(ang).astype(np.float32)
    sin = np.sin(ang).astype(np.float32)
    m = np.arange(128)
    f = m % 32
    sgn = np.where((m % 64) < 32, -1.0, 1.0).astype(np.float32)
    return np.ascontiguousarray(cos[f]), np.ascontiguousarray(sin[f] * sgn[:, None])


def make_core_inputs(cfg, segs, xs, shared):
    NSEG, PG = cfg.NSEG, cfg.PG
    x = np.concatenate([xs[k][b, s:s + SEG] for (k, b, s, _, _) in segs], axis=0)
    pos = np.concatenate([np.arange(s, s + SEG) for (_, _, s, _, _) in segs])
    rc, rs = _rope_tables(pos)
    segfl = np.ones((128, 2 * NSEG), np.float32)
    pfl = np.ones((128, 2 * NSEG), np.float32)
    prc = np.zeros((128, PG, NSEG, 16), np.float32)
    for ci, (_, _, _, cl, cr) in enumerate(segs):
        segfl[:64, 2 * ci] = cl
        segfl[64:, 2 * ci + 1] = cr
        pfl[:, 2 * ci] = cl
        pfl[:, 2 * ci + 1] = cr
        for g, w in enumerate(POOL_W):
            jj = np.arange(8)
            left = np.where(jj < w // 2, jj + w // 2, w) if not cl else np.full(8, w)
            right = np.where(jj > 8 - w // 2, 8 - jj + w // 2, w) if not cr else np.full(8, w)
            prc[:, g, ci, 0:8] = 1.0 / left
            prc[:, g, ci, 8:16] = 1.0 / right
    d = dict(shared)
    d.update({"x": np.ascontiguousarray(x, dtype=np.float32), "ropeC": rc, "ropeS": rs, "segfl": segfl, "pfl": pfl,
              "prc": np.ascontiguousarray(prc.reshape(128, -1))})
    return d


def make_shared(cfg, norm_g, w_in, w_pool, pool_scale, w_out, final_norm_g):
    L, DM, PC = cfg.L, cfg.DM, cfg.PC
    ident, perm, mask0 = _consts()
    g_all = np.concatenate([norm_g, final_norm_g[None, :]], axis=0)
    gB = np.ascontiguousarray(np.broadcast_to(g_all[:, None, :], (L + 1, 128, DM)), dtype=np.float32)
    psc = np.ascontiguousarray(pool_scale.reshape(L, PC, 128).transpose(0, 2, 1), dtype=np.float32)
    return {"w_in": np.ascontiguousarray(w_in, dtype=np.float32), "w_out": np.ascontiguousarray(w_out, dtype=np.float32),
            "w_pool": np.ascontiguousarray(w_pool, dtype=np.float32), "gB": gB, "psc": psc,
            "ident": ident, "perm": perm, "mask0": mask0}


_NC_CACHE = {}


def run_cfg(cfg, core_segs, xs, shared, n_cores):
    key = (cfg.DM, cfg.HP, cfg.PCG, cfg.L, cfg.NSEG)
    if key not in _NC_CACHE:
        _NC_CACHE[key] = build_program(cfg)
    nc = _NC_CACHE[key]
    in_maps = [make_core_inputs(cfg, segs, xs, shared) for segs in core_segs]
    res = run_bass_kernel_spmd(nc, in_maps, core_ids=list(range(n_cores)))
    return [r["y"] for r in res.results]


def kernel(x_prompt, x_sample, norm_g, w_in, w_pool, pool_scale, w_out, final_norm_g):
    cfg = Cfg(NSEG=7)
    xs = {"p": np.asarray(x_prompt, np.float32), "s": np.asarray(x_sample, np.float32)}
    shared = make_shared(cfg, np.asarray(norm_g), np.asarray(w_in), np.asarray(w_pool), np.asarray(pool_scale),
                         np.asarray(w_out), np.asarray(final_norm_g))
    core_segs, keep = [], []
    for b in range(2):
        for h in range(2):
            base = 0 if h == 0 else 4096
            segs = [("p", b, base + i * SEG, int(i > 0), int(i < 5)) for i in range(6)]
            segs.append(("s", 2 * b + h, 0, 0, 0))
            core_segs.append(segs)
            keep.append([0, 1, 2, 3] if h == 0 else [2, 3, 4, 5])
    for cidx in range(4):
        core_segs.append([("s", 4 + cidx * 7 + i, 0, 0, 0) for i in range(7)])
    outs = run_cfg(cfg, core_segs, xs, shared, 8)
    DM = cfg.DM
    y_p = np.empty((2, 16384, DM), np.float32)
    y_s = np.empty((32, SEG, DM), np.float32)
    for ci, segs in enumerate(core_segs):
        o = np.asarray(outs[ci]).reshape(cfg.NSEG, SEG, DM)
        for si, (k, b, start, _, _) in enumerate(segs):
            if k == "s":
                y_s[b] = o[si]
            elif si in keep[ci]:
                y_p[b, start:start + SEG] = o[si]
    return (y_p, y_s)
```

```python
import numpy as np
import ml_dtypes
from contextlib import ExitStack
import concourse.bass as bass
import concourse.mybir as mybir
from concourse.bass_utils import run_bass_kernel_spmd

F32 = mybir.dt.float32
BF16 = mybir.dt.bfloat16
AF = mybir.ActivationFunctionType
ALU = mybir.AluOpType
SEG = 2048
PATTERNS = (1, 4, 16)
POOL_W = (2, 4, 8, 16)
RMS_EPS = 1e-6


DBG = {}


class Cfg:
    def __init__(self, DM=1024, HP=8, PCG=2, L=4, NSEG=8):
        self.DM, self.HP, self.PCG, self.L, self.NSEG = DM, HP, PCG, L, NSEG
        self.PG = 4
        self.KC = DM // 128
        self.PC = self.PG * PCG
        self.DA = 128 * HP
        self.DP = 128 * self.PC
        self.DIN = 4 * self.DA + 2 * self.DP
        self.DMIX = self.DA + self.DP
        self.MC = HP + self.PC
        self.NCH = 4 * HP + 2 * self.PC
        self.NT = NSEG * SEG
        self.NTT = self.NT // 512
        self.G = 128 * PCG


class Rec:
    ENG = ("pe", "act", "dve", "pool", "sp")

    def __init__(self):
        self.ops = {e: [] for e in self.ENG}
        self.cnt = {}
        self.seen = {e: {} for e in self.ENG}

    def _waits(self, eng, deps):
        waits = []
        for d in deps:
            if d is None:
                continue
            key, val = d
            if self.seen[eng].get(key, 0) < val:
                self.seen[eng][key] = val
                waits.append((key, val))
        return waits

    def op(self, eng, fn, deps=(), sig=True):
        waits = self._waits(eng, deps)
        tok = None
        if sig:
            self.cnt[eng] = self.cnt.get(eng, 0) + 1
            tok = (eng, self.cnt[eng])
        self.ops[eng].append((waits, fn, eng if sig else None, 1))
        return tok

    def dma(self, q, fn, sem, deps=()):
        waits = self._waits(q, deps)
        self.cnt[sem] = self.cnt.get(sem, 0) + 16
        tok = (sem, self.cnt[sem])
        self.ops[q].append((waits, fn, sem, 16))
        return tok

    def wait_only(self, eng, deps):
        waits = self._waits(eng, deps)
        if waits:
            self.ops[eng].append((waits, None, None, 0))


def build_program(cfg, phases=(1, 2, 3, 4)):
    c = cfg
    DM, HP, PCG, L, NSEG, KC, PC, PG = c.DM, c.HP, c.PCG, c.L, c.NSEG, c.KC, c.PC, c.PG
    DIN, MC, NCH, NT, NTT, G = c.DIN, c.MC, c.NCH, c.NT, c.NTT, c.G
    nc = bass.Bass("TRN2", target_bir_lowering=False)
    dt = nc.dram_tensor
    x_in = dt("x", [NT, DM], F32, kind="ExternalInput").ap()
    y_out = dt("y", [NT, DM], F32, kind="ExternalOutput").ap()
    w_in = dt("w_in", [L, DM, DIN], F32, kind="ExternalInput").ap()
    w_out = dt("w_out", [L, c.DMIX, DM], F32, kind="ExternalInput").ap()
    w_pool = dt("w_pool", [L, PG, G, G], F32, kind="ExternalInput").ap()
    gB_d = dt("gB", [L + 1, 128, DM], F32, kind="ExternalInput").ap()
    psc_d = dt("psc", [L, 128, PC], F32, kind="ExternalInput").ap()
    ropeC_d = dt("ropeC", [128, NT], F32, kind="ExternalInput").ap()
    ropeS_d = dt("ropeS", [128, NT], F32, kind="ExternalInput").ap()
    segfl_d = dt("segfl", [128, 2 * NSEG], F32, kind="ExternalInput").ap()
    pfl_d = dt("pfl", [128, 2 * NSEG], F32, kind="ExternalInput").ap()
    prc_d = dt("prc", [128, PG * NSEG * 16], F32, kind="ExternalInput").ap()
    ident_d = dt("ident", [128, 128], BF16, kind="ExternalInput").ap()
    perm_d = dt("perm", [128, 128], BF16, kind="ExternalInput").ap()
    mask_d = dt("mask0", [128, 512], BF16, kind="ExternalInput").ap()
    PJ = dt("PJ", [NCH, 128, NT], BF16, kind="Internal").ap()
    YT = dt("YT", [MC, 128, NT], BF16, kind="Internal").ap()
    XS = dt("XS", [NT, DM], F32, kind="Internal").ap()

    R = Rec()
    es = ExitStack()
    sb = lambda n, s, d: es.enter_context(nc.sbuf_tensor(n, s, d))
    NBF = max(KC * DIN + 8 * DM + 2 * KC * 512 + 9 * 512 + DM, 46080, MC * DM + 2 * MC * 512 + DM)
    NFP = 12288
    BF = sb("BF", [128, NBF], BF16)
    FP = sb("FP", [128, NFP], F32)
    ident = sb("ident_sb", [128, 128], BF16)
    perm = sb("perm_sb", [128, 128], BF16)
    MK0 = sb("MK0", [128, 512], BF16)
    segfl = sb("segfl_sb", [128, 2 * NSEG], F32)
    pfl = sb("pfl_sb", [128, 2 * NSEG], F32)
    prc = sb("prc_sb", [128, PG * NSEG * 16], F32)
    gB = sb("gBt", [128, DM], F32)
    psc = sb("psct", [128, PC], F32)
    stat = sb("stat", [128, 64], F32)
    epsc = sb("epsc", [128, 1], F32)
    Bk = [es.enter_context(nc.psum_tensor(f"B{i}", [128, 512], F32)) for i in range(6)]
    Tk = [es.enter_context(nc.psum_tensor(f"T{i}", [128, 1024], BF16)) for i in range(2)]

    class Arena:
        def __init__(self, t):
            self.t, self.off = t, 0

        def reset(self):
            self.off = 0

        def take(self, n):
            a = self.t[:, self.off:self.off + n]
            self.off += n
            assert self.off <= self.t.shape[1], (self.off, self.t.shape)
            return a

    bfa, fpa = Arena(BF), Arena(FP)
    barrier = []

    def phase_sync(tokens):
        for e in Rec.ENG:
            R.wait_only(e, tokens)

    last = {}

    def OP(eng, fn, deps=(), sig=True):
        t = R.op(eng, fn, deps, sig)
        if t is not None:
            last[eng] = t
        return t

    def DMA(q, fn, sem, deps=()):
        t = R.dma(q, fn, sem, deps)
        last[sem] = t
        return t

    def end_phase():
        phase_sync([v for v in last.values()])

    ctoks = []
    for i, (dst, src) in enumerate([(ident, ident_d), (perm, perm_d), (MK0, mask_d), (segfl, segfl_d),
                                    (pfl, pfl_d), (prc, prc_d)]):
        ctoks.append(DMA("sp", lambda e, dst=dst, src=src: e.dma_start(out=dst[:], in_=src[:]), f"c{i}"))
    ctoks.append(OP("pool", lambda e: e.memset(epsc[:], RMS_EPS)))
    phase_sync(ctoks)

    def phase1(l):
        bfa.reset(); fpa.reset()
        Wi = bfa.take(KC * DIN).rearrange("p (k n) -> p k n", k=KC)
        hb = [bfa.take(4 * DM).rearrange("p (s d) -> p s d", s=4) for _ in range(2)]
        hT = [bfa.take(KC * 512).rearrange("p (k n) -> p k n", k=KC) for _ in range(2)]
        qb = [bfa.take(512) for _ in range(3)]
        NST = 6
        st = [bfa.take(512) for _ in range(NST)]
        junk = bfa.take(DM)
        XT = [fpa.take(DM) for _ in range(4)]
        CC = [fpa.take(512) for _ in range(2)]
        SSn = [fpa.take(512) for _ in range(2)]
        t1 = [fpa.take(512) for _ in range(2)]
        t2 = [fpa.take(512) for _ in range(2)]
        x_src = x_in if l == 0 else XS

        wt = [DMA("pool", lambda e, kc=kc: e.dma_start(out=Wi[:, kc, :], in_=w_in[l, kc * 128:(kc + 1) * 128, :]),
                  f"w{kc}") for kc in range(KC)]
        gt = DMA("sp", lambda e: e.dma_start(out=gB[:], in_=gB_d[l]), "g")

        xt_free = [None] * 4
        hb_free = [None] * 2
        hT_free = [None] * 2
        tb_free = [None] * 2
        pa_free = [None] * 4
        pr_free = [None] * 2
        qb_free = [None] * 3
        t1_free = [None] * 2
        st_free = [None] * NST
        cs_free = [None] * 2
        state = {"st": 0, "qk": 0}

        def norm_tile(t):
            tb = t % 2
            toks = []
            xts, sqs = [], []
            c0 = (4 * t) % 16
            for s in range(4):
                i = 4 * t + s
                sl = i % 4
                r0 = t * 512 + s * 128
                xt = DMA("sp", lambda e, sl=sl, r0=r0: e.dma_start(out=XT[sl], in_=x_src[r0:r0 + 128, :]),
                         f"x{sl}", deps=[xt_free[sl]])
                col = c0 + s
                a = OP("act", lambda e, sl=sl, col=col: e.activation(out=junk, in_=XT[sl], func=AF.Square,
                                                                     accum_out=stat[:, col:col + 1]), deps=[xt, state.get("junk")])
                state["junk"] = a
                xts.append(xt); sqs.append(a)
            b = OP("act", lambda e, c0=c0: e.activation(out=stat[:, 16 + c0:20 + c0], in_=stat[:, c0:c0 + 4], func=AF.Sqrt,
                                                        scale=1.0 / DM, bias=epsc[:, 0:1]), deps=sqs)
            b2 = OP("dve", lambda e, c0=c0: e.reciprocal(out=stat[:, 32 + c0:36 + c0], in_=stat[:, 16 + c0:20 + c0]), deps=[b])
            for s in range(4):
                sl = (4 * t + s) % 4
                col = c0 + s
                h = OP("dve", lambda e, sl=sl, s=s, col=col, tb=tb: e.scalar_tensor_tensor(
                    out=hb[tb][:, s, :], in0=XT[sl], scalar=stat[:, 32 + col:33 + col], in1=gB[:],
                    op0=ALU.mult, op1=ALU.mult), deps=[b2, xts[s], gt, hb_free[tb]])
                xt_free[sl] = h
                toks.append(h)
            return toks

        def transposes(t, htoks):
            tb = t % 2
            evs = []
            for f in range(KC // 2):
                bk = f % 2
                pt = None
                for kk in range(2):
                    for s in range(4):
                        kc = 2 * f + kk
                        lastone = (kk == 1 and s == 3)
                        pt = OP("pe", lambda e, bk=bk, kk=kk, s=s, kc=kc, tb=tb: e.transpose(
                            out=Tk[bk][:, kk * 512 + s * 128: kk * 512 + (s + 1) * 128],
                            in_=hb[tb][:, s, kc * 128:(kc + 1) * 128], identity=ident[:]),
                            deps=[htoks[s], tb_free[bk]], sig=lastone)
                ev = OP("act", lambda e, bk=bk, f=f, tb=tb: e.activation(
                    out=hT[tb][:, 2 * f:2 * f + 2, :], in_=Tk[bk][:, :].rearrange("p (k n) -> p k n", k=2),
                    func=AF.Copy), deps=[pt, hT_free[tb]])
                tb_free[bk] = ev
                evs.append(ev)
            hb_free[tb] = pt
            return evs

        def chunk(j, t, hT_toks, pending):
            tb = t % 2
            pa = j % 4
            mm = None
            for kc in range(KC):
                mm = OP("pe", lambda e, pa=pa, kc=kc, j=j, tb=tb: e.matmul(
                    Bk[pa][:, :], lhsT=Wi[:, kc, j * 128:(j + 1) * 128], rhs=hT[tb][:, kc, :],
                    start=(kc == 0), stop=(kc == KC - 1)),
                    deps=[wt[kc]] + list(hT_toks), sig=(kc == KC - 1))
            if pending:
                pending.pop()()
            k = state["st"] % NST
            state["st"] += 1
            dst = PJ[j, :, t * 512:(t + 1) * 512]
            if j < 2 * HP:
                n = state["qk"]; state["qk"] += 1
                qs, ts, pr, cs = n % 3, n % 2, n % 2, t % 2
                a = OP("act", lambda e, pa=pa, qs=qs: e.activation(out=qb[qs], in_=Bk[pa][:, :], func=AF.Copy),
                       deps=[mm, qb_free[qs]])
                d1 = OP("dve", lambda e, pa=pa, ts=ts, cs=cs: e.tensor_tensor(out=t1[ts], in0=Bk[pa][:, :], in1=CC[cs],
                                                                              op=ALU.mult),
                        deps=[mm, t1_free[ts], rope_c[cs]])

                def perm_mm(qs=qs, pr=pr, ts=ts, cs=cs, a=a, d1=d1, k=k, dst=dst, pa=pa):
                    pm = OP("pe", lambda e: e.matmul(Bk[4 + pr][:, :], lhsT=perm[:], rhs=qb[qs], start=True, stop=True),
                            deps=[a, pr_free[pr]])
                    qb_free[qs] = pm
                    d2 = OP("dve", lambda e: e.tensor_tensor(out=t2[ts], in0=Bk[4 + pr][:, :], in1=SSn[cs], op=ALU.mult),
                            deps=[pm, d1, rope_s[cs], t1_free[ts]])
                    pr_free[pr] = d2
                    ad = OP("pool", lambda e: e.tensor_tensor(out=st[k], in0=t1[ts], in1=t2[ts], op=ALU.add),
                            deps=[d1, d2, st_free[k]])
                    t1_free[ts] = ad
                    st_free[k] = DMA("sp", lambda e: e.dma_start(out=dst, in_=st[k]), f"st{k}", deps=[ad])
                    rope_use[cs] = ad
                pending.append(perm_mm)
                pa_wait[pa] = [a, d1]
            else:
                if j < 3 * HP:
                    ev = OP("act", lambda e, pa=pa, k=k: e.activation(out=st[k], in_=Bk[pa][:, :], func=AF.Copy),
                            deps=[mm, st_free[k]])
                elif j < 4 * HP or j >= 4 * HP + PC:
                    ev = OP("act", lambda e, pa=pa, k=k: e.activation(out=st[k], in_=Bk[pa][:, :], func=AF.Silu),
                            deps=[mm, st_free[k]])
                else:
                    ev = OP("dve", lambda e, pa=pa, k=k: e.tensor_copy(out=st[k], in_=Bk[pa][:, :]),
                            deps=[mm, st_free[k]])
                pa_wait[pa] = [ev]
                st_free[k] = DMA("sp", lambda e, k=k, dst=dst: e.dma_start(out=dst, in_=st[k]), f"st{k}", deps=[ev])

        pa_wait = [[] for _ in range(4)]
        rope_c = [None, None]
        rope_s = [None, None]
        rope_use = [None, None]

        htoks = norm_tile(0)
        hT_toks = transposes(0, htoks)
        def load_rope(t):
            cs = t % 2
            rope_c[cs] = DMA("sp", lambda e: e.dma_start(out=CC[cs], in_=ropeC_d[:, t * 512:(t + 1) * 512]),
                             f"rc{cs}", deps=[rope_use[cs]])
            rope_s[cs] = DMA("sp", lambda e: e.dma_start(out=SSn[cs], in_=ropeS_d[:, t * 512:(t + 1) * 512]),
                             f"rs{cs}", deps=[rope_use[cs]])

        for t in range(NTT):
            cs = t % 2
            load_rope(t)
            if t + 1 < NTT:
                nh = norm_tile(t + 1)
            pending = []
            nxt = None
            for j in range(NCH):
                pa = j % 4
                R.wait_only("pe", pa_wait[pa])
                pa_wait[pa] = []
                chunk(j, t, hT_toks, pending)
                if j == NCH // 2 and t + 1 < NTT:
                    nxt = transposes(t + 1, nh)
            if pending:
                pending.pop()()
            hT_free[t % 2] = last["pe"]
            if t + 1 < NTT:
                hT_toks = nxt
        end_phase()

    def phase2(l):
        bfa.reset(); fpa.reset()
        KT = [bfa.take(4096) for _ in range(2)]
        VT = [bfa.take(4096) for _ in range(2)]
        QT = [[bfa.take(2048) for _ in range(2)] for _ in range(2)]
        GA = [bfa.take(2048) for _ in range(2)]
        E = [bfa.take(512) for _ in range(3)]
        NP = 5
        P = [bfa.take(512) for _ in range(NP)]
        NV = 24
        VA = [bfa.take(256) for _ in range(NV)]
        YB = [bfa.take(2048) for _ in range(2)]
        MKV = [[bfa.take(512) for _ in range(3)] for _ in range(2)]
        ACC = [[fpa.take(2048) for _ in range(2)] for _ in range(2)]
        RR = fpa.take(2048)
        TT = fpa.take(2048)
        init = []
        for b in KT + VT + QT[0] + QT[1]:
            init.append(OP("pool", lambda e, b=b: e.memset(b, 0.0)))
        for v in VA:
            init.append(OP("pool", lambda e, v=v: e.memset(v, 1.0)))
        phase_sync(init)

        kv_free = [None, None]; q_free = [None, None]; ga_free = [None, None]
        e_free = [None] * 3; p_free = [None] * NP
        s_free = [None, None]; t_free = [None, None]
        o_free = [None, None]
        va_free = [None] * NV
        acc_free = [None, None]; yb_free = [None, None]; mk_free = [None, None]
        rr_free = None; tt_free = None
        vstate = {"n": 0}
        it = 0
        iters = [(hp, cseg) for hp in range(HP) for cseg in range(NSEG)]
        av = lambda m: m.rearrange("p (h x) -> p h x", h=2)[:, :, 0:128]
        bv = lambda m: m.rearrange("p (h x) -> p h x", h=2)[:, :, 128:256]

        def prefetch(it_):
            hp, cseg = iters[it_]
            sl = it_ % 2
            lo = SEG * cseg - 1024
            g0, g1 = max(lo, 0), min(lo + 4096, NT)
            kt = DMA("sp", lambda e: e.dma_start(out=KT[sl][:, g0 - lo:g1 - lo], in_=PJ[HP + hp, :, g0:g1]),
                     f"kt{sl}", deps=[kv_free[sl]])
            vt = DMA("sp", lambda e: e.dma_start(out=VT[sl][:, g0 - lo:g1 - lo], in_=PJ[2 * HP + hp, :, g0:g1]),
                     f"vt{sl}", deps=[kv_free[sl]])
            qt0 = DMA("sp", lambda e: e.dma_start(out=QT[sl][0][0:64, :], in_=PJ[hp, 0:64, cseg * SEG:(cseg + 1) * SEG]),
                      f"qa{sl}", deps=[q_free[sl]])
            qt = DMA("sp", lambda e: e.dma_start(out=QT[sl][1][64:128, :],
                                                 in_=PJ[hp, 64:128, cseg * SEG:(cseg + 1) * SEG]),
                     f"qb{sl}", deps=[q_free[sl]])
            gat = DMA("sp", lambda e: e.dma_start(out=GA[sl], in_=PJ[3 * HP + hp, :, cseg * SEG:(cseg + 1) * SEG]),
                      f"ga{sl}", deps=[ga_free[sl]])
            cL = segfl[:, 2 * cseg:2 * cseg + 1]
            cR = segfl[:, 2 * cseg + 1:2 * cseg + 2]
            mL, mR, mLR = MKV[sl]
            m1 = OP("pool", lambda e: e.tensor_copy(out=mL, in_=MK0[:]), deps=[mk_free[sl]])
            m2 = OP("pool", lambda e: e.tensor_scalar(out=av(mL), in0=av(MK0[:]), scalar1=cL, scalar2=None, op0=ALU.mult),
                    deps=[m1])
            m3 = OP("pool", lambda e: e.tensor_copy(out=mR, in_=MK0[:]), deps=[mk_free[sl]])
            m4 = OP("pool", lambda e: e.tensor_scalar(out=bv(mR), in0=bv(MK0[:]), scalar1=cR, scalar2=None, op0=ALU.mult),
                    deps=[m3])
            m5 = OP("pool", lambda e: e.tensor_copy(out=mLR, in_=mL), deps=[m2])
            m6 = OP("pool", lambda e: e.tensor_scalar(out=bv(mLR), in0=bv(MK0[:]), scalar1=cR, scalar2=None, op0=ALU.mult),
                    deps=[m5])
            return dict(kt=kt, vt=vt, qt0=qt0, qt=qt, gat=gat, mtok=[m2, m4, m6])

        pre = {0: prefetch(0)}
        for hp in range(HP):
            for cseg in range(NSEG):
                sl = it % 2
                if it + 1 < len(iters):
                    pre[it + 1] = prefetch(it + 1)
                cur = pre.pop(it)
                kt, vt, qt0, qt, gat, mtok = cur["kt"], cur["vt"], cur["qt0"], cur["qt"], cur["gat"], cur["mtok"]

                combos = []
                for d in PATTERNS:
                    nb = SEG // (128 * d)
                    if d == 1:
                        order = [(0, b) for b in range(16)]
                    elif d == 4:
                        order = [(r, b) for b in range(4) for r in range(4)]
                    else:
                        order = [(r, 0) for r in range(16)]
                    for (r, b) in order:
                        combos.append((d, r, b, nb))
                if 'combos' in DBG:
                    combos = [combos[ii] for ii in DBG['combos']]
                vcache = {}
                copy_toks = [None] * 4
                add_toks = []
                cp_all = []
                slot_key = {}
                acc = ACC[sl]
                NCB = len(combos)
                info = [dict() for _ in range(NCB)]

                def st_T(i):
                    d, r, b, nb = combos[i]
                    vts, newt = [], []
                    for tau in (b, b + 1):
                        key = (d, r, tau)
                        if key not in vcache:
                            slot = vstate["n"] % NV
                            vstate["n"] += 1
                            vcache[key] = [slot, None]
                            slot_key[slot] = key
                            newt.append((key, slot))
                        assert slot_key[vcache[key][0]] == key
                        vts.append(key)
                    tbk = i % 2
                    evl = []
                    tfl = t_free[tbk] if isinstance(t_free[tbk], list) else [t_free[tbk]]
                    tp = None
                    for n, (key, slot) in enumerate(newt):
                        start = 1024 + d * (128 * key[2] - 64) + r
                        tp = OP("pe", lambda e, tbk=tbk, n=n, start=start, d=d, sl=sl: e.transpose(
                            out=Tk[tbk][:, n * 128:(n + 1) * 128],
                            in_=VT[sl][:, start:start + 127 * d + 1:d], identity=ident[:]),
                            deps=[vt] + tfl, sig=(n == len(newt) - 1))
                    for n, (key, slot) in enumerate(newt):
                        if n == 5:
                            ev = OP("act", lambda e, tbk=tbk, n=n, slot=slot: e.activation(
                                out=VA[slot].rearrange("p (a x) -> p a x", a=4)[:, 0:4:3, :],
                                in_=Tk[tbk][:, n * 128:(n + 1) * 128].rearrange("p (a x) -> p a x", a=2), func=AF.Copy),
                                deps=[tp, va_free[slot]])
                        else:
                            ev = OP("dve", lambda e, tbk=tbk, n=n, slot=slot: e.tensor_copy(
                                out=VA[slot].rearrange("p (a x) -> p a x", a=4)[:, 0:4:3, :],
                                in_=Tk[tbk][:, n * 128:(n + 1) * 128].rearrange("p (a x) -> p a x", a=2)),
                                deps=[tp, va_free[slot]])
                        vcache[key][1] = ev
                        evl.append(ev)
                    if evl:
                        t_free[tbk] = evl
                    info[i]["vts"] = vts

                def st_S(i):
                    d, r, b, nb = combos[i]
                    sb_ = i % 2
                    qs0 = d * 128 * b + r
                    smm = None
                    for hh in range(2):
                        for ab in range(2):
                            ks0 = 1024 + d * (128 * (b + ab) - 64) + r
                            smm = OP("pe", lambda e, sb_=sb_, hh=hh, ab=ab, ks0=ks0, qs0=qs0, d=d, sl=sl: e.matmul(
                                Bk[sb_][:, (2 * hh + ab) * 128:(2 * hh + ab + 1) * 128],
                                lhsT=KT[sl][:, ks0:ks0 + 127 * d + 1:d],
                                rhs=QT[sl][hh][:, qs0:qs0 + 127 * d + 1:d], start=True, stop=True),
                                deps=[kt, qt0, qt, s_free[sb_]], sig=(hh == 1 and ab == 1))
                    es_ = i % 3
                    ex = OP("act", lambda e, es_=es_, sb_=sb_: e.activation(out=E[es_], in_=Bk[sb_][:, :], func=AF.Exp,
                                                                           scale=0.125), deps=[smm, e_free[es_]])
                    s_free[sb_] = ex
                    info[i]["ex"] = ex

                def st_M(i):
                    d, r, b, nb = combos[i]
                    es_, ps_ = i % 3, i % NP
                    first, lastb = (b == 0), (b == nb - 1)
                    if first and lastb:
                        mk, mt = MKV[sl][2], mtok[2]
                    elif first:
                        mk, mt = MKV[sl][0], mtok[0]
                    elif lastb:
                        mk, mt = MKV[sl][1], mtok[1]
                    else:
                        mk, mt = MK0[:], None
                    pm = OP("dve", lambda e, es_=es_, ps_=ps_, mk=mk: e.tensor_tensor(out=P[ps_], in0=E[es_], in1=mk,
                                                                                     op=ALU.mult),
                            deps=[info[i]["ex"], p_free[ps_], mt])
                    e_free[es_] = pm
                    info[i]["pm"] = pm

                def st_V(i):
                    d, r, b, nb = combos[i]
                    gidx, k = i // 4, i % 4
                    pat, gi = i // 16, (i // 4) % 4
                    ps_ = i % NP
                    pm, vts = info[i]["pm"], info[i]["vts"]
                    oset = gidx % 2
                    pv = None
                    for hh in range(2):
                        for ab in range(2):
                            slot, evt = vcache[vts[ab]]
                            assert slot_key[slot] == vts[ab]
                            pv = OP("pe", lambda e, oset=oset, hh=hh, ab=ab, k=k, slot=slot, ps_=ps_: e.matmul(
                                Bk[2 + 2 * oset + hh][:, k * 128:(k + 1) * 128],
                                lhsT=VA[slot][:, 128 * hh:128 * hh + 128],
                                rhs=P[ps_][:, (2 * hh + ab) * 128:(2 * hh + ab + 1) * 128],
                                start=(ab == 0), stop=(ab == 1)),
                                deps=[pm, evt, o_free[oset] if k == 0 else None], sig=(hh == 1 and ab == 1))
                    p_free[ps_] = pv
                    for key in vts:
                        va_free[vcache[key][0]] = pv
                    if k == 3:
                        pend_comb.append((pv, pat, gi, oset))
                    return pv

                def st_C(pv, pat, gi, oset):
                    if True:
                        toks = []
                        for hh in range(2):
                            ob = Bk[2 + 2 * oset + hh]
                            if pat == 0:
                                tk = OP("act", lambda e, hh=hh, gi=gi, ob=ob, acc=acc: e.activation(
                                    out=acc[hh][:, 512 * gi:512 * gi + 512], in_=ob[:, :], func=AF.Copy),
                                    deps=[pv, acc_free[sl]])
                            else:
                                if pat == 1:
                                    view = acc[hh][:, 512 * gi:512 * gi + 512].rearrange("p (t r) -> p r t", r=4)
                                    dd = [copy_toks[gi]]
                                else:
                                    view = acc[hh][:, :].rearrange("p (t r) -> p r t", r=16)[:, 4 * gi:4 * gi + 4, :]
                                    dd = list(copy_toks) + add_toks[-1:]
                                tk = OP("dve", lambda e, view=view, ob=ob: e.tensor_tensor(
                                    out=view, in0=ob[:, :].rearrange("p (r t) -> p r t", r=4), in1=view, op=ALU.add),
                                    deps=[pv] + dd)
                            toks.append(tk)
                        if pat == 0:
                            copy_toks[gi] = toks[-1]
                            cp_all.extend(toks)
                        else:
                            add_toks.extend(toks)
                        o_free[oset] = toks[-1]

                LAG = 3
                lastpv = None
                pend_comb = []
                for n in range(NCB + LAG):
                    if n < NCB:
                        st_T(n)
                        st_S(n)
                    if 1 <= n <= NCB:
                        st_M(n - 1)
                    while pend_comb:
                        st_C(*pend_comb.pop(0))
                    if n - LAG >= 0:
                        lastpv = st_V(n - LAG)
                while pend_comb:
                    st_C(*pend_comb.pop(0))
                kv_free[sl] = lastpv
                q_free[sl] = lastpv
                mk_free[sl] = lastpv
                fin = add_toks[-4:] + cp_all
                if DBG.get('nonorm'):
                    it += 1
                    continue
                l0 = OP("act", lambda e, acc=acc: e.activation(out=RR[0:64, :], in_=acc[0][64:128, :], func=AF.Ln),
                        deps=fin + [rr_free])
                l1 = OP("act", lambda e, acc=acc: e.activation(out=RR[64:128, :], in_=acc[1][0:64, :], func=AF.Ln),
                        deps=fin + [rr_free])
                r0 = OP("act", lambda e: e.activation(out=RR[0:64, :], in_=RR[0:64, :], func=AF.Exp, scale=-1.0), deps=[l0])
                r1 = OP("act", lambda e: e.activation(out=RR[64:128, :], in_=RR[64:128, :], func=AF.Exp, scale=-1.0), deps=[l1])
                n0 = OP("pool", lambda e, acc=acc: e.tensor_tensor(out=TT[0:64, :], in0=acc[0][0:64, :], in1=RR[0:64, :],
                                                                   op=ALU.mult), deps=[r0, tt_free])
                n1 = OP("pool", lambda e, acc=acc: e.tensor_tensor(out=TT[64:128, :], in0=acc[1][64:128, :],
                                                                   in1=RR[64:128, :], op=ALU.mult), deps=[r1, tt_free])
                yy = OP("pool", lambda e, sl=sl: e.tensor_tensor(out=YB[sl], in0=TT, in1=GA[sl], op=ALU.mult),
                        deps=[n0, n1, gat, yb_free[sl]])
                rr_free = n1
                tt_free = yy
                acc_free[sl] = n1
                ga_free[sl] = yy
                yb_free[sl] = DMA("sp", lambda e, sl=sl, hp=hp, cseg=cseg: e.dma_start(
                    out=YT[hp, :, cseg * SEG:(cseg + 1) * SEG], in_=YB[sl]), f"yb{sl}", deps=[yy])
                it += 1
        end_phase()

    def phase2b(l):
        bfa.reset(); fpa.reset()
        EXT = SEG + 16
        UE = [[bfa.take(EXT) for _ in range(PCG)] for _ in range(2)]
        GP = [[bfa.take(SEG) for _ in range(PCG)] for _ in range(2)]
        PLD = [bfa.take(SEG) for _ in range(PCG)]
        YP = [bfa.take(SEG) for _ in range(2)]
        Wp = bfa.take(PG * PCG * G).rearrange("p (g c n) -> p g c n", g=PG, c=PCG)
        SAB = [(fpa.take(EXT), fpa.take(EXT)) for _ in range(min(PCG, 2))]
        TS = fpa.take(16)
        wts = []
        for g in range(PG):
            for ci in range(PCG):
                wts.append(DMA("pool", lambda e, g=g, ci=ci: e.dma_start(out=Wp[:, g, ci, :],
                                                                         in_=w_pool[l, g, ci * 128:(ci + 1) * 128, :]),
                               f"wp{g}_{ci}"))
        pt = DMA("sp", lambda e: e.dma_start(out=psc[:], in_=psc_d[l]), "psc")
        init = []
        for s_ in range(2):
            for ci in range(PCG):
                init.append(OP("pool", lambda e, u=UE[s_][ci]: e.memset(u, 0.0)))
        phase_sync(init + wts + [pt])
        ue_free = [None, None]; gp_free = [None, None]; pld_free = [None] * PCG
        yp_free = [None, None]; po_free = [None] * 6
        s_free = [None, None]
        it = 0
        pon = 0
        ypn = 0
        piters = [(g, cseg) for g in range(PG) for cseg in range(NSEG)]

        def pprefetch(it_):
            g, cseg = piters[it_]
            sl = it_ % 2
            lo = SEG * cseg - 8
            g0, g1 = max(lo, 0), min(lo + EXT, NT)
            uts, gts = [], []
            for ci in range(PCG):
                pc = g * PCG + ci
                uts.append(DMA("sp", lambda e, ci=ci, pc=pc: e.dma_start(
                    out=UE[sl][ci][:, g0 - lo:g1 - lo], in_=PJ[4 * HP + pc, :, g0:g1]), f"ue{sl}_{ci}",
                    deps=[ue_free[sl]]))
                gts.append(DMA("sp", lambda e, ci=ci, pc=pc: e.dma_start(
                    out=GP[sl][ci], in_=PJ[4 * HP + PC + pc, :, cseg * SEG:(cseg + 1) * SEG]), f"gp{sl}_{ci}",
                    deps=[gp_free[sl]]))
            return uts, gts

        ppre = {0: pprefetch(0)}
        for g in range(PG):
            w = POOL_W[g]
            for cseg in range(NSEG):
                sl = it % 2
                if it + 1 < len(piters):
                    ppre[it + 1] = pprefetch(it + 1)
                uts, gts = ppre.pop(it)
                fL = pfl[:, 2 * cseg:2 * cseg + 1]
                fR = pfl[:, 2 * cseg + 1:2 * cseg + 2]
                plds = []
                for ci in range(PCG):
                    u = UE[sl][ci]
                    h1 = OP("pool", lambda e, u=u, fL=fL: e.tensor_scalar(out=u[:, 0:8], in0=u[:, 0:8], scalar1=fL,
                                                                          scalar2=None, op0=ALU.mult), deps=[uts[ci]])
                    h2 = OP("pool", lambda e, u=u, fR=fR: e.tensor_scalar(out=u[:, EXT - 8:EXT], in0=u[:, EXT - 8:EXT],
                                                                          scalar1=fR, scalar2=None, op0=ALU.mult),
                            deps=[uts[ci]])
                    SA, SBb = SAB[ci % 2]
                    seng = "pool" if ci % 2 == 0 else "dve"
                    a = OP(seng, lambda e, u=u, SA=SA: e.tensor_tensor(out=SA[:, 1:EXT], in0=u[:, 0:EXT - 1], in1=u[:, 1:EXT],
                                                                       op=ALU.add), deps=[h1, h2, s_free[ci % 2]])
                    cur, oth = SA, SBb
                    lo_v, hi_v, sh = 1, EXT, 1
                    ww = 2
                    while ww < w:
                        nl, nh = lo_v + sh, hi_v - sh
                        a = OP(seng, lambda e, cur=cur, oth=oth, nl=nl, nh=nh, sh=sh: e.tensor_tensor(
                            out=oth[:, nl:nh], in0=cur[:, nl - sh:nh - sh], in1=cur[:, nl + sh:nh + sh], op=ALU.add),
                            deps=[a])
                        cur, oth = oth, cur
                        lo_v, hi_v, sh, ww = nl, nh, sh * 2, ww * 2
                    assert lo_v <= 8 and hi_v >= EXT - 8
                    pl = OP("dve", lambda e, cur=cur, u=u, ci=ci, w=w: e.scalar_tensor_tensor(
                        out=PLD[ci], in0=cur[:, 8:8 + SEG], scalar=1.0 / w, in1=u[:, 8:8 + SEG], op0=ALU.mult,
                        op1=ALU.subtract), deps=[a, pld_free[ci]])
                    pb = (g * NSEG + cseg) * 16
                    s1 = OP("dve", lambda e, cur=cur, pb=pb: e.tensor_tensor(out=TS[:, 0:8], in0=cur[:, 8:16],
                                                                             in1=prc[:, pb:pb + 8], op=ALU.mult), deps=[pl])
                    s2 = OP("dve", lambda e, u=u, ci=ci: e.tensor_tensor(out=PLD[ci][:, 0:8], in0=TS[:, 0:8], in1=u[:, 8:16],
                                                                         op=ALU.subtract), deps=[s1])
                    s3 = OP("dve", lambda e, cur=cur, pb=pb: e.tensor_tensor(out=TS[:, 8:16], in0=cur[:, SEG:SEG + 8],
                                                                             in1=prc[:, pb + 8:pb + 16], op=ALU.mult),
                            deps=[s2])
                    s4 = OP("dve", lambda e, u=u, ci=ci: e.tensor_tensor(out=PLD[ci][:, SEG - 8:SEG], in0=TS[:, 8:16],
                                                                         in1=u[:, SEG:SEG + 8], op=ALU.subtract), deps=[s3])
                    s_free[ci % 2] = s4
                    plds.append(s4)
                ue_free[sl] = plds[-1]
                for do in range(PCG):
                    pc_out = g * PCG + do
                    ys = ypn % 2
                    ypn += 1
                    evs = []
                    for tt in range(4):
                        pb_ = pon % 6
                        pon += 1
                        mm = None
                        for ci in range(PCG):
                            mm = OP("pe", lambda e, pb_=pb_, g=g, ci=ci, do=do, tt=tt: e.matmul(
                                Bk[pb_][:, :], lhsT=Wp[:, g, ci, do * 128:(do + 1) * 128],
                                rhs=PLD[ci][:, tt * 512:(tt + 1) * 512], start=(ci == 0), stop=(ci == PCG - 1)),
                                deps=plds + [po_free[pb_]], sig=(ci == PCG - 1))
                        ev = OP("dve", lambda e, pb_=pb_, ys=ys, tt=tt, pc_out=pc_out, sl=sl, do=do: e.scalar_tensor_tensor(
                            out=YP[ys][:, tt * 512:(tt + 1) * 512], in0=Bk[pb_][:, :], scalar=psc[:, pc_out:pc_out + 1],
                            in1=GP[sl][do][:, tt * 512:(tt + 1) * 512], op0=ALU.mult, op1=ALU.mult),
                            deps=[mm, gts[do], yp_free[ys]])
                        po_free[pb_] = ev
                        evs.append(ev)
                    yp_free[ys] = DMA("sp", lambda e, ys=ys, pc_out=pc_out, cseg=cseg: e.dma_start(
                        out=YT[HP + pc_out, :, cseg * SEG:(cseg + 1) * SEG], in_=YP[ys]), f"yp{ys}", deps=[evs[-1]])
                    lastmm = mm
                for ci in range(PCG):
                    pld_free[ci] = lastmm
                gp_free[sl] = evs[-1]
                it += 1
        end_phase()

    def phase3(l):
        bfa.reset(); fpa.reset()
        Wo = bfa.take(MC * DM).rearrange("p (m n) -> p m n", m=MC)
        yT = [bfa.take(MC * 512).rearrange("p (m n) -> p m n", m=MC) for _ in range(2)]
        junk = bfa.take(DM)
        X3 = [fpa.take(DM) for _ in range(3)]
        XN = [fpa.take(DM) for _ in range(3)]
        x_src = x_in if l == 0 else XS
        final = (l == L - 1)
        NH = DM // 512
        wts = [DMA("pool", lambda e, m=m: e.dma_start(out=Wo[:, m, :], in_=w_out[l, m * 128:(m + 1) * 128, :]), f"wo{m}")
               for m in range(MC)]
        gt = DMA("sp", lambda e: e.dma_start(out=gB[:], in_=gB_d[L]), "g") if final else None
        phase_sync(wts)
        yt_free = [None, None]; x3_free = [None] * 3; xn_free = [None] * 3; po_free = [None] * 6
        pon = 0
        jstate = {}
        ytoks, xtoks = {}, {}

        def load_y(t):
            tb = t % 2
            ytoks[t] = DMA("sp", lambda e: e.dma_start(
                out=yT[tb], in_=YT[:, :, t * 512:(t + 1) * 512].rearrange("m p n -> p m n")), f"yt{tb}",
                deps=[yt_free[tb]])

        def load_x(i):
            xs = i % 3
            r0 = i * 128
            xtoks[i] = DMA("sp", lambda e: e.dma_start(out=X3[xs], in_=x_src[r0:r0 + 128, :]), f"x3{xs}",
                           deps=[x3_free[xs]])

        load_y(0)
        load_x(0)
        load_x(1)
        for t in range(NTT):
            tb = t % 2
            if t + 1 < NTT:
                load_y(t + 1)
            ytk = ytoks.pop(t)
            for s in range(4):
                i = 4 * t + s
                xs = i % 3
                r0 = t * 512 + s * 128
                xt = xtoks.pop(i)
                adds = []
                for hf in range(NH):
                    pb_ = pon % 6
                    pon += 1
                    mm = None
                    for m in range(MC):
                        mm = OP("pe", lambda e, pb_=pb_, m=m, s=s, hf=hf, tb=tb: e.matmul(
                            Bk[pb_][:, :], lhsT=yT[tb][:, m, s * 128:(s + 1) * 128], rhs=Wo[:, m, hf * 512:(hf + 1) * 512],
                            start=(m == 0), stop=(m == MC - 1)), deps=[ytk, po_free[pb_]], sig=(m == MC - 1))
                    ad = OP("dve", lambda e, pb_=pb_, xs=xs, hf=hf: e.tensor_tensor(
                        out=XN[xs][:, hf * 512:(hf + 1) * 512], in0=Bk[pb_][:, :], in1=X3[xs][:, hf * 512:(hf + 1) * 512],
                        op=ALU.add), deps=[mm, xt, xn_free[xs]])
                    po_free[pb_] = ad
                    adds.append(ad)
                x3_free[xs] = adds[-1]
                if i + 2 < 4 * NTT:
                    load_x(i + 2)
                if not final:
                    xn_free[xs] = DMA("sp", lambda e, xs=xs, r0=r0: e.dma_start(out=XS[r0:r0 + 128, :], in_=XN[xs]),
                                      f"xn{xs}", deps=adds)
                else:
                    col = i % 16
                    a = OP("act", lambda e, xs=xs, col=col: e.activation(out=junk, in_=XN[xs], func=AF.Square,
                                                                         accum_out=stat[:, col:col + 1]),
                           deps=adds + [jstate.get("junk")])
                    jstate["junk"] = a
                    b = OP("act", lambda e, col=col: e.activation(out=stat[:, 16 + col:17 + col], in_=stat[:, col:col + 1],
                                                                  func=AF.Sqrt, scale=1.0 / DM, bias=epsc[:, 0:1]), deps=[a])
                    b2 = OP("dve", lambda e, col=col: e.reciprocal(out=stat[:, 32 + col:33 + col],
                                                                   in_=stat[:, 16 + col:17 + col]), deps=[b])
                    fo = OP("dve", lambda e, xs=xs, col=col: e.scalar_tensor_tensor(
                        out=XN[xs], in0=XN[xs], scalar=stat[:, 32 + col:33 + col], in1=gB[:], op0=ALU.mult, op1=ALU.mult),
                        deps=[b2, gt])
                    xn_free[xs] = DMA("sp", lambda e, xs=xs, r0=r0: e.dma_start(out=y_out[r0:r0 + 128, :], in_=XN[xs]),
                                      f"xn{xs}", deps=[fo])
            yt_free[tb] = last["pe"]
        end_phase()

    for l in range(L):
        if 1 in phases:
            phase1(l)
        if 2 in phases:
            phase2(l)
        if 3 in phases:
            phase2b(l)
        if 4 in phases:
            phase3(l)

    keys = sorted(set(R.cnt.keys()))
    sems = {k: es.enter_context(nc.semaphore(f"s_{k}")) for k in keys}
    block = es.enter_context(nc.Block())

    def replay(engname):
        def f(eng):
            for waits, fn, semk, inc in R.ops[engname]:
                for (k, v) in waits:
                    eng.wait_ge(sems[k], v)
                if fn is None:
                    continue
                ins = fn(eng)
                if semk is not None:
                    ins.then_inc(sems[semk], inc)
        return f

    block.tensor(replay("pe"))
    block.scalar(replay("act"))
    block.vector(replay("dve"))
    block.gpsimd(replay("pool"))
    block.sync(replay("sp"))
    es.close()
    return nc


def _consts():
    ident = np.eye(128, dtype=np.float32)
    m = np.arange(128)
    sw = np.where((m % 64) < 32, m + 32, m - 32)
    perm = np.zeros((128, 128), np.float32)
    perm[sw, m] = 1.0
    i = np.arange(128)[:, None]
    j = np.arange(128)[None, :]
    ma = (i >= j).astype(np.float32)
    mb = (i <= j).astype(np.float32)
    mask0 = np.concatenate([ma, mb, ma, mb], axis=1)
    bf = ml_dtypes.bfloat16
    return ident.astype(bf), perm.astype(bf), mask0.astype(bf)


def _rope_tables(pos):
    hd = 64
    inv_freq = (np.float32(10000.0) ** (-(np.arange(0, hd, 2, dtype=np.float32)) / np.float32(hd))).astype(np.float32)
    ang = pos.astype(np.float32)[None, :] * inv_freq[:, None]
    cos = np.cos(ang).astype(np.float32)
    sin = np.sin(ang).astype(np.float32)
    m = np.arange(128)
    f = m % 32
    sgn = np.where((m % 64) < 32, -1.0, 1.0).astype(np.float32)
    return np.ascontiguousarray(cos[f]), np.ascontiguousarray(sin[f] * sgn[:, None])


def make_core_inputs(cfg, segs, xs, shared):
    NSEG, PG = cfg.NSEG, cfg.PG
    x = np.concatenate([xs[k][b, s:s + SEG] for (k, b, s, _, _) in segs], axis=0)
    pos = np.concatenate([np.arange(s, s + SEG) for (_, _, s, _, _) in segs])
    rc, rs = _rope_tables(pos)
    segfl = np.ones((128, 2 * NSEG), np.float32)
    pfl = np.ones((128, 2 * NSEG), np.float32)
    prc = np.zeros((128, PG, NSEG, 16), np.float32)
    for ci, (_, _, _, cl, cr) in enumerate(segs):
        segfl[:64, 2 * ci] = cl
        segfl[64:, 2 * ci + 1] = cr
        pfl[:, 2 * ci] = cl
        pfl[:, 2 * ci + 1] = cr
        for g, w in enumerate(POOL_W):
            jj = np.arange(8)
            left = np.where(jj < w // 2, jj + w // 2, w) if not cl else np.full(8, w)
            right = np.where(jj > 8 - w // 2, 8 - jj + w // 2, w) if not cr else np.full(8, w)
            prc[:, g, ci, 0:8] = 1.0 / left
            prc[:, g, ci, 8:16] = 1.0 / right
    d = dict(shared)
    d.update({"x": np.ascontiguousarray(x, dtype=np.float32), "ropeC": rc, "ropeS": rs, "segfl": segfl, "pfl": pfl,
              "prc": np.ascontiguousarray(prc.reshape(128, -1))})
    return d


def make_shared(cfg, norm_g, w_in, w_pool, pool_scale, w_out, final_norm_g):
    L, DM, PC = cfg.L, cfg.DM, cfg.PC
    ident, perm, mask0 = _consts()
    g_all = np.concatenate([norm_g, final_norm_g[None, :]], axis=0)
    gB = np.ascontiguousarray(np.broadcast_to(g_all[:, None, :], (L + 1, 128, DM)), dtype=np.float32)
    psc = np.ascontiguousarray(pool_scale.reshape(L, PC, 128).transpose(0, 2, 1), dtype=np.float32)
    return {"w_in": np.ascontiguousarray(w_in, dtype=np.float32), "w_out": np.ascontiguousarray(w_out, dtype=np.float32),
            "w_pool": np.ascontiguousarray(w_pool, dtype=np.float32), "gB": gB, "psc": psc,
            "ident": ident, "perm": perm, "mask0": mask0}


_NC_CACHE = {}


def run_cfg(cfg, core_segs, xs, shared, n_cores):
    key = (cfg.DM, cfg.HP, cfg.PCG, cfg.L, cfg.NSEG)
    if key not in _NC_CACHE:
        _NC_CACHE[key] = build_program(cfg)
    nc = _NC_CACHE[key]
    in_maps = [make_core_inputs(cfg, segs, xs, shared) for segs in core_segs]
    res = run_bass_kernel_spmd(nc, in_maps, core_ids=list(range(n_cores)))
    return [r["y"] for r in res.results]


def kernel(x_prompt, x_sample, norm_g, w_in, w_pool, pool_scale, w_out, final_norm_g):
    cfg = Cfg(NSEG=7)
    xs = {"p": np.asarray(x_prompt, np.float32), "s": np.asarray(x_sample, np.float32)}
    shared = make_shared(cfg, np.asarray(norm_g), np.asarray(w_in), np.asarray(w_pool), np.asarray(pool_scale),
                         np.asarray(w_out), np.asarray(final_norm_g))
    core_segs, keep = [], []
    for b in range(2):
        for h in range(2):
            base = 0 if h == 0 else 4096
            segs = [("p", b, base + i * SEG, int(i > 0), int(i < 5)) for i in range(6)]
            segs.append(("s", 2 * b + h, 0, 0, 0))
            core_segs.append(segs)
            keep.append([0, 1, 2, 3] if h == 0 else [2, 3, 4, 5])
    for cidx in range(4):
        core_segs.append([("s", 4 + cidx * 7 + i, 0, 0, 0) for i in range(7)])
    outs = run_cfg(cfg, core_segs, xs, shared, 8)
    DM = cfg.DM
    y_p = np.empty((2, 16384, DM), np.float32)
    y_s = np.empty((32, SEG, DM), np.float32)
    for ci, segs in enumerate(core_segs):
        o = np.asarray(outs[ci]).reshape(cfg.NSEG, SEG, DM)
        for si, (k, b, start, _, _) in enumerate(segs):
            if k == "s":
                y_s[b] = o[si]
            elif si in keep[ci]:
                y_p[b, start:start + SEG] = o[si]
    return (y_p, y_s)
```

```python
import numpy as np
import ml_dtypes
from contextlib import ExitStack
import concourse.bass as bass
import concourse.mybir as mybir
from concourse.bass_utils import run_bass_kernel_spmd

F32 = mybir.dt.float32
BF16 = mybir.dt.bfloat16
AF = mybir.ActivationFunctionType
ALU = mybir.AluOpType
SEG = 2048
PATTERNS = (1, 4, 16)
POOL_W = (2, 4, 8, 16)
RMS_EPS = 1e-6


DBG = {}


class Cfg:
    def __init__(self, DM=1024, HP=8, PCG=2, L=4, NSEG=8):
        self.DM, self.HP, self.PCG, self.L, self.NSEG = DM, HP, PCG, L, NSEG
        self.PG = 4
        self.KC = DM // 128
        self.PC = self.PG * PCG
        self.DA = 128 * HP
        self.DP = 128 * self.PC
        self.DIN = 4 * self.DA + 2 * self.DP
        self.DMIX = self.DA + self.DP
        self.MC = HP + self.PC
        self.NCH = 4 * HP + 2 * self.PC
        self.NT = NSEG * SEG
        self.NTT = self.NT // 512
        self.G = 128 * PCG


class Rec:
    ENG = ("pe", "act", "dve", "pool", "sp")

    def __init__(self):
        self.ops = {e: [] for e in self.ENG}
        self.cnt = {}
        self.seen = {e: {} for e in self.ENG}

    def _waits(self, eng, deps):
        waits = []
        for d in deps:
            if d is None:
                continue
            key, val = d
            if self.seen[eng].get(key, 0) < val:
                self.seen[eng][key] = val
                waits.append((key, val))
        return waits

    def op(self, eng, fn, deps=(), sig=True):
        waits = self._waits(eng, deps)
        tok = None
        if sig:
            self.cnt[eng] = self.cnt.get(eng, 0) + 1
            tok = (eng, self.cnt[eng])
        self.ops[eng].append((waits, fn, eng if sig else None, 1))
        return tok

    def dma(self, q, fn, sem, deps=()):
        waits = self._waits(q, deps)
        self.cnt[sem] = self.cnt.get(sem, 0) + 16
        tok = (sem, self.cnt[sem])
        self.ops[q].append((waits, fn, sem, 16))
        return tok

    def wait_only(self, eng, deps):
        waits = self._waits(eng, deps)
        if waits:
            self.ops[eng].append((waits, None, None, 0))


def build_program(cfg, phases=(1, 2, 3, 4)):
    c = cfg
    DM, HP, PCG, L, NSEG, KC, PC, PG = c.DM, c.HP, c.PCG, c.L, c.NSEG, c.KC, c.PC, c.PG
    DIN, MC, NCH, NT, NTT, G = c.DIN, c.MC, c.NCH, c.NT, c.NTT, c.G
    nc = bass.Bass("TRN2", target_bir_lowering=False)
    dt = nc.dram_tensor
    x_in = dt("x", [NT, DM], F32, kind="ExternalInput").ap()
    y_out = dt("y", [NT, DM], F32, kind="ExternalOutput").ap()
    w_in = dt("w_in", [L, DM, DIN], F32, kind="ExternalInput").ap()
    w_out = dt("w_out", [L, c.DMIX, DM], F32, kind="ExternalInput").ap()
    w_pool = dt("w_pool", [L, PG, G, G], F32, kind="ExternalInput").ap()
    gB_d = dt("gB", [L + 1, 128, DM], F32, kind="ExternalInput").ap()
    psc_d = dt("psc", [L, 128, PC], F32, kind="ExternalInput").ap()
    ropeC_d = dt("ropeC", [128, NT], F32, kind="ExternalInput").ap()
    ropeS_d = dt("ropeS", [128, NT], F32, kind="ExternalInput").ap()
    segfl_d = dt("segfl", [128, 2 * NSEG], F32, kind="ExternalInput").ap()
    pfl_d = dt("pfl", [128, 2 * NSEG], F32, kind="ExternalInput").ap()
    prc_d = dt("prc", [128, PG * NSEG * 16], F32, kind="ExternalInput").ap()
    ident_d = dt("ident", [128, 128], BF16, kind="ExternalInput").ap()
    perm_d = dt("perm", [128, 128], BF16, kind="ExternalInput").ap()
    mask_d = dt("mask0", [128, 512], BF16, kind="ExternalInput").ap()
    PJ = dt("PJ", [NCH, 128, NT], BF16, kind="Internal").ap()
    YT = dt("YT", [MC, 128, NT], BF16, kind="Internal").ap()
    XS = dt("XS", [NT, DM], F32, kind="Internal").ap()

    R = Rec()
    es = ExitStack()
    sb = lambda n, s, d: es.enter_context(nc.sbuf_tensor(n, s, d))
    NBF = max(KC * DIN + 8 * DM + 2 * KC * 512 + 9 * 512 + DM, 46080, MC * DM + 2 * MC * 512 + DM)
    NFP = 12288
    BF = sb("BF", [128, NBF], BF16)
    FP = sb("FP", [128, NFP], F32)
    ident = sb("ident_sb", [128, 128], BF16)
    perm = sb("perm_sb", [128, 128], BF16)
    MK0 = sb("MK0", [128, 512], BF16)
    NM0 = sb("NM0", [128, 512], BF16)
    segfl = sb("segfl_sb", [128, 2 * NSEG], F32)
    pfl = sb("pfl_sb", [128, 2 * NSEG], F32)
    prc = sb("prc_sb", [128, PG * NSEG * 16], F32)
    gB = sb("gBt", [128, DM], F32)
    psc = sb("psct", [128, PC], F32)
    stat = sb("stat", [128, 64], F32)
    epsc = sb("epsc", [128, 1], F32)
    Bk = [es.enter_context(nc.psum_tensor(f"B{i}", [128, 512], F32)) for i in range(6)]
    Tk = [es.enter_context(nc.psum_tensor(f"T{i}", [128, 1024], BF16)) for i in range(2)]

    class Arena:
        def __init__(self, t):
            self.t, self.off = t, 0

        def reset(self):
            self.off = 0

        def take(self, n):
            a = self.t[:, self.off:self.off + n]
            self.off += n
            assert self.off <= self.t.shape[1], (self.off, self.t.shape)
            return a

    bfa, fpa = Arena(BF), Arena(FP)
    barrier = []

    def phase_sync(tokens):
        for e in Rec.ENG:
            R.wait_only(e, tokens)

    last = {}

    def OP(eng, fn, deps=(), sig=True):
        t = R.op(eng, fn, deps, sig)
        if t is not None:
            last[eng] = t
        return t

    def DMA(q, fn, sem, deps=()):
        t = R.dma(q, fn, sem, deps)
        last[sem] = t
        return t

    def end_phase():
        phase_sync([v for v in last.values()])

    ctoks = []
    for i, (dst, src) in enumerate([(ident, ident_d), (perm, perm_d), (MK0, mask_d), (segfl, segfl_d),
                                    (pfl, pfl_d), (prc, prc_d)]):
        ctoks.append(DMA("sp", lambda e, dst=dst, src=src: e.dma_start(out=dst[:], in_=src[:]), f"c{i}"))
    ctoks.append(OP("pool", lambda e: e.memset(epsc[:], RMS_EPS)))
    phase_sync(ctoks)
    nm0 = OP("pool", lambda e: e.tensor_scalar(out=NM0[:], in0=MK0[:], scalar1=30000.0, scalar2=-30000.0, op0=ALU.mult,
                                               op1=ALU.add))
    phase_sync([nm0])

    def phase1(l):
        bfa.reset(); fpa.reset()
        Wi = bfa.take(KC * DIN).rearrange("p (k n) -> p k n", k=KC)
        hb = [bfa.take(4 * DM).rearrange("p (s d) -> p s d", s=4) for _ in range(2)]
        hT = [bfa.take(KC * 512).rearrange("p (k n) -> p k n", k=KC) for _ in range(2)]
        qb = [bfa.take(512) for _ in range(3)]
        NST = 6
        st = [bfa.take(512) for _ in range(NST)]
        junk = bfa.take(DM)
        XT = [fpa.take(DM) for _ in range(4)]
        CC = [fpa.take(512) for _ in range(2)]
        SSn = [fpa.take(512) for _ in range(2)]
        t1 = [fpa.take(512) for _ in range(2)]
        t2 = [fpa.take(512) for _ in range(2)]
        x_src = x_in if l == 0 else XS

        wt = [DMA("pool", lambda e, kc=kc: e.dma_start(out=Wi[:, kc, :], in_=w_in[l, kc * 128:(kc + 1) * 128, :]),
                  f"w{kc}") for kc in range(KC)]
        gt = DMA("sp", lambda e: e.dma_start(out=gB[:], in_=gB_d[l]), "g")

        xt_free = [None] * 4
        hb_free = [None] * 2
        hT_free = [None] * 2
        tb_free = [None] * 2
        pa_free = [None] * 4
        pr_free = [None] * 2
        qb_free = [None] * 3
        t1_free = [None] * 2
        st_free = [None] * NST
        cs_free = [None] * 2
        state = {"st": 0, "qk": 0}

        def norm_tile(t):
            tb = t % 2
            toks = []
            xts, sqs = [], []
            c0 = (4 * t) % 16
            for s in range(4):
                i = 4 * t + s
                sl = i % 4
                r0 = t * 512 + s * 128
                xt = DMA("sp", lambda e, sl=sl, r0=r0: e.dma_start(out=XT[sl], in_=x_src[r0:r0 + 128, :]),
                         f"x{sl}", deps=[xt_free[sl]])
                col = c0 + s
                a = OP("act", lambda e, sl=sl, col=col: e.activation(out=junk, in_=XT[sl], func=AF.Square,
                                                                     accum_out=stat[:, col:col + 1]), deps=[xt, state.get("junk")])
                state["junk"] = a
                xts.append(xt); sqs.append(a)
            b = OP("act", lambda e, c0=c0: e.activation(out=stat[:, 16 + c0:20 + c0], in_=stat[:, c0:c0 + 4], func=AF.Sqrt,
                                                        scale=1.0 / DM, bias=epsc[:, 0:1]), deps=sqs)
            b2 = OP("dve", lambda e, c0=c0: e.reciprocal(out=stat[:, 32 + c0:36 + c0], in_=stat[:, 16 + c0:20 + c0]), deps=[b])
            for s in range(4):
                sl = (4 * t + s) % 4
                col = c0 + s
                h = OP("dve", lambda e, sl=sl, s=s, col=col, tb=tb: e.scalar_tensor_tensor(
                    out=hb[tb][:, s, :], in0=XT[sl], scalar=stat[:, 32 + col:33 + col], in1=gB[:],
                    op0=ALU.mult, op1=ALU.mult), deps=[b2, xts[s], gt, hb_free[tb]])
                xt_free[sl] = h
                toks.append(h)
            return toks

        def transposes(t, htoks):
            tb = t % 2
            evs = []
            for f in range(KC // 2):
                bk = f % 2
                pt = None
                for kk in range(2):
                    for s in range(4):
                        kc = 2 * f + kk
                        lastone = (kk == 1 and s == 3)
                        pt = OP("pe", lambda e, bk=bk, kk=kk, s=s, kc=kc, tb=tb: e.transpose(
                            out=Tk[bk][:, kk * 512 + s * 128: kk * 512 + (s + 1) * 128],
                            in_=hb[tb][:, s, kc * 128:(kc + 1) * 128], identity=ident[:]),
                            deps=[htoks[s], tb_free[bk]], sig=lastone)
                ev = OP("act", lambda e, bk=bk, f=f, tb=tb: e.activation(
                    out=hT[tb][:, 2 * f:2 * f + 2, :], in_=Tk[bk][:, :].rearrange("p (k n) -> p k n", k=2),
                    func=AF.Copy), deps=[pt, hT_free[tb]])
                tb_free[bk] = ev
                evs.append(ev)
            hb_free[tb] = pt
            return evs

        def chunk(j, t, hT_toks, pending):
            tb = t % 2
            pa = j % 4
            mm = None
            for kc in range(KC):
                mm = OP("pe", lambda e, pa=pa, kc=kc, j=j, tb=tb: e.matmul(
                    Bk[pa][:, :], lhsT=Wi[:, kc, j * 128:(j + 1) * 128], rhs=hT[tb][:, kc, :],
                    start=(kc == 0), stop=(kc == KC - 1)),
                    deps=[wt[kc]] + list(hT_toks), sig=(kc == KC - 1))
            if pending:
                pending.pop()()
            k = state["st"] % NST
            state["st"] += 1
            dst = PJ[j, :, t * 512:(t + 1) * 512]
            if j < 2 * HP:
                n = state["qk"]; state["qk"] += 1
                qs, ts, pr, cs = n % 3, n % 2, n % 2, t % 2
                a = OP("act", lambda e, pa=pa, qs=qs: e.activation(out=qb[qs], in_=Bk[pa][:, :], func=AF.Copy),
                       deps=[mm, qb_free[qs]])
                d1 = OP("dve", lambda e, pa=pa, ts=ts, cs=cs: e.tensor_tensor(out=t1[ts], in0=Bk[pa][:, :], in1=CC[cs],
                                                                              op=ALU.mult),
                        deps=[mm, t1_free[ts], rope_c[cs]])

                def perm_mm(qs=qs, pr=pr, ts=ts, cs=cs, a=a, d1=d1, k=k, dst=dst, pa=pa):
                    pm = OP("pe", lambda e: e.matmul(Bk[4 + pr][:, :], lhsT=perm[:], rhs=qb[qs], start=True, stop=True),
                            deps=[a, pr_free[pr]])
                    qb_free[qs] = pm
                    d2 = OP("dve", lambda e: e.tensor_tensor(out=t2[ts], in0=Bk[4 + pr][:, :], in1=SSn[cs], op=ALU.mult),
                            deps=[pm, d1, rope_s[cs], t1_free[ts]])
                    pr_free[pr] = d2
                    ad = OP("pool", lambda e: e.tensor_tensor(out=st[k], in0=t1[ts], in1=t2[ts], op=ALU.add),
                            deps=[d1, d2, st_free[k]])
                    t1_free[ts] = ad
                    st_free[k] = DMA("sp", lambda e: e.dma_start(out=dst, in_=st[k]), f"st{k}", deps=[ad])
                    rope_use[cs] = ad
                pending.append(perm_mm)
                pa_wait[pa] = [a, d1]
            else:
                if j < 3 * HP:
                    ev = OP("act", lambda e, pa=pa, k=k: e.activation(out=st[k], in_=Bk[pa][:, :], func=AF.Copy),
                            deps=[mm, st_free[k]])
                elif j < 4 * HP or j >= 4 * HP + PC:
                    ev = OP("act", lambda e, pa=pa, k=k: e.activation(out=st[k], in_=Bk[pa][:, :], func=AF.Silu),
                            deps=[mm, st_free[k]])
                else:
                    ev = OP("dve", lambda e, pa=pa, k=k: e.tensor_copy(out=st[k], in_=Bk[pa][:, :]),
                            deps=[mm, st_free[k]])
                pa_wait[pa] = [ev]
                st_free[k] = DMA("sp", lambda e, k=k, dst=dst: e.dma_start(out=dst, in_=st[k]), f"st{k}", deps=[ev])

        pa_wait = [[] for _ in range(4)]
        rope_c = [None, None]
        rope_s = [None, None]
        rope_use = [None, None]

        htoks = norm_tile(0)
        hT_toks = transposes(0, htoks)
        def load_rope(t):
            cs = t % 2
            rope_c[cs] = DMA("sp", lambda e: e.dma_start(out=CC[cs], in_=ropeC_d[:, t * 512:(t + 1) * 512]),
                             f"rc{cs}", deps=[rope_use[cs]])
            rope_s[cs] = DMA("sp", lambda e: e.dma_start(out=SSn[cs], in_=ropeS_d[:, t * 512:(t + 1) * 512]),
                             f"rs{cs}", deps=[rope_use[cs]])

        for t in range(NTT):
            cs = t % 2
            load_rope(t)
            if t + 1 < NTT:
                nh = norm_tile(t + 1)
            pending = []
            nxt = None
            for j in range(NCH):
                pa = j % 4
                R.wait_only("pe", pa_wait[pa])
                pa_wait[pa] = []
                chunk(j, t, hT_toks, pending)
                if j == NCH // 2 and t + 1 < NTT:
                    nxt = transposes(t + 1, nh)
            if pending:
                pending.pop()()
            hT_free[t % 2] = last["pe"]
            if t + 1 < NTT:
                hT_toks = nxt
        end_phase()

    def phase2(l):
        bfa.reset(); fpa.reset()
        KT = [bfa.take(4096) for _ in range(2)]
        VT = [bfa.take(4096) for _ in range(2)]
        QT = [[bfa.take(2048) for _ in range(2)] for _ in range(2)]
        GA = [bfa.take(2048) for _ in range(2)]
        E = [bfa.take(512) for _ in range(3)]
        NP = 5
        P = [bfa.take(512) for _ in range(NP)]
        NV = 24
        VA = [bfa.take(256) for _ in range(NV)]
        YB = [bfa.take(2048) for _ in range(2)]
        MKV = [[bfa.take(512) for _ in range(3)] for _ in range(2)]
        ACC = [[fpa.take(2048) for _ in range(2)] for _ in range(2)]
        RR = fpa.take(2048)
        TT = fpa.take(2048)
        init = []
        for b in KT + VT + QT[0] + QT[1]:
            init.append(OP("pool", lambda e, b=b: e.memset(b, 0.0)))
        for v in VA:
            init.append(OP("pool", lambda e, v=v: e.memset(v, 1.0)))
        phase_sync(init)

        kv_free = [None, None]; q_free = [None, None]; ga_free = [None, None]
        e_free = [None] * 3; p_free = [None] * NP
        s_free = [None, None]; t_free = [None, None]
        o_free = [None, None]
        va_free = [None] * NV
        acc_free = [None, None]; yb_free = [None, None]; mk_free = [None, None]
        rr_free = None; tt_free = None
        vstate = {"n": 0}
        it = 0
        iters = [(hp, cseg) for hp in range(HP) for cseg in range(NSEG)]
        av = lambda m: m.rearrange("p (h x) -> p h x", h=2)[:, :, 0:128]
        bv = lambda m: m.rearrange("p (h x) -> p h x", h=2)[:, :, 128:256]

        def prefetch(it_):
            hp, cseg = iters[it_]
            sl = it_ % 2
            lo = SEG * cseg - 1024
            g0, g1 = max(lo, 0), min(lo + 4096, NT)
            kt = DMA("sp", lambda e: e.dma_start(out=KT[sl][:, g0 - lo:g1 - lo], in_=PJ[HP + hp, :, g0:g1]),
                     f"kt{sl}", deps=[kv_free[sl]])
            vt = DMA("sp", lambda e: e.dma_start(out=VT[sl][:, g0 - lo:g1 - lo], in_=PJ[2 * HP + hp, :, g0:g1]),
                     f"vt{sl}", deps=[kv_free[sl]])
            qt0 = DMA("sp", lambda e: e.dma_start(out=QT[sl][0][0:64, :], in_=PJ[hp, 0:64, cseg * SEG:(cseg + 1) * SEG]),
                      f"qa{sl}", deps=[q_free[sl]])
            qt = DMA("sp", lambda e: e.dma_start(out=QT[sl][1][64:128, :],
                                                 in_=PJ[hp, 64:128, cseg * SEG:(cseg + 1) * SEG]),
                     f"qb{sl}", deps=[q_free[sl]])
            gat = DMA("sp", lambda e: e.dma_start(out=GA[sl], in_=PJ[3 * HP + hp, :, cseg * SEG:(cseg + 1) * SEG]),
                      f"ga{sl}", deps=[ga_free[sl]])
            cL = segfl[:, 2 * cseg:2 * cseg + 1]
            cR = segfl[:, 2 * cseg + 1:2 * cseg + 2]
            mL, mR, mLR = MKV[sl]
            m1 = OP("pool", lambda e: e.tensor_copy(out=mL, in_=MK0[:]), deps=[mk_free[sl]])
            m2 = OP("pool", lambda e: e.tensor_scalar(out=av(mL), in0=av(MK0[:]), scalar1=cL, scalar2=None, op0=ALU.mult),
                    deps=[m1])
            m3 = OP("pool", lambda e: e.tensor_copy(out=mR, in_=MK0[:]), deps=[mk_free[sl]])
            m4 = OP("pool", lambda e: e.tensor_scalar(out=bv(mR), in0=bv(MK0[:]), scalar1=cR, scalar2=None, op0=ALU.mult),
                    deps=[m3])
            m5 = OP("pool", lambda e: e.tensor_copy(out=mLR, in_=mL), deps=[m2])
            m6 = OP("pool", lambda e: e.tensor_scalar(out=bv(mLR), in0=bv(MK0[:]), scalar1=cR, scalar2=None, op0=ALU.mult),
                    deps=[m5])
            cv = []
            for m_, t_ in ((mL, m5), (mR, m4), (mLR, m6)):
                cv.append(OP("pool", lambda e, m_=m_: e.tensor_scalar(out=m_, in0=m_, scalar1=30000.0, scalar2=-30000.0,
                                                                       op0=ALU.mult, op1=ALU.add), deps=[t_, m2, m6]))
            return dict(kt=kt, vt=vt, qt0=qt0, qt=qt, gat=gat, mtok=cv)

        pre = {0: prefetch(0)}
        for hp in range(HP):
            for cseg in range(NSEG):
                sl = it % 2
                if it + 1 < len(iters):
                    pre[it + 1] = prefetch(it + 1)
                cur = pre.pop(it)
                kt, vt, qt0, qt, gat, mtok = cur["kt"], cur["vt"], cur["qt0"], cur["qt"], cur["gat"], cur["mtok"]

                combos = []
                for d in PATTERNS:
                    nb = SEG // (128 * d)
                    if d == 1:
                        order = [(0, b) for b in range(16)]
                    elif d == 4:
                        order = [(r, b) for b in range(4) for r in range(4)]
                    else:
                        order = [(r, 0) for r in range(16)]
                    for (r, b) in order:
                        combos.append((d, r, b, nb))
                if 'combos' in DBG:
                    combos = [combos[ii] for ii in DBG['combos']]
                vcache = {}
                copy_toks = [None] * 4
                add_toks = []
                cp_all = []
                slot_key = {}
                acc = ACC[sl]
                NCB = len(combos)
                info = [dict() for _ in range(NCB)]

                def st_T(i):
                    d, r, b, nb = combos[i]
                    vts, newt = [], []
                    for tau in (b, b + 1):
                        key = (d, r, tau)
                        if key not in vcache:
                            slot = vstate["n"] % NV
                            vstate["n"] += 1
                            vcache[key] = [slot, None]
                            slot_key[slot] = key
                            newt.append((key, slot))
                        assert slot_key[vcache[key][0]] == key
                        vts.append(key)
                    tbk = i % 2
                    evl = []
                    tfl = t_free[tbk] if isinstance(t_free[tbk], list) else [t_free[tbk]]
                    tp = None
                    for n, (key, slot) in enumerate(newt):
                        start = 1024 + d * (128 * key[2] - 64) + r
                        tp = OP("pe", lambda e, tbk=tbk, n=n, start=start, d=d, sl=sl: e.transpose(
                            out=Tk[tbk][:, n * 128:(n + 1) * 128],
                            in_=VT[sl][:, start:start + 127 * d + 1:d], identity=ident[:]),
                            deps=[vt] + tfl, sig=(n == len(newt) - 1))
                    for n, (key, slot) in enumerate(newt):
                        if n == 5:
                            ev = OP("act", lambda e, tbk=tbk, n=n, slot=slot: e.activation(
                                out=VA[slot].rearrange("p (a x) -> p a x", a=4)[:, 0:4:3, :],
                                in_=Tk[tbk][:, n * 128:(n + 1) * 128].rearrange("p (a x) -> p a x", a=2), func=AF.Copy),
                                deps=[tp, va_free[slot]])
                        else:
                            ev = OP("dve", lambda e, tbk=tbk, n=n, slot=slot: e.tensor_copy(
                                out=VA[slot].rearrange("p (a x) -> p a x", a=4)[:, 0:4:3, :],
                                in_=Tk[tbk][:, n * 128:(n + 1) * 128].rearrange("p (a x) -> p a x", a=2)),
                                deps=[tp, va_free[slot]])
                        vcache[key][1] = ev
                        evl.append(ev)
                    if evl:
                        t_free[tbk] = evl
                    info[i]["vts"] = vts

                def st_S(i):
                    d, r, b, nb = combos[i]
                    sb_ = i % 2
                    ps_ = i % NP
                    first, lastb = (b == 0), (b == nb - 1)
                    if first and lastb:
                        mk, mt = MKV[sl][2], mtok[2]
                    elif first:
                        mk, mt = MKV[sl][0], mtok[0]
                    elif lastb:
                        mk, mt = MKV[sl][1], mtok[1]
                    else:
                        mk, mt = NM0[:], None
                    OP("pe", lambda e, sb_=sb_, mk=mk: e.matmul(Bk[sb_][:, :], lhsT=ident[:], rhs=mk, start=True, stop=False),
                       deps=[s_free[sb_], mt], sig=False)
                    qs0 = d * 128 * b + r
                    smm = None
                    for hh in range(2):
                        for ab in range(2):
                            ks0 = 1024 + d * (128 * (b + ab) - 64) + r
                            smm = OP("pe", lambda e, sb_=sb_, hh=hh, ab=ab, ks0=ks0, qs0=qs0, d=d, sl=sl: e.matmul(
                                Bk[sb_][:, (2 * hh + ab) * 128:(2 * hh + ab + 1) * 128],
                                lhsT=KT[sl][:, ks0:ks0 + 127 * d + 1:d],
                                rhs=QT[sl][hh][:, qs0:qs0 + 127 * d + 1:d], start=False, stop=(hh == 1 and ab == 1)),
                                deps=[kt, qt0, qt], sig=(hh == 1 and ab == 1))
                    ex = OP("act", lambda e, ps_=ps_, sb_=sb_: e.activation(out=P[ps_], in_=Bk[sb_][:, :], func=AF.Exp,
                                                                           scale=0.125), deps=[smm, p_free[ps_]])
                    s_free[sb_] = ex
                    info[i]["pm"] = ex

                def st_M(i):
                    pass

                def st_V(i):
                    d, r, b, nb = combos[i]
                    gidx, k = i // 4, i % 4
                    pat, gi = i // 16, (i // 4) % 4
                    ps_ = i % NP
                    pm, vts = info[i]["pm"], info[i]["vts"]
                    oset = gidx % 2
                    pv = None
                    for hh in range(2):
                        for ab in range(2):
                            slot, evt = vcache[vts[ab]]
                            assert slot_key[slot] == vts[ab]
                            pv = OP("pe", lambda e, oset=oset, hh=hh, ab=ab, k=k, slot=slot, ps_=ps_: e.matmul(
                                Bk[2 + 2 * oset + hh][:, k * 128:(k + 1) * 128],
                                lhsT=VA[slot][:, 128 * hh:128 * hh + 128],
                                rhs=P[ps_][:, (2 * hh + ab) * 128:(2 * hh + ab + 1) * 128],
                                start=(ab == 0), stop=(ab == 1)),
                                deps=[pm, evt, o_free[oset] if k == 0 else None], sig=(hh == 1 and ab == 1))
                    p_free[ps_] = pv
                    for key in vts:
                        va_free[vcache[key][0]] = pv
                    if k == 3:
                        pend_comb.append((pv, pat, gi, oset))
                    return pv

                def st_C(pv, pat, gi, oset):
                    if True:
                        toks = []
                        for hh in range(2):
                            ob = Bk[2 + 2 * oset + hh]
                            if pat == 0:
                                tk = OP("act", lambda e, hh=hh, gi=gi, ob=ob, acc=acc: e.activation(
                                    out=acc[hh][:, 512 * gi:512 * gi + 512], in_=ob[:, :], func=AF.Copy),
                                    deps=[pv, acc_free[sl]])
                            else:
                                if pat == 1:
                                    view = acc[hh][:, 512 * gi:512 * gi + 512].rearrange("p (t r) -> p r t", r=4)
                                    dd = [copy_toks[gi]]
                                else:
                                    view = acc[hh][:, :].rearrange("p (t r) -> p r t", r=16)[:, 4 * gi:4 * gi + 4, :]
                                    dd = list(copy_toks) + add_toks[-1:]
                                tk = OP("dve", lambda e, view=view, ob=ob: e.tensor_tensor(
                                    out=view, in0=ob[:, :].rearrange("p (r t) -> p r t", r=4), in1=view, op=ALU.add),
                                    deps=[pv] + dd)
                            toks.append(tk)
                        if pat == 0:
                            copy_toks[gi] = toks[-1]
                            cp_all.extend(toks)
                        else:
                            add_toks.extend(toks)
                        o_free[oset] = toks[-1]

                LAG = 3
                lastpv = None
                pend_comb = []
                for n in range(NCB + LAG):
                    if n < NCB:
                        st_T(n)
                        st_S(n)
                    if 1 <= n <= NCB:
                        st_M(n - 1)
                    while pend_comb:
                        st_C(*pend_comb.pop(0))
                    if n - LAG >= 0:
                        lastpv = st_V(n - LAG)
                while pend_comb:
                    st_C(*pend_comb.pop(0))
                kv_free[sl] = lastpv
                q_free[sl] = lastpv
                mk_free[sl] = lastpv
                fin = add_toks[-4:] + cp_all
                if DBG.get('nonorm'):
                    it += 1
                    continue
                l0 = OP("act", lambda e, acc=acc: e.activation(out=RR[0:64, :], in_=acc[0][64:128, :], func=AF.Ln),
                        deps=fin + [rr_free])
                l1 = OP("act", lambda e, acc=acc: e.activation(out=RR[64:128, :], in_=acc[1][0:64, :], func=AF.Ln),
                        deps=fin + [rr_free])
                r0 = OP("act", lambda e: e.activation(out=RR[0:64, :], in_=RR[0:64, :], func=AF.Exp, scale=-1.0), deps=[l0])
                r1 = OP("act", lambda e: e.activation(out=RR[64:128, :], in_=RR[64:128, :], func=AF.Exp, scale=-1.0), deps=[l1])
                n0 = OP("pool", lambda e, acc=acc: e.tensor_tensor(out=TT[0:64, :], in0=acc[0][0:64, :], in1=RR[0:64, :],
                                                                   op=ALU.mult), deps=[r0, tt_free])
                n1 = OP("pool", lambda e, acc=acc: e.tensor_tensor(out=TT[64:128, :], in0=acc[1][64:128, :],
                                                                   in1=RR[64:128, :], op=ALU.mult), deps=[r1, tt_free])
                yy = OP("pool", lambda e, sl=sl: e.tensor_tensor(out=YB[sl], in0=TT, in1=GA[sl], op=ALU.mult),
                        deps=[n0, n1, gat, yb_free[sl]])
                rr_free = n1
                tt_free = yy
                acc_free[sl] = n1
                ga_free[sl] = yy
                yb_free[sl] = DMA("sp", lambda e, sl=sl, hp=hp, cseg=cseg: e.dma_start(
                    out=YT[hp, :, cseg * SEG:(cseg + 1) * SEG], in_=YB[sl]), f"yb{sl}", deps=[yy])
                it += 1
        end_phase()

    def phase2b(l):
        bfa.reset(); fpa.reset()
        EXT = SEG + 16
        UE = [[bfa.take(EXT) for _ in range(PCG)] for _ in range(2)]
        GP = [[bfa.take(SEG) for _ in range(PCG)] for _ in range(2)]
        PLD = [bfa.take(SEG) for _ in range(PCG)]
        YP = [bfa.take(SEG) for _ in range(2)]
        Wp = bfa.take(PG * PCG * G).rearrange("p (g c n) -> p g c n", g=PG, c=PCG)
        SAB = [(fpa.take(EXT), fpa.take(EXT)) for _ in range(min(PCG, 2))]
        TS = fpa.take(16)
        wts = []
        for g in range(PG):
            for ci in range(PCG):
                wts.append(DMA("pool", lambda e, g=g, ci=ci: e.dma_start(out=Wp[:, g, ci, :],
                                                                         in_=w_pool[l, g, ci * 128:(ci + 1) * 128, :]),
                               f"wp{g}_{ci}"))
        pt = DMA("sp", lambda e: e.dma_start(out=psc[:], in_=psc_d[l]), "psc")
        init = []
        for s_ in range(2):
            for ci in range(PCG):
                init.append(OP("pool", lambda e, u=UE[s_][ci]: e.memset(u, 0.0)))
        phase_sync(init + wts + [pt])
        ue_free = [None, None]; gp_free = [None, None]; pld_free = [None] * PCG
        yp_free = [None, None]; po_free = [None] * 6
        s_free = [None, None]
        it = 0
        pon = 0
        ypn = 0
        piters = [(g, cseg) for g in range(PG) for cseg in range(NSEG)]

        def pprefetch(it_):
            g, cseg = piters[it_]
            sl = it_ % 2
            lo = SEG * cseg - 8
            g0, g1 = max(lo, 0), min(lo + EXT, NT)
            uts, gts = [], []
            for ci in range(PCG):
                pc = g * PCG + ci
                uts.append(DMA("sp", lambda e, ci=ci, pc=pc: e.dma_start(
                    out=UE[sl][ci][:, g0 - lo:g1 - lo], in_=PJ[4 * HP + pc, :, g0:g1]), f"ue{sl}_{ci}",
                    deps=[ue_free[sl]]))
                gts.append(DMA("sp", lambda e, ci=ci, pc=pc: e.dma_start(
                    out=GP[sl][ci], in_=PJ[4 * HP + PC + pc, :, cseg * SEG:(cseg + 1) * SEG]), f"gp{sl}_{ci}",
                    deps=[gp_free[sl]]))
            return uts, gts

        ppre = {0: pprefetch(0)}
        for g in range(PG):
            w = POOL_W[g]
            for cseg in range(NSEG):
                sl = it % 2
                if it + 1 < len(piters):
                    ppre[it + 1] = pprefetch(it + 1)
                uts, gts = ppre.pop(it)
                fL = pfl[:, 2 * cseg:2 * cseg + 1]
                fR = pfl[:, 2 * cseg + 1:2 * cseg + 2]
                plds = []
                for ci in range(PCG):
                    u = UE[sl][ci]
                    h1 = OP("pool", lambda e, u=u, fL=fL: e.tensor_scalar(out=u[:, 0:8], in0=u[:, 0:8], scalar1=fL,
                                                                          scalar2=None, op0=ALU.mult), deps=[uts[ci]])
                    h2 = OP("pool", lambda e, u=u, fR=fR: e.tensor_scalar(out=u[:, EXT - 8:EXT], in0=u[:, EXT - 8:EXT],
                                                                          scalar1=fR, scalar2=None, op0=ALU.mult),
                            deps=[uts[ci]])
                    SA, SBb = SAB[ci % 2]
                    seng = "pool" if ci % 2 == 0 else "dve"
                    a = OP(seng, lambda e, u=u, SA=SA: e.tensor_tensor(out=SA[:, 1:EXT], in0=u[:, 0:EXT - 1], in1=u[:, 1:EXT],
                                                                       op=ALU.add), deps=[h1, h2, s_free[ci % 2]])
                    cur, oth = SA, SBb
                    lo_v, hi_v, sh = 1, EXT, 1
                    ww = 2
                    while ww < w:
                        nl, nh = lo_v + sh, hi_v - sh
                        a = OP(seng, lambda e, cur=cur, oth=oth, nl=nl, nh=nh, sh=sh: e.tensor_tensor(
                            out=oth[:, nl:nh], in0=cur[:, nl - sh:nh - sh], in1=cur[:, nl + sh:nh + sh], op=ALU.add),
                            deps=[a])
                        cur, oth = oth, cur
                        lo_v, hi_v, sh, ww = nl, nh, sh * 2, ww * 2
                    assert lo_v <= 8 and hi_v >= EXT - 8
                    pl = OP("dve", lambda e, cur=cur, u=u, ci=ci, w=w: e.scalar_tensor_tensor(
                        out=PLD[ci], in0=cur[:, 8:8 + SEG], scalar=1.0 / w, in1=u[:, 8:8 + SEG], op0=ALU.mult,
                        op1=ALU.subtract), deps=[a, pld_free[ci]])
                    pb = (g * NSEG + cseg) * 16
                    s1 = OP("dve", lambda e, cur=cur, pb=pb: e.tensor_tensor(out=TS[:, 0:8], in0=cur[:, 8:16],
                                                                             in1=prc[:, pb:pb + 8], op=ALU.mult), deps=[pl])
                    s2 = OP("dve", lambda e, u=u, ci=ci: e.tensor_tensor(out=PLD[ci][:, 0:8], in0=TS[:, 0:8], in1=u[:, 8:16],
                                                                         op=ALU.subtract), deps=[s1])
                    s3 = OP("dve", lambda e, cur=cur, pb=pb: e.tensor_tensor(out=TS[:, 8:16], in0=cur[:, SEG:SEG + 8],
                                                                             in1=prc[:, pb + 8:pb + 16], op=ALU.mult),
                            deps=[s2])
                    s4 = OP("dve", lambda e, u=u, ci=ci: e.tensor_tensor(out=PLD[ci][:, SEG - 8:SEG], in0=TS[:, 8:16],
                                                                         in1=u[:, SEG:SEG + 8], op=ALU.subtract), deps=[s3])
                    s_free[ci % 2] = s4
                    plds.append(s4)
                ue_free[sl] = plds[-1]
                for do in range(PCG):
                    pc_out = g * PCG + do
                    ys = ypn % 2
                    ypn += 1
                    evs = []
                    for tt in range(4):
                        pb_ = pon % 6
                        pon += 1
                        mm = None
                        for ci in range(PCG):
                            mm = OP("pe", lambda e, pb_=pb_, g=g, ci=ci, do=do, tt=tt: e.matmul(
                                Bk[pb_][:, :], lhsT=Wp[:, g, ci, do * 128:(do + 1) * 128],
                                rhs=PLD[ci][:, tt * 512:(tt + 1) * 512], start=(ci == 0), stop=(ci == PCG - 1)),
                                deps=plds + [po_free[pb_]], sig=(ci == PCG - 1))
                        ev = OP("dve", lambda e, pb_=pb_, ys=ys, tt=tt, pc_out=pc_out, sl=sl, do=do: e.scalar_tensor_tensor(
                            out=YP[ys][:, tt * 512:(tt + 1) * 512], in0=Bk[pb_][:, :], scalar=psc[:, pc_out:pc_out + 1],
                            in1=GP[sl][do][:, tt * 512:(tt + 1) * 512], op0=ALU.mult, op1=ALU.mult),
                            deps=[mm, gts[do], yp_free[ys]])
                        po_free[pb_] = ev
                        evs.append(ev)
                    yp_free[ys] = DMA("sp", lambda e, ys=ys, pc_out=pc_out, cseg=cseg: e.dma_start(
                        out=YT[HP + pc_out, :, cseg * SEG:(cseg + 1) * SEG], in_=YP[ys]), f"yp{ys}", deps=[evs[-1]])
                    lastmm = mm
                for ci in range(PCG):
                    pld_free[ci] = lastmm
                gp_free[sl] = evs[-1]
                it += 1
        end_phase()

    def phase3(l):
        bfa.reset(); fpa.reset()
        Wo = bfa.take(MC * DM).rearrange("p (m n) -> p m n", m=MC)
        yT = [bfa.take(MC * 512).rearrange("p (m n) -> p m n", m=MC) for _ in range(2)]
        junk = bfa.take(DM)
        X3 = [fpa.take(DM) for _ in range(3)]
        XN = [fpa.take(DM) for _ in range(3)]
        x_src = x_in if l == 0 else XS
        final = (l == L - 1)
        NH = DM // 512
        wts = [DMA("pool", lambda e, m=m: e.dma_start(out=Wo[:, m, :], in_=w_out[l, m * 128:(m + 1) * 128, :]), f"wo{m}")
               for m in range(MC)]
        gt = DMA("sp", lambda e: e.dma_start(out=gB[:], in_=gB_d[L]), "g") if final else None
        phase_sync(wts)
        yt_free = [None, None]; x3_free = [None] * 3; xn_free = [None] * 3; po_free = [None] * 6
        pon = 0
        jstate = {}
        ytoks, xtoks = {}, {}

        def load_y(t):
            tb = t % 2
            ytoks[t] = DMA("sp", lambda e: e.dma_start(
                out=yT[tb], in_=YT[:, :, t * 512:(t + 1) * 512].rearrange("m p n -> p m n")), f"yt{tb}",
                deps=[yt_free[tb]])

        def load_x(i):
            xs = i % 3
            r0 = i * 128
            xtoks[i] = DMA("sp", lambda e: e.dma_start(out=X3[xs], in_=x_src[r0:r0 + 128, :]), f"x3{xs}",
                           deps=[x3_free[xs]])

        load_y(0)
        load_x(0)
        load_x(1)
        for t in range(NTT):
            tb = t % 2
            if t + 1 < NTT:
                load_y(t + 1)
            ytk = ytoks.pop(t)
            for s in range(4):
                i = 4 * t + s
                xs = i % 3
                r0 = t * 512 + s * 128
                xt = xtoks.pop(i)
                adds = []
                for hf in range(NH):
                    pb_ = pon % 6
                    pon += 1
                    mm = None
                    for m in range(MC):
                        mm = OP("pe", lambda e, pb_=pb_, m=m, s=s, hf=hf, tb=tb: e.matmul(
                            Bk[pb_][:, :], lhsT=yT[tb][:, m, s * 128:(s + 1) * 128], rhs=Wo[:, m, hf * 512:(hf + 1) * 512],
                            start=(m == 0), stop=(m == MC - 1)), deps=[ytk, po_free[pb_]], sig=(m == MC - 1))
                    ad = OP("dve", lambda e, pb_=pb_, xs=xs, hf=hf: e.tensor_tensor(
                        out=XN[xs][:, hf * 512:(hf + 1) * 512], in0=Bk[pb_][:, :], in1=X3[xs][:, hf * 512:(hf + 1) * 512],
                        op=ALU.add), deps=[mm, xt, xn_free[xs]])
                    po_free[pb_] = ad
                    adds.append(ad)
                x3_free[xs] = adds[-1]
                if i + 2 < 4 * NTT:
                    load_x(i + 2)
                if not final:
                    xn_free[xs] = DMA("sp", lambda e, xs=xs, r0=r0: e.dma_start(out=XS[r0:r0 + 128, :], in_=XN[xs]),
                                      f"xn{xs}", deps=adds)
                else:
                    col = i % 16
                    a = OP("act", lambda e, xs=xs, col=col: e.activation(out=junk, in_=XN[xs], func=AF.Square,
                                                                         accum_out=stat[:, col:col + 1]),
                           deps=adds + [jstate.get("junk")])
                    jstate["junk"] = a
                    b = OP("act", lambda e, col=col: e.activation(out=stat[:, 16 + col:17 + col], in_=stat[:, col:col + 1],
                                                                  func=AF.Sqrt, scale=1.0 / DM, bias=epsc[:, 0:1]), deps=[a])
                    b2 = OP("dve", lambda e, col=col: e.reciprocal(out=stat[:, 32 + col:33 + col],
                                                                   in_=stat[:, 16 + col:17 + col]), deps=[b])
                    fo = OP("dve", lambda e, xs=xs, col=col: e.scalar_tensor_tensor(
                        out=XN[xs], in0=XN[xs], scalar=stat[:, 32 + col:33 + col], in1=gB[:], op0=ALU.mult, op1=ALU.mult),
                        deps=[b2, gt])
                    xn_free[xs] = DMA("sp", lambda e, xs=xs, r0=r0: e.dma_start(out=y_out[r0:r0 + 128, :], in_=XN[xs]),
                                      f"xn{xs}", deps=[fo])
            yt_free[tb] = last["pe"]
        end_phase()

    for l in range(L):
        if 1 in phases:
            phase1(l)
        if 2 in phases:
            phase2(l)
        if 3 in phases:
            phase2b(l)
        if 4 in phases:
            phase3(l)

    keys = sorted(set(R.cnt.keys()))
    sems = {k: es.enter_context(nc.semaphore(f"s_{k}")) for k in keys}
    block = es.enter_context(nc.Block())

    def replay(engname):
        def f(eng):
            for waits, fn, semk, inc in R.ops[engname]:
                for (k, v) in waits:
                    eng.wait_ge(sems[k], v)
                if fn is None:
                    continue
                ins = fn(eng)
                if semk is not None:
                    ins.then_inc(sems[semk], inc)
        return f

    block.tensor(replay("pe"))
    block.scalar(replay("act"))
    block.vector(replay("dve"))
    block.gpsimd(replay("pool"))
    block.sync(replay("sp"))
    es.close()
    return nc


def _consts():
    ident = np.eye(128, dtype=np.float32)
    m = np.arange(128)
    sw = np.where((m % 64) < 32, m + 32, m - 32)
    perm = np.zeros((128, 128), np.float32)
    perm[sw, m] = 1.0
    i = np.arange(128)[:, None]
    j = np.arange(128)[None, :]
    ma = (i >= j).astype(np.float32)
    mb = (i <= j).astype(np.float32)
    mask0 = np.concatenate([ma, mb, ma, mb], axis=1)
    bf = ml_dtypes.bfloat16
    return ident.astype(bf), perm.astype(bf), mask0.astype(bf)


def _rope_tables(pos):
    hd = 64
    inv_freq = (np.float32(10000.0) ** (-(np.arange(0, hd, 2, dtype=np.float32)) / np.float32(hd))).astype(np.float32)
    ang = pos.astype(np.float32)[None, :] * inv_freq[:, None]
    cos = np.cos(ang).astype(np.float32)
    sin = np.sin(ang).astype(np.float32)
    m = np.arange(128)
    f = m % 32
    sgn = np.where((m % 64) < 32, -1.0, 1.0).astype(np.float32)
    return np.ascontiguousarray(cos[f]), np.ascontiguousarray(sin[f] * sgn[:, None])


def make_core_inputs(cfg, segs, xs, shared):
    NSEG, PG = cfg.NSEG, cfg.PG
    x = np.concatenate([xs[k][b, s:s + SEG] for (k, b, s, _, _) in segs], axis=0)
    pos = np.concatenate([np.arange(s, s + SEG) for (_, _, s, _, _) in segs])
    rc, rs = _rope_tables(pos)
    segfl = np.ones((128, 2 * NSEG), np.float32)
    pfl = np.ones((128, 2 * NSEG), np.float32)
    prc = np.zeros((128, PG, NSEG, 16), np.float32)
    for ci, (_, _, _, cl, cr) in enumerate(segs):
        segfl[:64, 2 * ci] = cl
        segfl[64:, 2 * ci + 1] = cr
        pfl[:, 2 * ci] = cl
        pfl[:, 2 * ci + 1] = cr
        for g, w in enumerate(POOL_W):
            jj = np.arange(8)
            left = np.where(jj < w // 2, jj + w // 2, w) if not cl else np.full(8, w)
            right = np.where(jj > 8 - w // 2, 8 - jj + w // 2, w) if not cr else np.full(8, w)
            prc[:, g, ci, 0:8] = 1.0 / left
            prc[:, g, ci, 8:16] = 1.0 / right
    d = dict(shared)
    d.update({"x": np.ascontiguousarray(x, dtype=np.float32), "ropeC": rc, "ropeS": rs, "segfl": segfl, "pfl": pfl,
              "prc": np.ascontiguousarray(prc.reshape(128, -1))})
    return d


def make_shared(cfg, norm_g, w_in, w_pool, pool_scale, w_out, final_norm_g):
    L, DM, PC = cfg.L, cfg.DM, cfg.PC
    ident, perm, mask0 = _consts()
    g_all = np.concatenate([norm_g, final_norm_g[None, :]], axis=0)
    gB = np.ascontiguousarray(np.broadcast_to(g_all[:, None, :], (L + 1, 128, DM)), dtype=np.float32)
    psc = np.ascontiguousarray(pool_scale.reshape(L, PC, 128).transpose(0, 2, 1), dtype=np.float32)
    return {"w_in": np.ascontiguousarray(w_in, dtype=np.float32), "w_out": np.ascontiguousarray(w_out, dtype=np.float32),
            "w_pool": np.ascontiguousarray(w_pool, dtype=np.float32), "gB": gB, "psc": psc,
            "ident": ident, "perm": perm, "mask0": mask0}


_NC_CACHE = {}


def run_cfg(cfg, core_segs, xs, shared, n_cores):
    key = (cfg.DM, cfg.HP, cfg.PCG, cfg.L, cfg.NSEG)
    if key not in _NC_CACHE:
        _NC_CACHE[key] = build_program(cfg)
    nc = _NC_CACHE[key]
    in_maps = [make_core_inputs(cfg, segs, xs, shared) for segs in core_segs]
    res = run_bass_kernel_spmd(nc, in_maps, core_ids=list(range(n_cores)))
    return [r["y"] for r in res.results]


def kernel(x_prompt, x_sample, norm_g, w_in, w_pool, pool_scale, w_out, final_norm_g):
    cfg = Cfg(NSEG=7)
    xs = {"p": np.asarray(x_prompt, np.float32), "s": np.asarray(x_sample, np.float32)}
    shared = make_shared(cfg, np.asarray(norm_g), np.asarray(w_in), np.asarray(w_pool), np.asarray(pool_scale),
                         np.asarray(w_out), np.asarray(final_norm_g))
    core_segs, keep = [], []
    for b in range(2):
        for h in range(2):
            base = 0 if h == 0 else 4096
            segs = [("p", b, base + i * SEG, int(i > 0), int(i < 5)) for i in range(6)]
            segs.append(("s", 2 * b + h, 0, 0, 0))
            core_segs.append(segs)
            keep.append([0, 1, 2, 3] if h == 0 else [2, 3, 4, 5])
    for cidx in range(4):
        core_segs.append([("s", 4 + cidx * 7 + i, 0, 0, 0) for i in range(7)])
    outs = run_cfg(cfg, core_segs, xs, shared, 8)
    DM = cfg.DM
    y_p = np.empty((2, 16384, DM), np.float32)
    y_s = np.empty((32, SEG, DM), np.float32)
    for ci, segs in enumerate(core_segs):
        o = np.asarray(outs[ci]).reshape(cfg.NSEG, SEG, DM)
        for si, (k, b, start, _, _) in enumerate(segs):
            if k == "s":
                y_s[b] = o[si]
            elif si in keep[ci]:
                y_p[b, start:start + SEG] = o[si]
    return (y_p, y_s)
```
